# Optimizing a Trainium2 kernel written in Bass

```python
import jax, jax.numpy as jnp
from jax import lax
import numpy as np

D_MODEL = 2048
BATCH = 4
SEQ = 2048
DEPTH = 2

PLE_DIM = 256
N_EVEN = (DEPTH + 1) // 2
N_ODD = DEPTH // 2
CHUNK = 64
CONV_W = 4
EPS = 1e-6
A_DK = 128
A_WIDTH = D_MODEL // 2
A_HEADS = A_WIDTH // A_DK
A_DV = A_WIDTH // A_HEADS
A_QK = A_HEADS * A_DK
B_WIDTH = D_MODEL // 2
B_BLOCK = 128
B_BLOCKS = B_WIDTH // B_BLOCK
LRU_C = 8.0
C_DK = 128
C_DV = 256
C_WIDTH = D_MODEL
C_HEADS = C_WIDTH // C_DV
C_QK = C_HEADS * C_DK
EVEN_IN = 2 * A_QK + 2 * A_WIDTH + 2 * B_WIDTH
EVEN_MIX = A_WIDTH + B_WIDTH
ODD_IN = 2 * C_QK + 3 * C_WIDTH + 2 * C_HEADS
ODD_MIX = C_WIDTH

kernel_name = "hybrid_hgrn2_rglru_mlstm_trunk"


def rmsnorm(x, g):
    xf = x.astype(jnp.float32)
    y = xf * lax.rsqrt(jnp.mean(xf * xf, axis=-1, keepdims=True) + EPS)
    return (y * g.astype(jnp.float32)).astype(x.dtype)


def head_rmsnorm(y, g):
    b, s, h, d = y.shape
    yn = y * lax.rsqrt(jnp.mean(y * y, axis=-1, keepdims=True) + EPS)
    return (yn * g.astype(jnp.float32).reshape(h, d)).reshape(b, s, h * d)


def causal_conv(u, w, bias):
    k_w = w.shape[0]
    s = u.shape[1]
    up = jnp.pad(u, ((0, 0), (k_w - 1, 0), (0, 0)))
    out = bias
    for k in range(k_w):
        out = out + up[:, k:k + s, :] * w[k]
    return out


def to_chunks(t):
    b, s = t.shape[:2]
    t = t.reshape((b, s // CHUNK, CHUNK) + t.shape[2:])
    return jnp.moveaxis(t, (1, 3), (0, 2))


def from_chunks(t):
    nc, b, h, c, d = t.shape
    return t.transpose(1, 0, 3, 2, 4).reshape(b, nc * c, h, d)


def hgrn2(q_pre, f_pre, v, lb):
    bsz, s, h, dk = q_pre.shape
    dv = v.shape[-1]
    lb = lb.reshape(h, dk)
    fp = f_pre.astype(jnp.float32)
    q = jax.nn.silu(q_pre.astype(jnp.float32))
    f = lb + (1.0 - lb) * jax.nn.sigmoid(fp)
    log_f = jnp.log(f)
    k = (1.0 - lb) * jax.nn.sigmoid(-fp)
    causal = jnp.tril(jnp.ones((CHUNK, CHUNK), dtype=bool))

    def step(state, inp):
        qc, kc, vc, gc = inp
        b = jnp.cumsum(gc, axis=2)
        diff = b[:, :, :, None, :] - b[:, :, None, :, :]
        w = jnp.exp(jnp.where(causal[:, :, None], diff, -jnp.inf))
        attn = jnp.einsum('bhtd,bhsd,bhtsd->bhts', qc, kc, w)
        o = (jnp.einsum('bhtd,bhde->bhte', qc * jnp.exp(b), state)
             + jnp.einsum('bhts,bhse->bhte', attn, vc))
        b_last = b[:, :, -1:, :]
        new_state = (jnp.exp(b_last[:, :, 0, :, None]) * state
                     + jnp.einsum('bhsd,bhse->bhde', kc * jnp.exp(b_last - b), vc))
        return new_state, o

    s0 = jnp.zeros((bsz, h, dk, dv), jnp.float32)
    xs = (to_chunks(q), to_chunks(k), to_chunks(v.astype(jnp.float32)), to_chunks(log_f))
    _, o = lax.scan(step, s0, xs)
    return from_chunks(o)


def rg_lru(xc, w_r, b_r, w_i, b_i, lam):
    bsz, s, wdt = xc.shape
    xf = xc.astype(jnp.float32)
    xh = xf.reshape(bsz, s, B_BLOCKS, B_BLOCK)
    r = jax.nn.sigmoid(jnp.einsum('bsnc,ncd->bsnd', xh, w_r.astype(jnp.float32)).reshape(bsz, s, wdt)
                       + b_r.astype(jnp.float32))
    ig = jax.nn.sigmoid(jnp.einsum('bsnc,ncd->bsnd', xh, w_i.astype(jnp.float32)).reshape(bsz, s, wdt)
                        + b_i.astype(jnp.float32))
    log_a = -LRU_C * r * jax.nn.softplus(-lam.astype(jnp.float32))
    a = jnp.exp(log_a)
    u = jnp.sqrt(-jnp.expm1(2.0 * log_a)) * (ig * xf)

    def combine(left, right):
        a1, b1 = left
        a2, b2 = right
        return a1 * a2, a2 * b1 + b2

    _, hs = lax.associative_scan(combine, (a, u), axis=1)
    return hs


def mlstm(q, k, v, li, lf):
    bsz, s, h, dk = q.shape
    dv = v.shape[-1]
    q = q.astype(jnp.float32) * (dk ** -0.5)
    causal = jnp.tril(jnp.ones((CHUNK, CHUNK), dtype=bool))

    def step(carry, inp):
        c_st, n_st, m_st = carry
        qc, kc, vc, lic, lfc = inp
        b = jnp.cumsum(lfc, axis=-1)
        dmat = jnp.where(causal, b[..., :, None] - b[..., None, :] + lic[..., None, :], -jnp.inf)
        inter = b + m_st[..., None]
        m_t = jnp.maximum(jnp.max(dmat, axis=-1), inter)
        wts = jnp.exp(dmat - m_t[..., None])
        g_inter = jnp.exp(inter - m_t)
        sc = jnp.einsum('bhtd,bhsd->bhts', qc, kc) * wts
        num = (g_inter[..., None] * jnp.einsum('bhtd,bhde->bhte', qc, c_st)
               + jnp.einsum('bhts,bhse->bhte', sc, vc))
        den = g_inter * jnp.einsum('bhtd,bhd->bht', qc, n_st) + jnp.sum(sc, axis=-1)
        hout = num / jnp.maximum(jnp.abs(den), jnp.exp(-m_t))[..., None]
        b_last = b[..., -1]
        logw = b_last[..., None] - b + lic
        m_new = jnp.maximum(b_last + m_st, jnp.max(logw, axis=-1))
        decay = jnp.exp(b_last + m_st - m_new)
        kw = kc * jnp.exp(logw - m_new[..., None])[..., None]
        c_new = decay[..., None, None] * c_st + jnp.einsum('bhsd,bhse->bhde', kw, vc)
        n_new = decay[..., None] * n_st + jnp.sum(kw, axis=2)
        return (c_new, n_new, m_new), hout

    init = (jnp.zeros((bsz, h, dk, dv), jnp.float32),
            jnp.zeros((bsz, h, dk), jnp.float32),
            jnp.zeros((bsz, h), jnp.float32))
    xs = (to_chunks(q), to_chunks(k.astype(jnp.float32)), to_chunks(v.astype(jnp.float32)),
          to_chunks(li), to_chunks(lf))
    _, hs = lax.scan(step, init, xs)
    return from_chunks(hs)


def setup_inputs(seed: int = 0) -> dict:
    key = jax.random.key(seed)
    ks = jax.random.split(key, 32)

    def nrm(k, shape, scale):
        return jax.random.normal(k, shape, jnp.float32) * scale

    u = jax.random.uniform(ks[12], (N_EVEN, B_WIDTH), jnp.float32, minval=0.9, maxval=0.999)
    a0 = u ** (1.0 / LRU_C)
    return {
        "x": nrm(ks[0], (BATCH, SEQ, D_MODEL), 1.0),
        "p": nrm(ks[1], (DEPTH, BATCH, SEQ, PLE_DIM), 1.0),
        "e_norm": 1.0 + nrm(ks[2], (N_EVEN, D_MODEL), 0.02),
        "e_w_in": nrm(ks[3], (N_EVEN, D_MODEL, EVEN_IN), D_MODEL ** -0.5),
        "a_lb_logits": nrm(ks[4], (N_EVEN + 1, A_QK), 0.1),
        "a_norm": 1.0 + nrm(ks[5], (N_EVEN, A_WIDTH), 0.02),
        "b_conv_w": nrm(ks[6], (N_EVEN, CONV_W, B_WIDTH), CONV_W ** -0.5),
        "b_conv_b": nrm(ks[7], (N_EVEN, B_WIDTH), 0.01),
        "b_w_r": nrm(ks[8], (N_EVEN, B_BLOCKS, B_BLOCK, B_BLOCK), B_BLOCK ** -0.5),
        "b_b_r": nrm(ks[9], (N_EVEN, B_WIDTH), 0.1),
        "b_w_i": nrm(ks[10], (N_EVEN, B_BLOCKS, B_BLOCK, B_BLOCK), B_BLOCK ** -0.5),
        "b_b_i": nrm(ks[11], (N_EVEN, B_WIDTH), 0.1),
        "b_lambda": jnp.log(a0) - jnp.log1p(-a0),
        "e_w_out": nrm(ks[13], (N_EVEN, EVEN_MIX, D_MODEL), EVEN_MIX ** -0.5),
        "o_norm": 1.0 + nrm(ks[14], (N_ODD, D_MODEL), 0.02),
        "o_w_in": nrm(ks[15], (N_ODD, D_MODEL, ODD_IN), D_MODEL ** -0.5),
        "c_conv_w": nrm(ks[16], (N_ODD, CONV_W, 2 * C_QK), CONV_W ** -0.5),
        "c_conv_b": nrm(ks[17], (N_ODD, 2 * C_QK), 0.01),
        "c_b_i": nrm(ks[18], (N_ODD, C_HEADS), 0.1),
        "c_b_f": jnp.linspace(3.0, 6.0, C_HEADS, dtype=jnp.float32)[None, :] + nrm(ks[19], (N_ODD, C_HEADS), 0.1),
        "c_norm": 1.0 + nrm(ks[20], (N_ODD, C_WIDTH), 0.02),
        "o_w_out": nrm(ks[21], (N_ODD, ODD_MIX, D_MODEL), ODD_MIX ** -0.5),
        "ple_w": nrm(ks[22], (DEPTH, PLE_DIM, D_MODEL), PLE_DIM ** -0.5),
        "ple_norm": 1.0 + nrm(ks[23], (DEPTH, D_MODEL), 0.02),
        "ple_gate_w": nrm(ks[24], (DEPTH, D_MODEL, D_MODEL), D_MODEL ** -0.5),
        "final_norm": 1.0 + nrm(ks[25], (D_MODEL,), 0.02),
    }


def reference(x, p, e_norm, e_w_in, a_lb_logits, a_norm, b_conv_w, b_conv_b, b_w_r, b_b_r,
              b_w_i, b_b_i, b_lambda, e_w_out, o_norm, o_w_in, c_conv_w, c_conv_b, c_b_i,
              c_b_f, c_norm, o_w_out, ple_w, ple_norm, ple_gate_w, final_norm):
    bsz, s, _ = x.shape
    lbs = jnp.cumsum(jax.nn.softmax(a_lb_logits.astype(jnp.float32), axis=0), axis=0)
    h = x
    for i in range(DEPTH):
        j = i // 2
        hn = rmsnorm(h, e_norm[j] if i % 2 == 0 else o_norm[j])
        if i % 2 == 0:
            u = hn @ e_w_in[j]
            o0 = 0
            qa = u[..., o0:o0 + A_QK]; o0 += A_QK
            fa = u[..., o0:o0 + A_QK]; o0 += A_QK
            ia = u[..., o0:o0 + A_WIDTH]; o0 += A_WIDTH
            za = u[..., o0:o0 + A_WIDTH]; o0 += A_WIDTH
            xb = u[..., o0:o0 + B_WIDTH]; o0 += B_WIDTH
            zb = u[..., o0:o0 + B_WIDTH]
            ya = hgrn2(qa.reshape(bsz, s, A_HEADS, A_DK), fa.reshape(bsz, s, A_HEADS, A_DK),
                       ia.reshape(bsz, s, A_HEADS, A_DV), lbs[j])
            ya = head_rmsnorm(ya, a_norm[j]) * jax.nn.silu(za.astype(jnp.float32))
            xc = causal_conv(xb, b_conv_w[j], b_conv_b[j])
            yb = rg_lru(xc, b_w_r[j], b_b_r[j], b_w_i[j], b_b_i[j], b_lambda[j])
            yb = yb * jax.nn.silu(zb.astype(jnp.float32))
            y = jnp.concatenate([ya, yb], axis=-1).astype(h.dtype)
            mix = y @ e_w_out[j]
        else:
            u = hn @ o_w_in[j]
            o0 = 0
            qk = u[..., o0:o0 + 2 * C_QK]; o0 += 2 * C_QK
            v = u[..., o0:o0 + C_WIDTH]; o0 += C_WIDTH
            og = u[..., o0:o0 + C_WIDTH]; o0 += C_WIDTH
            z = u[..., o0:o0 + C_WIDTH]; o0 += C_WIDTH
            ig = u[..., o0:o0 + C_HEADS]; o0 += C_HEADS
            fg = u[..., o0:o0 + C_HEADS]
            qk = jax.nn.silu(causal_conv(qk, c_conv_w[j], c_conv_b[j]))
            q = qk[..., :C_QK].reshape(bsz, s, C_HEADS, C_DK)
            k = qk[..., C_QK:].reshape(bsz, s, C_HEADS, C_DK)
            li = ig.astype(jnp.float32) + c_b_i[j].astype(jnp.float32)
            lf = jax.nn.log_sigmoid(fg.astype(jnp.float32) + c_b_f[j].astype(jnp.float32))
            yc = mlstm(q, k, v.reshape(bsz, s, C_HEADS, C_DV), li, lf)
            yc = (head_rmsnorm(yc, c_norm[j]) * jax.nn.sigmoid(og.astype(jnp.float32))
                  * jax.nn.silu(z.astype(jnp.float32)))
            mix = yc.astype(h.dtype) @ o_w_out[j]
        h = h + mix
        pl = rmsnorm(p[i].astype(h.dtype) @ ple_w[i], ple_norm[i])
        h = h + jax.nn.sigmoid(h @ ple_gate_w[i]) * pl
    return rmsnorm(h, final_norm)
```

```python
import numpy as np
import ml_dtypes
import concourse.bass as bass
import concourse.mybir as mybir
from concourse.bass_utils import run_bass_kernel_spmd

F32 = mybir.dt.float32
BF16 = mybir.dt.bfloat16
AF = mybir.ActivationFunctionType
ALU = mybir.AluOpType
AX = mybir.AxisListType


class Buf:
    __slots__ = ("name", "last_w", "readers")

    def __init__(self, name):
        self.name = name
        self.last_w = None
        self.readers = []


class Op:
    __slots__ = ("eng", "idx", "thunk", "waits", "dma_waits", "signal", "sigval",
                 "is_dma", "dsem", "dval")

    def __init__(self, eng, thunk):
        self.eng = eng
        self.thunk = thunk
        self.waits = {}
        self.dma_waits = {}
        self.signal = False
        self.sigval = 0
        self.is_dma = False
        self.dsem = None
        self.dval = 0


ENGS = ("pe", "act", "dve", "pool", "sp")


class Prog:
    def __init__(self, nc):
        self.nc = nc
        self.ops = {e: [] for e in ENGS}
        self.waited = {e: {} for e in ENGS}
        self.dma_sem_val = {}
        self.dma_last = {}
        self.dma_keys = []

    def eng_obj(self, e):
        nc = self.nc
        return {"pe": nc.tensor, "act": nc.scalar, "dve": nc.vector,
                "pool": nc.gpsimd, "sp": nc.sync}[e]

    def _deps(self, op, reads, writes, acc_ok=()):
        deps = []
        for b in reads:
            if b.last_w is not None:
                deps.append(b.last_w)
        for b in writes:
            if b.last_w is not None:
                if not (b in acc_ok and b.last_w.eng == op.eng):
                    deps.append(b.last_w)
            for r in b.readers:
                deps.append(r)
        e = op.eng
        for d in deps:
            if d is op:
                continue
            if d.is_dma:
                cur = self.waited[e].get(("dma", d.dsem), 0)
                if d.dval > cur:
                    op.dma_waits[d.dsem] = max(op.dma_waits.get(d.dsem, 0), d.dval)
                    self.waited[e][("dma", d.dsem)] = d.dval
            else:
                if d.eng == e and e == "pe":
                    continue
                cur = self.waited[e].get(d.eng, -1)
                if d.idx > cur:
                    prev = op.waits.get(d.eng)
                    if prev is None or d.idx > prev.idx:
                        op.waits[d.eng] = d
        for k, d in op.waits.items():
            d.signal = True
            self.waited[e][k] = max(self.waited[e].get(k, -1), d.idx)
        for b in reads:
            b.readers.append(op)
        for b in writes:
            b.last_w = op
            b.readers = []

    def emit(self, eng, thunk, reads=(), writes=(), acc_ok=()):
        op = Op(eng, thunk)
        op.idx = len(self.ops[eng])
        self._deps(op, reads, writes, acc_ok)
        self.ops[eng].append(op)
        return op

    def dma(self, eng, key, out, in_, reads=(), writes=(), fn=None, **kw):
        if key not in self.dma_sem_val:
            self.dma_sem_val[key] = 0
            self.dma_keys.append(key)
        op = Op(eng, None)
        op.idx = len(self.ops[eng])
        op.is_dma = True
        op.dsem = key
        prev = self.dma_last.get(key)
        self._deps(op, reads, writes)
        if prev is not None:
            cur = self.waited[eng].get(("dma", key), 0)
            if prev.dval > cur:
                op.dma_waits[key] = max(op.dma_waits.get(key, 0), prev.dval)
                self.waited[eng][("dma", key)] = prev.dval
        self.dma_sem_val[key] += 16
        op.dval = self.dma_sem_val[key]
        self.dma_last[key] = op
        op.thunk = (out, in_, kw, fn)
        self.ops[eng].append(op)
        return op

    def finalize(self, sems):
        for e in ENGS:
            c = 0
            for op in self.ops[e]:
                if op.signal:
                    c += 1
                    op.sigval = c

        def run_engine(e, eng):
            for op in self.ops[e]:
                for k, d in op.waits.items():
                    eng.wait_ge(sems[k], d.sigval)
                for k, v in op.dma_waits.items():
                    eng.wait_ge(sems[("dma", k)], v)
                if op.is_dma:
                    out, in_, kw, fn = op.thunk
                    ins = fn(eng) if fn is not None else eng.dma_start(out=out, in_=in_, **kw)
                    ins.then_inc(sems[("dma", op.dsem)], 16)
                    if op.signal:
                        raise RuntimeError("dma op cannot signal engine sem")
                else:
                    ins = op.thunk(eng)
                    if op.signal:
                        ins.then_inc(sems[e], 1)
        return run_engine


def run_prog(nc, prog, final_waits=()):
    from contextlib import ExitStack
    with ExitStack() as st:
        sems = {}
        for e in ENGS:
            sems[e] = st.enter_context(nc.semaphore("s_" + e))
        for k in prog.dma_keys:
            sems[("dma", k)] = st.enter_context(nc.semaphore("d_" + str(k)))
        block = st.enter_context(nc.Block())
        runner = prog.finalize(sems)

        @block.tensor
        def _(eng):
            runner("pe", eng)

        @block.scalar
        def _(eng):
            runner("act", eng)

        @block.vector
        def _(eng):
            runner("dve", eng)

        @block.gpsimd
        def _(eng):
            runner("pool", eng)

        @block.sync
        def _(eng):
            runner("sp", eng)
            for k in final_waits:
                eng.wait_ge(sems[("dma", k)], prog.dma_sem_val[k])


D = 2048
T = 1024
NT = 8
KC = 16
EPS = 1e-6
NW = 6
ARENA = 57600
L0_NSLOT = 40 + 8 + 40


def _fm_slot(W, c0):
    blk = W[:, c0:c0 + 128].reshape(KC, 128, 128)
    return np.ascontiguousarray(blk.transpose(1, 0, 2)).reshape(128, 2048)


def _tm_slots(W, c0):
    out = []
    K = W.shape[0] // 128
    blk = W[:, c0:c0 + 512].reshape(K, 128, 512)
    for kcg in range(K // 4):
        out.append(np.ascontiguousarray(blk[kcg * 4:(kcg + 1) * 4].transpose(1, 0, 2)).reshape(128, 2048))
    return out


def _pl_slot(ple_w, g):
    out = np.zeros((128, 2048), np.float32)
    blk = ple_w[:, g * 512:(g + 1) * 512].reshape(2, 128, 512)
    out[:, 0:1024] = blk.transpose(1, 0, 2).reshape(128, 1024)
    return out


def pack_tail(w_out, gate_w, ple_w):
    slots = []
    for g in range(4):
        slots += _tm_slots(w_out, 512 * g)
    for g in range(4):
        slots.append(_pl_slot(ple_w, g))
    for g in range(4):
        slots += _tm_slots(gate_w, 512 * g)
        slots.append(_pl_slot(ple_w, g))
    return slots


def pack_l0(e_w_in, e_w_out, gate_w, ple_w):
    slots = []
    for n in range(8):
        slots.append(_fm_slot(e_w_in, 4096 + 128 * n))
        slots.append(_fm_slot(e_w_in, 5120 + 128 * n))
    for h in range(8):
        slots.append(_fm_slot(e_w_in, 1024 + 128 * h))
        slots.append(_fm_slot(e_w_in, 128 * h))
        slots.append(_fm_slot(e_w_in, 3072 + 128 * h))
    for g in range(2):
        slots += _tm_slots(e_w_in, 2048 + 512 * g)
    slots += pack_tail(e_w_out, gate_w, ple_w)
    return np.stack(slots).astype(np.float32)


def make_consts():
    c = np.zeros((128, 7, 128), np.float32)
    idx = np.arange(128)
    c[:, 0] = np.eye(128)
    same = (idx[:, None] // 64) == (idx[None, :] // 64)
    c[:, 1] = (same & (idx[:, None] <= idx[None, :])).astype(np.float32)
    c[:, 2] = (idx[:, None] < 64).astype(np.float32) * np.ones((1, 128), np.float32)
    c[:, 3] = (idx[:, None] >= 64).astype(np.float32) * np.ones((1, 128), np.float32)
    c[:, 4] = 1.0 / 128.0
    c[:, 5] = same.astype(np.float32)
    c[:, 6] = (idx[:, None] <= idx[None, :]).astype(np.float32)
    return c.reshape(128, 896)


class Ctx:
    pass


def fence(C):
    P = C.P
    ops = [P.ops[e][-1] for e in ENGS if P.ops[e]] + list(P.dma_last.values())
    C.fence_ops = ops
    for b in C.bufs.values():
        b.readers = b.readers + ops


def build_program(layers, fused=False):
    nc = bass.Bass("TRN2", target_bir_lowering=False)
    from contextlib import ExitStack
    st = ExitStack()
    with st:
        P = Prog(nc)
        C = Ctx()
        C.nc, C.P = nc, P
        C.bufs = {}
        C.fence_ops = []
        C.sbs = {}
        C.drams = {}

        def dram(name, shape, dt=F32, kind="ExternalInput"):
            if name not in C.drams:
                if kind == "Internal":
                    C.drams[name] = nc.dram_tensor(name, list(shape), dt, kind=kind, addr_space="Local").ap()
                else:
                    C.drams[name] = nc.dram_tensor(name, list(shape), dt, kind=kind).ap()
            return C.drams[name]

        def sb(name, shape, dt=F32):
            if name not in C.sbs:
                C.sbs[name] = st.enter_context(nc.sbuf_tensor(name, list(shape), dt))
            return C.sbs[name]

        def B(name):
            if name not in C.bufs:
                b = Buf(name)
                b.readers = list(C.fence_ops)
                C.bufs[name] = b
            return C.bufs[name]

        C.dram, C.sb, C.B = dram, sb, B
        C.PA = st.enter_context(nc.psum_tensor("PA", [128, 1024], F32))
        C.PB = st.enter_context(nc.psum_tensor("PB", [128, 1024], F32))
        C.PC = st.enter_context(nc.psum_tensor("PC", [128, 1024], F32))
        C.PT = st.enter_context(nc.psum_tensor("PT", [128, 2048], BF16))
        C.wring = [sb(f"wr{i}", [128, 2048], BF16) for i in range(NW)]
        C.wslot_n = 0
        C.consts = sb("consts", [128, 896], F32)
        C.ident_bf = sb("ident_bf", [128, 128], BF16)
        C.gain = sb("gain", [128, 2048], F32)
        C.big = sb("big", [128, KC, T], BF16)
        C.bigb = [B(f"big{kc}") for kc in range(KC)]
        C.arena = sb("arena", [128, ARENA], BF16)
        C.small = sb("small", [128, 64], F32)
        C.stf = sb("stf", [128, 8 * 260], F32)
        C.stb = sb("stb", [128, 8 * 260], BF16)
        C.msk = sb("msk_sb", [128, 1], F32)
        consts_d = dram("consts_d", [128, 896])
        P.dma("sp", "c", C.consts[:], consts_d, writes=[B("consts")])
        P.dma("sp", "c", C.msk[:], dram("msk", [128, 1]), writes=[B("msk")])
        P.emit("dve", lambda e: e.tensor_copy(out=C.ident_bf[:], in_=C.consts[:, 0:128]),
               reads=[B("consts")], writes=[B("ident_bf")])
        C.out_keys = []
        import os
        C.stop = int(os.environ.get('STOP', '99'))
        C.rstop = int(os.environ.get('RSTOP', '99'))
        if layers == "fused":
            layer0(C, "A")
            fence(C)
            layer1(C, "A")
            fence(C)
            layer0(C, "B")
            fence(C)
            layer1(C, "B")
        else:
            for li in layers:
                if li == 0:
                    layer0(C, "U")
                else:
                    layer1(C, "U")
        run_prog(nc, P, final_waits=[k for k in C.out_keys if k in P.dma_sem_val])
    return nc


def ACT(C, out, in_, func, R, W, **kw):
    return C.P.emit("act", lambda e: e.activation(out=out, in_=in_, func=func, **kw), reads=R, writes=W)


def TS(C, out, in0, s1, s2, op0, op1, R, W, eng="dve"):
    return C.P.emit(eng, lambda e: e.tensor_scalar(out=out, in0=in0, scalar1=s1, scalar2=s2, op0=op0, op1=op1),
                    reads=R, writes=W)


def TT(C, out, in0, in1, op, R, W, eng="dve"):
    return C.P.emit(eng, lambda e: e.tensor_tensor(out=out, in0=in0, in1=in1, op=op), reads=R, writes=W)


def STT(C, out, in0, scalar, in1, op0, op1, R, W):
    return C.P.emit("dve", lambda e: e.scalar_tensor_tensor(out=out, in0=in0, scalar=scalar, in1=in1, op0=op0, op1=op1),
                    reads=R, writes=W)


def CP(C, out, in_, R, W, eng="dve"):
    return C.P.emit(eng, lambda e: e.tensor_copy(out=out, in_=in_), reads=R, writes=W)


def MM(C, out, lhsT, rhs, start, stop, R, W):
    return C.P.emit("pe", lambda e: e.matmul(out, lhsT=lhsT, rhs=rhs, start=start, stop=stop),
                    reads=R, writes=W, acc_ok=W)


def TR(C, out, in_, R, W):
    return C.P.emit("pe", lambda e: e.transpose(out, in_, C.ident_bf[:]), reads=R + [C.B("ident_bf")], writes=W, acc_ok=W)


def load_w(C, wd, slot):
    i = C.wslot_n % NW
    C.wslot_n += 1
    b = C.B(f"wr{i}")
    C.P.dma("pool", f"w{i}", C.wring[i][:], wd[slot], writes=[b])
    return C.wring[i], b


def rms_to_T(C, src_tile, src_buf, j, gain_ready_buf, dstT, dst_bufs, hb):
    B = C.B
    sm = C.small[:, 16 + 4 * hb:20 + 4 * hb]
    junk = C.hbf[hb]
    ACT(C, junk[:], src_tile, AF.Square, [src_buf], [B(f"hbf{hb}"), B(f"sm_ssq{hb}")], accum_out=sm[:, 0:1])
    ACT(C, sm[:, 1:2], sm[:, 0:1], AF.Sqrt, [B(f"sm_ssq{hb}")], [B(f"sm_sd{hb}")], scale=1.0 / D, bias=EPS)
    C.P.emit("dve", lambda e: e.reciprocal(out=sm[:, 2:3], in_=sm[:, 1:2]), reads=[B(f"sm_sd{hb}")], writes=[B(f"sm_rstd{hb}")])
    STT(C, junk[:], src_tile, sm[:, 2:3], C.gain[:], ALU.mult, ALU.mult,
        [src_buf, B(f"sm_rstd{hb}"), gain_ready_buf], [B(f"hbf{hb}")])
    for half in range(2):
        ptv = C.PT[:, half * 1024:(half + 1) * 1024]
        pb = B(f"PT{half}")
        for k in range(8):
            kc = half * 8 + k
            TR(C, ptv[:, k * 128:(k + 1) * 128], junk[:, kc * 128:(kc + 1) * 128], [B(f"hbf{hb}")], [pb])
        eng = "act" if half == 0 else "dve"
        dst = dstT[:, half * 8:(half + 1) * 8, j * 128:(j + 1) * 128]
        srcv = ptv.rearrange("p (a b) -> p a b", b=128)
        if eng == "act":
            ACT(C, dst, srcv, AF.Copy, [pb], dst_bufs)
        else:
            CP(C, dst, srcv, [pb], dst_bufs)


def alias_bufs(new_bufs, old_bufs):
    ops = []
    for ob in old_bufs:
        ops += ob.readers
        if ob.last_w is not None:
            ops.append(ob.last_w)
    for nb in new_bufs:
        nb.readers = nb.readers + ops


def inproj_fm(C, wd, slot, PS, psb):
    wt, wb = load_w(C, wd, slot)
    for half in range(2):
        for kc in range(KC):
            MM(C, PS[:, half * 512:(half + 1) * 512], wt[:, kc * 128:(kc + 1) * 128],
               C.big[:, kc, half * 512:(half + 1) * 512], kc == 0, kc == KC - 1,
               [wb, C.bigb[kc]], [psb])


def tm_group(C, wd, slot0, K4, lhs_fn, lhs_bufs_fn, consume):
    wts = [load_w(C, wd, slot0 + i) for i in range(K4)]
    for j in range(NT):
        PS, nm = [(C.PA, "PA"), (C.PB, "PB")][(j // 2) % 2]
        ps = PS[:, (j % 2) * 512:(j % 2) * 512 + 512]
        pb = C.B(f"{nm}h{j % 2}")
        nk = K4 * 4
        for kc in range(nk):
            wt, wb = wts[kc // 4]
            MM(C, ps, lhs_fn(kc, j), wt[:, (kc % 4) * 512:(kc % 4) * 512 + 512], kc == 0, kc == nk - 1,
               [wb] + lhs_bufs_fn(kc, j), [pb])
        consume(j, ps, pb)


def layer0(C, seg="U"):
    P, B, nc = C.P, C.B, C.nc
    dram, sb = C.dram, C.sb
    w0 = dram("w0", [L0_NSLOT, 128, 2048])
    wri_d = dram("wri", [128, 16, 128])
    sm_d = dram("sm0", [128, 96])
    g_e = dram("g_e", [D])
    g_ple = dram("g_ple0", [D])
    if seg == "U":
        hin, p0 = dram("hin", [T, D]), dram("p0", [T, 256])
        stS_d, stv_d = dram("stS", [128, 1024]), dram("stv", [128, 32])
        hout = dram("hout", [T, D], kind="ExternalOutput")
        stS_o = dram("stS_o", [128, 1024], kind="ExternalOutput")
        stv_o = dram("stv_o", [128, 32], kind="ExternalOutput")
        nm = {"hin": "d_hin", "hout": "d_hout", "stS": "d_stS", "stv": "d_stv", "stS_o": "d_stS_o", "stv_o": "d_stv_o"}
        C.out_keys += ["hout0", "hout1", "stS_o", "stv_o"]
    elif seg == "A":
        hin, p0 = dram("xA", [T, D]), dram("p0A", [T, 256])
        stS_d, stv_d = dram("zS", [128, 2080])[:, 0:1024], dram("zv", [128, 48])[:, 0:32]
        hout = dram("h2A", [T, D], kind="Internal")
        stS_o = dram("sS0", [128, 1024], kind="Internal")
        stv_o = dram("sv0", [128, 32], kind="Internal")
        nm = {"hin": "d_xA", "hout": "d_h2A", "stS": "d_zS", "stv": "d_zv", "stS_o": "d_sS0", "stv_o": "d_sv0"}
    else:
        hin, p0 = dram("xB", [T, D]), dram("p0B", [T, 256])
        stS_d, stv_d = dram("sS0", [128, 1024], kind="Internal"), dram("sv0", [128, 32], kind="Internal")
        hout = dram("h2B", [T, D], kind="Internal")
        stS_o = stv_o = None
        nm = {"hin": "d_xB", "hout": "d_h2B", "stS": "d_sS0", "stv": "d_sv0"}

    A = C.arena
    def abf(off, a, b):
        return A[:, off:off + a * b].rearrange("p (a b) -> p a b", b=b)
    G1 = abf(0, 8, 1024)
    G2 = abf(8192, 8, 1024)
    Vt = abf(8192, 8, 1024)
    qT = abf(16384, 8, 1024)
    kT = abf(24576, 8, 1024)
    zaT = abf(32768, 8, 1024)
    TB = 40960
    def tmpf(i, w=1032):
        return A[:, TB + i * 2064:TB + (i + 1) * 2064].bitcast(F32)[:, 0:w]
    tmps = [tmpf(i) for i in range(7)]
    tb = [B(f"tmp{i}") for i in range(7)]
    XCB = A[:, TB + 7 * 2064:TB + 7 * 2064 + 1024]
    ZB = A[:, TB + 7 * 2064 + 1024:TB + 7 * 2064 + 2048]
    htile = [A[:, 4096 * i:4096 * (i + 1)].bitcast(F32) for i in range(4)]
    hbf = [A[:, 16384 + 2048 * i:16384 + 2048 * (i + 1)] for i in range(4)]
    C.htile, C.hbf = htile, hbf
    HP = A[:, 0:32768].bitcast(F32).rearrange("p (a b) -> p a b", b=2048)

    wri = sb("wri_sb", [128, 16, 128], BF16)
    sm = sb("sm0_sb", [128, 96], F32)
    smx = sb("smx", [128, 64], F32)
    plast = sb("plast", [128, 8, 17], F32)
    PP = sb("PP", [128, 8, 8], F32)
    khat = [sb(f"khat{i}", [128, 64], BF16) for i in range(2)]
    stv = sb("stv_sb", [128, 32], F32)
    stvo = sb("stvo_sb", [128, 32], F32)
    ktok = [sb(f"ktok{i}", [128, 128], BF16) for i in range(2)]
    attm = [sb(f"attm{i}", [128, 128], BF16) for i in range(2)]
    sqf = [A[:, TB + 256 * i:TB + 256 * (i + 1)].bitcast(F32) for i in range(2)]
    oc = [A[:, TB + 512 + 256 * i:TB + 512 + 256 * (i + 1)].bitcast(F32) for i in range(2)]
    sqb = [A[:, TB + 1024 + 128 * i:TB + 1024 + 128 * (i + 1)] for i in range(2)]
    ones_bf = sb("ones_bf", [128, 128], BF16)
    rpl = sb("rpl", [128, 16], F32)

    P.dma("pool", "wri", wri[:], wri_d, writes=[B("wri")])
    P.dma("sp", "sm", sm[:], sm_d, writes=[B("sm")])
    P.dma("sp", "stv", stv[:], stv_d, reads=[B(nm["stv"])], writes=[B("stv")])
    P.dma("sp", "stS", C.stf[:, 0:1024], stS_d, reads=[B(nm["stS"])], writes=[B("stf")])
    TS(C, stv[:], stv[:], C.msk[:, 0:1], None, ALU.mult, ALU.bypass, [B("stv"), B("msk")], [B("stv")])
    TS(C, C.stf[:, 0:1024], C.stf[:, 0:1024], C.msk[:, 0:1], None, ALU.mult, ALU.bypass, [B("stf"), B("msk")], [B("stf")])
    P.emit("pool", lambda e: e.memset(ZB, 0.0), writes=[B("zb")])
    P.emit("pool", lambda e: e.memset(plast[:], 1.0), writes=[B("plast")])
    TT(C, smx[:, 0:8], sm[:, 0:8], sm[:, 8:16], ALU.subtract, [B("sm")], [B("smx")])
    ACT(C, smx[:, 0:8], smx[:, 0:8], AF.Sigmoid, [B("smx")], [B("smx")])
    TS(C, smx[:, 8:16], smx[:, 0:8], -1.0, 1.0, ALU.mult, ALU.add, [B("smx")], [B("smx")])
    ACT(C, smx[:, 32:40], sm[:, 80:88], AF.Exp, [B("sm")], [B("smx")], scale=-1.0)
    ACT(C, smx[:, 32:40], smx[:, 32:40], AF.Ln, [B("smx")], [B("smx")], bias=1.0)
    TS(C, smx[:, 16:24], smx[:, 32:40], -8.0, None, ALU.mult, ALU.bypass, [B("smx")], [B("smx")])
    TS(C, smx[:, 24:32], smx[:, 32:40], -16.0, None, ALU.mult, ALU.bypass, [B("smx")], [B("smx")])

    P.dma("sp", "g", C.gain[:], g_e.partition_broadcast(128), writes=[B("gain")])
    for j in range(NT):
        hb = j % 4
        P.dma("sp", f"h{hb}", htile[hb], hin[j * 128:(j + 1) * 128, :], reads=[B(nm["hin"])], writes=[B(f"htile{hb}")])
        rms_to_T(C, htile[hb], B(f"htile{hb}"), j, B("gain"), C.big, C.bigb, hb)
    if C.stop <= 1:
        return
    mix_bufs = [B(n) for n in ("G1", "G2", "qT", "kT", "zaT")] + tb + [B("xcb"), B("Vt")]
    alias_bufs(mix_bufs, [B(f"htile{i}") for i in range(4)] + [B(f"hbf{i}") for i in range(4)])

    X, XC, R_, I_, M_, AC, ZS = tmps
    bX, bXC, bR, bI, bM, bAC, bZS = tb
    slot = 0
    inproj_fm(C, w0, slot, C.PA, B("PA")); slot += 1
    for n in range(8):
        ACT(C, X[:, 3:1027], C.PA[:], AF.Copy, [B("PA")], [bX])
        CP(C, X[:, 0:3], stv[:, 8 + 3 * n:11 + 3 * n], [B("stv")], [bX])
        CP(C, stvo[:, 8 + 3 * n:11 + 3 * n], X[:, 1024:1027], [bX], [B("stvo")])
        inproj_fm(C, w0, slot, C.PA, B("PA")); slot += 1
        ACT(C, ZS[:, 0:1024], C.PA[:], AF.Silu, [B("PA")], [bZS])
        if n + 1 < 8:
            inproj_fm(C, w0, slot, C.PA, B("PA")); slot += 1
        cw = lambda k: sm[:, 24 + 4 * n + k:25 + 4 * n + k]
        TS(C, XC[:, 0:1024], X[:, 3:1027], cw(3), sm[:, 56 + n:57 + n], ALU.mult, ALU.add, [bX, B("sm")], [bXC])
        for k in (2, 1, 0):
            STT(C, XC[:, 0:1024], X[:, k:k + 1024], cw(k), XC[:, 0:1024], ALU.mult, ALU.add, [bX, bXC, B("sm")], [bXC])
        ACT(C, XCB, XC[:, 0:1024], AF.Copy, [bXC], [B("xcb")])
        for half in range(2):
            MM(C, C.PB[:, half * 512:(half + 1) * 512], wri[:, n, :], XCB[:, half * 512:(half + 1) * 512], True, True,
               [B("wri"), B("xcb")], [B("PB")])
            MM(C, C.PC[:, half * 512:(half + 1) * 512], wri[:, 8 + n, :], XCB[:, half * 512:(half + 1) * 512], True, True,
               [B("wri"), B("xcb")], [B("PC")])
        ACT(C, R_[:, 0:1024], C.PB[:], AF.Sigmoid, [B("PB"), B("sm")], [bR], bias=sm[:, 64 + n:65 + n])
        ACT(C, I_[:, 0:1024], C.PC[:], AF.Sigmoid, [B("PC"), B("sm")], [bI], bias=sm[:, 72 + n:73 + n])
        ACT(C, M_[:, 0:1024], R_[:, 0:1024], AF.Exp, [bR, B("smx")], [bM], scale=smx[:, 24 + n:25 + n])
        ACT(C, R_[:, 0:1024], R_[:, 0:1024], AF.Exp, [bR, B("smx")], [bR], scale=smx[:, 16 + n:17 + n])
        ACT(C, M_[:, 0:1024], M_[:, 0:1024], AF.Sqrt, [bM], [bM], scale=-1.0, bias=1.0)
        TT(C, I_[:, 0:1024], I_[:, 0:1024], XC[:, 0:1024], ALU.mult, [bI, bXC], [bI])
        TT(C, I_[:, 0:1024], I_[:, 0:1024], M_[:, 0:1024], ALU.mult, [bI, bM], [bI])
        P.emit("dve", lambda e: e.tensor_tensor_scan(out=M_[:, 0:1024], data0=R_[:, 0:1024], data1=I_[:, 0:1024],
                                                     initial=0.0, op0=ALU.mult, op1=ALU.add),
               reads=[bR, bI], writes=[bM])
        P.emit("dve", lambda e: e.tensor_tensor_scan(out=AC[:, 0:1024], data0=R_[:, 0:1024], data1=ZB,
                                                     initial=1.0, op0=ALU.mult, op1=ALU.add),
               reads=[bR, B("zb")], writes=[bAC])
        TT(C, G1[:, n, :], M_[:, 0:1024], ZS[:, 0:1024], ALU.mult, [bM, bZS], [B("G1")])
        TT(C, G2[:, n, :], AC[:, 0:1024], ZS[:, 0:1024], ALU.mult, [bAC, bZS], [B("G2")])
        CP(C, stvo[:, n:n + 1], M_[:, 1023:1024], [bM], [B("stvo")])
    if C.stop <= 2:
        return
    F_, KK, Pc, Rc, Q_ = tmps[0], tmps[1], tmps[2], tmps[3], tmps[4]
    bF, bKK, bPc, bRc, bQ = tb[0], tb[1], tb[2], tb[3], tb[4]
    for h in range(8):
        inproj_fm(C, w0, slot, C.PA, B("PA")); slot += 1
        ACT(C, F_[:, 0:1024], C.PA[:], AF.Sigmoid, [B("PA")], [bF])
        TS(C, F_[:, 0:1024], F_[:, 0:1024], smx[:, 8 + h:9 + h], smx[:, h:h + 1], ALU.mult, ALU.add, [bF, B("smx")], [bF])
        TS(C, KK[:, 0:1024], F_[:, 0:1024], -1.0, 1.0, ALU.mult, ALU.add, [bF], [bKK])
        for c in range(16):
            P.emit("dve", lambda e, c=c: e.tensor_tensor_scan(out=Pc[:, c * 64:(c + 1) * 64], data0=F_[:, c * 64:(c + 1) * 64],
                                                              data1=ZB[:, 0:64], initial=1.0, op0=ALU.mult, op1=ALU.add),
                   reads=[bF, B("zb")], writes=[bPc])
        P.emit("dve", lambda e: e.reciprocal(out=Rc[:, 0:1024], in_=Pc[:, 0:1024]), reads=[bPc], writes=[bRc])
        TT(C, kT[:, h, :], KK[:, 0:1024], Rc[:, 0:1024], ALU.mult, [bKK, bRc], [B("kT")])
        CP(C, plast[:, h, 1:17], Pc[:, 0:1024].rearrange("p (c s) -> p c s", s=64)[:, :, 63], [bPc], [B("plast")])
        plv = plast[:, h, 0:16].rearrange("p (j two) -> p j two", two=2)
        TT(C, PP[:, h, :], plv[:, :, 0], plv[:, :, 1], ALU.mult, [B("plast")], [B("PP")])
        inproj_fm(C, w0, slot, C.PB, B("PB")); slot += 1
        ACT(C, Q_[:, 0:1024], C.PB[:], AF.Silu, [B("PB")], [bQ])
        TT(C, qT[:, h, :], Q_[:, 0:1024], Pc[:, 0:1024], ALU.mult, [bQ, bPc], [B("qT")])
        inproj_fm(C, w0, slot, C.PC, B("PC")); slot += 1
        ACT(C, zaT[:, h, :], C.PC[:], AF.Silu, [B("PC")], [B("zaT")])
    if C.stop <= 3:
        return
    for n in range(8):
        STT(C, G1[:, n, :], G2[:, n, :], stv[:, n:n + 1], G1[:, n, :], ALU.mult, ALU.add, [B("G2"), B("G1"), B("stv")], [B("G1")])
    alias_bufs([B("Vt")], [B("G2")])
    alias_bufs([B("PAh0"), B("PAh1"), B("PBh0"), B("PBh1")], [B("PA"), B("PB")])
    for g in range(2):
        def cons(j, ps, pb, g=g):
            ACT(C, Vt[:, j, g * 512:(g + 1) * 512], ps, AF.Copy, [pb], [B("Vt")])
        tm_group(C, w0, slot, 4, lambda kc, j: C.big[:, kc, j * 128:(j + 1) * 128], lambda kc, j: [C.bigb[kc]], cons)
        slot += 4
    if C.stop <= 4:
        return
    for n in range(8):
        CP(C, C.big[:, 8 + n, :], G1[:, n, :], [B("G1")], [C.bigb[8 + n]], eng="pool")
    alias_bufs([B("PCh0"), B("PCh1")], [B("PC")])
    alias_bufs([B("sqf0"), B("sqf1"), B("oc0"), B("oc1"), B("sqb0"), B("sqb1")], tb)
    CP(C, ones_bf[:], C.consts[:, 512:640], [B("consts")], [B("ones_bf")])
    U = C.stf[:, 0:1024].rearrange("p (h e) -> p h e", e=128)
    Sb = C.stb[:, 0:1024].rearrange("p (h e) -> p h e", e=128)
    Sh = C.stb[:, 1024:2048].rearrange("p (h e) -> p h e", e=128)
    caus = C.consts[:, 768:896]
    ident = C.consts[:, 0:128]
    maskA = C.consts[:, 128:256]
    ones128 = C.consts[:, 512:640]
    for h in range(8):
        ACT(C, Sb[:, h, :], U[:, h, :], AF.Copy, [B("stf")], [B(f"Sb{h}")])
        ACT(C, Sh[:, h, :], U[:, h, :], AF.Copy, [B("stf"), B("PP")], [B(f"Sh{h}")], scale=PP[:, h, 0:1])
    def FM(j, h, r):
        p_att = C.PC[:, r * 512 + 256:r * 512 + 384]
        b_att = B(f"PCh{r}")
        KVP, kvn = [(C.PB, "PB"), (C.PA, "PA")][r]
        p_kv0, p_kv1 = KVP[:, 0:128], KVP[:, 512:640]
        b_kv0, b_kv1 = B(kvn + "h0"), B(kvn + "h1")
        p_o = C.PC[:, r * 512:r * 512 + 128]
        b_o = B(f"PCh{r}")
        tk = slice(j * 128, (j + 1) * 128)
        ptk = C.PT[:, r * 1024:r * 1024 + 128]
        c0 = 2 * j
        t0_, t1_ = slice(j * 128, j * 128 + 64), slice(j * 128 + 64, (j + 1) * 128)
        TR(C, ptk, kT[:, h, tk], [B("kT")], [B(f"PT{r}")])
        MM(C, p_att, kT[:, h, tk], qT[:, h, tk], True, True, [B("kT"), B("qT")], [b_att])
        ACT(C, khat[r][:], kT[:, h, t0_], AF.Copy, [B("kT"), B("plast")], [B(f"khat{r}")], scale=plast[:, h, c0 + 1:c0 + 2])
        MM(C, p_att[0:64, 64:128], khat[r][:], qT[:, h, t1_], True, True, [B(f"khat{r}"), B("qT")], [b_att])
        ACT(C, ktok[r][:], ptk, AF.Copy, [B(f"PT{r}")], [B(f"ktok{r}")])
        TT(C, attm[r][:], p_att, caus, ALU.mult, [b_att, B("consts")], [B(f"attm{r}")])
        MM(C, p_kv0, ktok[r][0:64, :], Vt[0:64, j, h * 128:(h + 1) * 128], True, True, [B(f"ktok{r}"), B("Vt")], [b_kv0])
        MM(C, p_kv1, ktok[r][64:128, :], Vt[64:128, j, h * 128:(h + 1) * 128], True, True, [B(f"ktok{r}"), B("Vt")], [b_kv1])
        MM(C, p_o[:, 0:64], Sb[:, h, :], qT[:, h, t0_], True, False, [B(f"Sb{h}"), B("qT")], [b_o])
        MM(C, p_o[:, 64:128], Sh[:, h, :], qT[:, h, t1_], False, False, [B(f"Sh{h}"), B("qT")], [b_o])
        MM(C, p_o, Vt[:, j, h * 128:(h + 1) * 128], attm[r][:], False, True, [B("Vt"), B(f"attm{r}")], [b_o])
        ACT(C, oc[r][:], p_o, AF.Copy, [b_o], [B(f"oc{r}")])
        ACT(C, sqb[r][:], oc[r][:], AF.Square, [B(f"oc{r}")], [B(f"sqb{r}")])
        p_ms = KVP[:, 128:256]
        STT(C, U[:, h, :], U[:, h, :], plast[:, h, c0:c0 + 1], p_kv0, ALU.mult, ALU.add, [B(f"U{h}"), B("stf"), B("plast"), b_kv0], [B(f"U{h}")])
        STT(C, U[:, h, :], U[:, h, :], plast[:, h, c0 + 1:c0 + 2], p_kv1, ALU.mult, ALU.add, [B(f"U{h}"), B("plast"), b_kv1], [B(f"U{h}")])
        MM(C, p_ms, ones_bf[:], sqb[r][:], True, True, [B("ones_bf"), B(f"sqb{r}")], [b_kv0])
        ACT(C, Sb[:, h, :], U[:, h, :], AF.Copy, [B(f"U{h}"), B("plast")], [B(f"Sb{h}")], scale=plast[:, h, c0 + 2:c0 + 3])
        if j + 1 < NT:
            ACT(C, Sh[:, h, :], U[:, h, :], AF.Copy, [B(f"U{h}"), B("PP")], [B(f"Sh{h}")], scale=PP[:, h, j + 1:j + 2])

    def KK_(j, h, r):
        KVP, kvn = [(C.PB, "PB"), (C.PA, "PA")][r]
        p_ms = KVP[:, 128:256]
        b_ms = B(kvn + "h0")
        tk = slice(j * 128, (j + 1) * 128)
        ACT(C, sqf[r][:], p_ms, AF.Ln, [b_ms], [B(f"sqf{r}")], bias=EPS)
        ACT(C, sqf[r][:], sqf[r][:], AF.Exp, [B(f"sqf{r}")], [B(f"sqf{r}")], scale=-0.5)
        STT(C, sqf[r][:], oc[r][:], sm[:, 16 + h:17 + h], sqf[r][:], ALU.mult, ALU.mult, [B(f"oc{r}"), B("sm"), B(f"sqf{r}")], [B(f"sqf{r}")])
        TT(C, C.big[:, h, tk], sqf[r][:], zaT[:, h, tk], ALU.mult, [B(f"sqf{r}"), B("zaT")], [C.bigb[h]], eng="pool")

    its = [(j, h) for j in range(NT) for h in range(8)]
    FM(its[0][0], its[0][1], 0)
    for i in range(len(its)):
        if i + 1 < len(its):
            FM(its[i + 1][0], its[i + 1][1], (i + 1) % 2)
        KK_(its[i][0], its[i][1], i % 2)
    for h in range(8):
        ACT(C, C.stf[:, h * 128:(h + 1) * 128], U[:, h, :], AF.Copy, [B(f"U{h}"), B("plast")], [B(f"U{h}")], scale=plast[:, h, 16:17])
    if stS_o is not None:
        P.dma("sp", "stS_o", stS_o, C.stf[:, 0:1024], reads=[B(f"U{h}") for h in range(8)], writes=[B(nm["stS_o"])])
        P.dma("sp", "stv_o", stv_o, stvo[:], reads=[B("stvo")], writes=[B(nm["stv_o"])])
    if C.stop <= 5:
        return
    stage4(C, w0, slot, hin, p0, rpl, g_ple, hout, mix_bufs + [B("xcb"), B("zb"), B("sqf0"), B("sqf1"), B("oc0"), B("oc1"), B("sqb0"), B("sqb1")], HP, tmps, tb, None, nm)


def to_T(C, src_bf, src_buf, j, dstT, dst_bufs):
    B = C.B
    for half in range(2):
        ptv = C.PT[:, half * 1024:(half + 1) * 1024]
        pb = B(f"PT{half}")
        for k in range(8):
            kc = half * 8 + k
            TR(C, ptv[:, k * 128:(k + 1) * 128], src_bf[:, kc * 128:(kc + 1) * 128], [src_buf], [pb])
        dst = dstT[:, half * 8:(half + 1) * 8, j * 128:(j + 1) * 128]
        srcv = ptv.rearrange("p (a b) -> p a b", b=128)
        if half == 0:
            ACT(C, dst, srcv, AF.Copy, [pb], dst_bufs[half * 8:(half + 1) * 8])
        else:
            CP(C, dst, srcv, [pb], dst_bufs[half * 8:(half + 1) * 8])


def stage4(C, wd, slot, hin, p_d, rpl, g_ple, hout, old_bufs, HP, tmps, tb, final_gain, names):
    P, B = C.P, C.B
    A = C.arena
    pT_buf = B("pT")
    pT = A[:, 55408:57456].rearrange("p (a b) -> p a b", b=T)
    hpb = [B(f"HP{j}") for j in range(NT)]
    hb2 = [A[:, 32768 + 2048 * i:32768 + 2048 * (i + 1)] for i in range(4)]
    hb2b = [B(f"hb2_{i}") for i in range(4)]
    alias_bufs(hpb + hb2b + tb + [pT_buf], old_bufs)
    alias_bufs([B("PAh0"), B("PAh1"), B("PBh0"), B("PBh1"), B("PCh0"), B("PCh1"), B("PT0"), B("PT1")],
               [B(n) for n in ("PA", "PB", "PC", "PT0", "PT1", "PAh0", "PAh1", "PBh0", "PBh1", "PCh0", "PCh1")])
    st = [tmps[0][:, 0:512], tmps[1][:, 0:512]]
    for j in range(NT):
        i = j % 2
        pst = [tmps[5], tmps[6]][i]
        P.dma("sp", f"hs{i}", pst[:, 0:256], p_d[j * 128:(j + 1) * 128, :], writes=[tb[5 + i]])
        CP(C, hb2[i][:, 0:256], pst[:, 0:256], [tb[5 + i]], [hb2b[i]])
        for kc in range(2):
            TR(C, C.PT[:, kc * 128:(kc + 1) * 128], hb2[i][:, kc * 128:(kc + 1) * 128], [hb2b[i]], [B("PT0")])
        CP(C, pT[:, :, j * 128:(j + 1) * 128], C.PT[:, 0:256].rearrange("p (a b) -> p a b", b=128), [B("PT0")], [pT_buf])
    cnt = [0]
    for g in range(4):
        def cons(j, ps, pb, g=g):
            i = cnt[0] % 2
            cnt[0] += 1
            P.dma("sp", f"hs{i}", st[i], hin[j * 128:(j + 1) * 128, g * 512:(g + 1) * 512], reads=[B(names["hin"])], writes=[tb[i]])
            TT(C, HP[:, j, g * 512:(g + 1) * 512], ps, st[i], ALU.add, [pb, tb[i]], [hpb[j]])
        tm_group(C, wd, slot, 4, lambda kc, j: C.big[:, kc, j * 128:(j + 1) * 128], lambda kc, j: [C.bigb[kc]], cons)
        slot += 4
    if C.stop <= 6:
        return slot
    plw = [load_w(C, wd, slot + i) for i in range(4)]
    slot += 4
    sm = C.small
    junk = tmps[2]
    for j in range(NT):
        i = j % 4
        if j % 2 == 0:
            ACT(C, hb2[i], HP[:, j, :], AF.Copy, [hpb[j]], [hb2b[i]])
        else:
            CP(C, hb2[i], HP[:, j, :], [hpb[j]], [hb2b[i]])
        for g in range(4):
            PS, nm = [(C.PA, "PA"), (C.PB, "PB")][g // 2]
            ps = PS[:, (g % 2) * 512:(g % 2) * 512 + 512]
            pb = B(f"{nm}h{g % 2}")
            wt, wb = plw[g]
            for kc in range(2):
                MM(C, ps, pT[:, kc, j * 128:(j + 1) * 128], wt[:, kc * 512:(kc + 1) * 512], kc == 0, kc == 1, [wb, pT_buf], [pb])
        to_T(C, hb2[i], hb2b[i], j, C.big, C.bigb)
        q = 8 + 4 * (j % 2)
        ACT(C, junk[:, 0:1024], C.PA[:], AF.Square, [B("PAh0"), B("PAh1")], [tb[2], B(f"sm_q0{j % 2}")], accum_out=sm[:, q:q + 1])
        ACT(C, junk[:, 0:1024], C.PB[:], AF.Square, [B("PBh0"), B("PBh1")], [tb[2], B(f"sm_q1{j % 2}")], accum_out=sm[:, q + 1:q + 2])
        TT(C, sm[:, q + 2:q + 3], sm[:, q:q + 1], sm[:, q + 1:q + 2], ALU.add, [B(f"sm_q0{j % 2}"), B(f"sm_q1{j % 2}")], [B(f"sm_q2{j % 2}")])
        ACT(C, sm[:, q + 3:q + 4], sm[:, q + 2:q + 3], AF.Sqrt, [B(f"sm_q2{j % 2}")], [B(f"sm_q3{j % 2}")], scale=1.0 / D, bias=EPS)
        P.emit("dve", lambda e, j=j, q=q: e.reciprocal(out=rpl[:, j:j + 1], in_=sm[:, q + 3:q + 4]), reads=[B(f"sm_q3{j % 2}")], writes=[B("rpl")])
    if C.stop <= 8:
        return slot
    P.dma("sp", "g", C.gain[:], g_ple.partition_broadcast(128), writes=[B("gain")])
    SG, PL = tmps[3], tmps[4]
    bSG, bPL = tb[3], tb[4]
    for g in range(4):
        pw, pwb = None, None

        def cons(j, ps, pb, g=g):
            ps2 = C.PC[:, (j % 2) * 512:(j % 2) * 512 + 512]
            pb2 = B(f"PCh{j % 2}")
            for kc in range(2):
                MM(C, ps2, pT[:, kc, j * 128:(j + 1) * 128], cons.pw[:, kc * 512:(kc + 1) * 512], kc == 0, kc == 1, [cons.pwb, pT_buf], [pb2])
            ACT(C, SG[:, 0:512], ps, AF.Sigmoid, [pb], [bSG])
            STT(C, PL[:, 0:512], ps2, rpl[:, j:j + 1], C.gain[:, g * 512:(g + 1) * 512], ALU.mult, ALU.mult, [pb2, B("rpl"), B("gain")], [bPL])
            TT(C, PL[:, 0:512], PL[:, 0:512], SG[:, 0:512], ALU.mult, [bPL, bSG], [bPL])
            TT(C, HP[:, j, g * 512:(g + 1) * 512], HP[:, j, g * 512:(g + 1) * 512], PL[:, 0:512], ALU.add, [hpb[j], bPL], [hpb[j]])
        cons.pw, cons.pwb = load_w(C, wd, slot + 4)
        tm_group(C, wd, slot, 4, lambda kc, j: C.big[:, kc, j * 128:(j + 1) * 128], lambda kc, j: [C.bigb[kc]], cons)
        slot += 5
    if C.stop <= 9:
        return slot
    if final_gain is not None:
        P.dma("sp", "g", C.gain[:], final_gain.partition_broadcast(128), writes=[B("gain")])
    for j in range(NT):
        i = j % 2
        if final_gain is None:
            for q in range(4):
                P.dma("sp", f"hout{i}", hout[j * 128:(j + 1) * 128, q * 512:(q + 1) * 512], HP[:, j, q * 512:(q + 1) * 512], reads=[hpb[j]], writes=[B(names["hout"])])
        else:
            ot = A[:, 32768 + i * 4096:32768 + (i + 1) * 4096].bitcast(F32)
            ob = B(f"ot{i}")
            if j < 2:
                alias_bufs([ob], hb2b)
            sm2 = C.small
            ACT(C, ot, HP[:, j, :], AF.Square, [hpb[j]], [ob, B("sm_ssq")], accum_out=sm2[:, 0:1])
            ACT(C, sm2[:, 1:2], sm2[:, 0:1], AF.Sqrt, [B("sm_ssq")], [B("sm_sd")], scale=1.0 / D, bias=EPS)
            P.emit("dve", lambda e: e.reciprocal(out=sm2[:, 2:3], in_=sm2[:, 1:2]), reads=[B("sm_sd")], writes=[B("sm_rstd")])
            STT(C, ot, HP[:, j, :], sm2[:, 2:3], C.gain[:], ALU.mult, ALU.mult, [hpb[j], B("sm_rstd"), B("gain")], [ob])
            for q in range(4):
                P.dma("sp", f"hout{i}", hout[j * 128:(j + 1) * 128, q * 512:(q + 1) * 512], ot[:, q * 512:(q + 1) * 512], reads=[ob], writes=[B(names["hout"])])
    return slot


def _pp(v, n):
    return np.ascontiguousarray(np.asarray(v, np.float32).reshape(n, 128).T)


def prep_l0(inp):
    sm = np.zeros((128, 96), np.float32)
    sm[:, 0:8] = _pp(inp["a_lb_logits"][0], 8)
    sm[:, 8:16] = _pp(inp["a_lb_logits"][1], 8)
    sm[:, 16:24] = _pp(inp["a_norm"][0], 8)
    sm[:, 24:56] = np.asarray(inp["b_conv_w"][0], np.float32).reshape(4, 8, 128).transpose(2, 1, 0).reshape(128, 32)
    sm[:, 56:64] = _pp(inp["b_conv_b"][0], 8)
    sm[:, 64:72] = _pp(inp["b_b_r"][0], 8)
    sm[:, 72:80] = _pp(inp["b_b_i"][0], 8)
    sm[:, 80:88] = _pp(inp["b_lambda"][0], 8)
    wri = np.concatenate([np.asarray(inp["b_w_r"][0], np.float32).transpose(1, 0, 2),
                          np.asarray(inp["b_w_i"][0], np.float32).transpose(1, 0, 2)], axis=1)
    return {
        "w0": pack_l0(np.asarray(inp["e_w_in"][0], np.float32), np.asarray(inp["e_w_out"][0], np.float32),
                      np.asarray(inp["ple_gate_w"][0], np.float32), np.asarray(inp["ple_w"][0], np.float32)),
        "wri": np.ascontiguousarray(wri), "sm0": sm,
        "g_e": np.asarray(inp["e_norm"][0], np.float32), "g_ple0": np.asarray(inp["ple_norm"][0], np.float32),
        "consts_d": make_consts(),
    }


def pack_l1(o_w_in, o_w_out, gate_w, ple_w):
    slots = []
    for g in range(4):
        slots += _tm_slots(o_w_in, 2048 + 512 * g)
    for g in range(4):
        slots += _tm_slots(o_w_in, 4096 + 512 * g)
        slots += _tm_slots(o_w_in, 6144 + 512 * g)
    for h in range(8):
        slots.append(_fm_slot(o_w_in, 128 * h))
        slots.append(_fm_slot(o_w_in, 1024 + 128 * h))
    slots += pack_tail(o_w_out, gate_w, ple_w)
    return np.stack(slots).astype(np.float32)


L1_NSLOT = 16 + 32 + 16 + 40


def prep_l1(inp):
    sm = np.zeros((128, 96), np.float32)
    sm[:, 0:64] = np.asarray(inp["c_conv_w"][0], np.float32).reshape(4, 16, 128).transpose(2, 1, 0).reshape(128, 64)
    sm[:, 64:80] = _pp(inp["c_conv_b"][0], 16)
    sm[:, 80:96] = _pp(inp["c_norm"][0], 16)
    wg = np.asarray(inp["o_w_in"][0][:, 8192:8208], np.float32).reshape(16, 128, 16).transpose(1, 0, 2)
    gb = np.concatenate([np.asarray(inp["c_b_i"][0], np.float32), np.asarray(inp["c_b_f"][0], np.float32)])
    return {
        "w1": pack_l1(np.asarray(inp["o_w_in"][0], np.float32), np.asarray(inp["o_w_out"][0], np.float32),
                      np.asarray(inp["ple_gate_w"][1], np.float32), np.asarray(inp["ple_w"][1], np.float32)),
        "wg": np.ascontiguousarray(wg), "sm1": sm, "gb": gb,
        "g_o": np.asarray(inp["o_norm"][0], np.float32), "g_ple1": np.asarray(inp["ple_norm"][1], np.float32),
        "g_fin": np.asarray(inp["final_norm"], np.float32),
        "consts_d": make_consts(),
    }


def layer1(C, seg="U"):
    P, B, nc = C.P, C.B, C.nc
    dram, sb = C.dram, C.sb
    w1 = dram("w1", [L1_NSLOT, 128, 2048])
    wg_d = dram("wg", [128, 16, 16])
    sm_d = dram("sm1", [128, 96])
    gb_d = dram("gb", [16])
    g_o = dram("g_o", [D])
    g_ple = dram("g_ple1", [D])
    g_fin = dram("g_fin", [D])
    states_only = (seg == "A")
    if seg == "U":
        hin, p1 = dram("hin1", [T, D]), dram("p1", [T, 256])
        stC_d, stv_d = dram("stC", [128, 8 * 260]), dram("stv1", [128, 48])
        hout = dram("hout1", [T, D], kind="ExternalOutput")
        stC_o = dram("stC_o", [128, 8 * 260], kind="ExternalOutput")
        stv_o = dram("stv1_o", [128, 48], kind="ExternalOutput")
        nm = {"hin": "d_hin1", "hout": "d_hout1", "stC": "d_stC", "stv": "d_stv1", "stC_o": "d_stC_o", "stv_o": "d_stv1_o"}
        C.out_keys += ["hout0", "hout1", "stC_o", "stv1_o"]
    elif seg == "A":
        hin, p1 = dram("h2A", [T, D], kind="Internal"), dram("p1A", [T, 256])
        stC_d, stv_d = dram("zS", [128, 2080]), dram("zv", [128, 48])
        hout = None
        stC_o = dram("sC1", [128, 8 * 260], kind="Internal")
        stv_o = dram("sv1", [128, 48], kind="Internal")
        nm = {"hin": "d_h2A", "hout": "d_none", "stC": "d_zS", "stv": "d_zv", "stC_o": "d_sC1", "stv_o": "d_sv1"}
    else:
        hin, p1 = dram("h2B", [T, D], kind="Internal"), dram("p1B", [T, 256])
        stC_d, stv_d = dram("sC1", [128, 8 * 260], kind="Internal"), dram("sv1", [128, 48], kind="Internal")
        hout = dram("out", [T, D], kind="ExternalOutput")
        stC_o = stv_o = None
        nm = {"hin": "d_h2B", "hout": "d_out", "stC": "d_sC1", "stv": "d_sv1"}
        C.out_keys += ["hout0", "hout1"]

    A = C.arena

    def abf(off, a, b):
        return A[:, off:off + a * b].rearrange("p (a b) -> p a b", b=b)
    qT = abf(0, 8, 1024)
    kT = abf(8192, 8, 1024)
    VX = abf(16384, 64, 260)
    GT = abf(33024, 8, 2048)
    TB1 = 49408
    SGO = A[:, TB1:TB1 + 8192].bitcast(F32).rearrange("p (a b) -> p a b", b=512)
    X = A[:, TB1:TB1 + 2064].bitcast(F32)
    XC = A[:, TB1 + 2064:TB1 + 4128].bitcast(F32)
    TB = 40960

    def tmpf(i, w=1032):
        return A[:, TB + i * 2064:TB + (i + 1) * 2064].bitcast(F32)[:, 0:w]
    tmps = [tmpf(i) for i in range(7)]
    tb = [B(f"tmp{i}") for i in range(7)]
    htile = [A[:, 4096 * i:4096 * (i + 1)].bitcast(F32) for i in range(4)]
    hbf = [A[:, 16384 + 2048 * i:16384 + 2048 * (i + 1)] for i in range(4)]
    C.htile, C.hbf = htile, hbf
    HP = A[:, 0:32768].bitcast(F32).rearrange("p (a b) -> p a b", b=2048)

    wg = sb("wg_sb", [128, 16, 16], BF16)
    sm = sb("sm1_sb", [128, 96], F32)
    gbb = sb("gbb", [128, 16], F32)
    stv = sb("stv1_sb", [128, 48], F32)
    stvo = sb("stvo1_sb", [128, 48], F32)
    EK = sb("EK", [128, 8, 8], F32)
    EKC = sb("EKC", [128, 8, 8], F32)
    THR = sb("THR", [128, 8, 8], F32)
    DEC = sb("DEC", [128, 8, 16], F32)
    g8 = sb("g8", [128, 4, 8], F32)
    r8 = sb("r8", [128, 2, 8], F32)
    ycb = A[:, TB1:TB1 + 2048]
    jk = A[:, TB1 + 2048:TB1 + 2304]
    ktok = [A[:, TB1 + 2304 + 128 * i:TB1 + 2432 + 128 * i] for i in range(2)]
    scw = [A[:, TB1 + 2560 + 128 * i:TB1 + 2688 + 128 * i] for i in range(2)]
    ndsb = [A[:, TB1 + 2816 + 520 * i:TB1 + 2816 + 520 * i + 516].bitcast(F32) for i in range(2)]
    rpl = sb("rpl1", [128, 16], F32)
    Cf = C.stf[:, :].rearrange("p (h e) -> p h e", e=260)
    Cb = C.stb[:, :].rearrange("p (h e) -> p h e", e=260)
    Chs = sb("Chs", [128, 8 * 260], BF16)
    Ch = Chs[:, :].rearrange("p (h e) -> p h e", e=260)

    P.dma("pool", "wri", wg[:], wg_d, writes=[B("wg")])
    P.dma("sp", "sm", sm[:], sm_d, writes=[B("sm1")])
    P.dma("sp", "sm", gbb[:], gb_d.partition_broadcast(128), writes=[B("gbb")])
    P.dma("sp", "stv", stv[:], stv_d, reads=[B(nm["stv"])], writes=[B("stv1")])
    for q in range(4):
        P.dma("sp", "stS", C.stf[:, q * 520:(q + 1) * 520], stC_d[:, q * 520:(q + 1) * 520], reads=[B(nm["stC"])], writes=[B("stf")])
    TS(C, stv[:], stv[:], C.msk[:, 0:1], None, ALU.mult, ALU.bypass, [B("stv1"), B("msk")], [B("stv1")])
    TS(C, C.stf[:], C.stf[:], C.msk[:, 0:1], None, ALU.mult, ALU.bypass, [B("stf"), B("msk")], [B("stf")])
    P.dma("sp", "g", C.gain[:], g_o.partition_broadcast(128), writes=[B("gain")])
    for j in range(NT):
        hb = j % 4
        P.dma("sp", f"h{hb}", htile[hb], hin[j * 128:(j + 1) * 128, :], reads=[B(nm["hin"])], writes=[B(f"htile{hb}")])
        rms_to_T(C, htile[hb], B(f"htile{hb}"), j, B("gain"), C.big, C.bigb, hb)
    if C.stop <= 1:
        return
    mix_bufs = [B(n) for n in ("qT1", "kT1", "VX", "GT", "SGO", "X1", "XC1")]
    alias_bufs(mix_bufs, [B(f"htile{i}") for i in range(4)] + [B(f"hbf{i}") for i in range(4)])
    P.emit("pool", lambda e: e.memset(VX[:, :, 256:260], 0.0), writes=[B("VX")])
    P.emit("pool", lambda e: e.memset(VX[:, :, 256:257], 1.0), writes=[B("VX")])
    maskA = C.consts[:, 128:256]
    H0, H1 = C.consts[:, 256:384], C.consts[:, 384:512]
    SAME = C.consts[:, 640:768]
    LN_S = float(np.log(np.sqrt(128.0)))
    for j in range(NT):
        tk = slice(j * 128, (j + 1) * 128)
        pg = C.PC[:, 0:16]
        for kc in range(KC):
            MM(C, pg, C.big[:, kc, tk], wg[:, kc, :], kc == 0, kc == KC - 1, [B("wg"), C.bigb[kc]], [B("PCh0")])
        li, nlf, a1, t1 = g8[:, 0, :], g8[:, 1, :], g8[:, 2, :], g8[:, 3, :]
        TT(C, t1, pg[:, 8:16], gbb[:, 8:16], ALU.add, [B("PCh0"), B("gbb")], [B("g8t")])
        TT(C, li, pg[:, 0:8], gbb[:, 0:8], ALU.add, [B("PCh0"), B("gbb")], [B("g8l")])
        ACT(C, t1, t1, AF.Exp, [B("g8t")], [B("g8t")], scale=-1.0)
        ACT(C, nlf, t1, AF.Ln, [B("g8t")], [B("g8n")], bias=1.0)
        pq = C.PC[:, 512:544]
        MM(C, pq[:, 0:8], maskA, nlf, True, True, [B("consts"), B("g8n")], [B("PCh1")])
        MM(C, pq[:, 8:16], SAME, nlf, True, True, [B("consts"), B("g8n")], [B("PCh1")])
        MM(C, pq[:, 16:24], H0, nlf, True, True, [B("consts"), B("g8n")], [B("PCh1")])
        MM(C, pq[:, 24:32], H1, nlf, True, True, [B("consts"), B("g8n")], [B("PCh1")])
        TT(C, a1, li, pq[:, 0:8], ALU.add, [B("g8l"), B("PCh1")], [B("g8a")])
        ACT(C, EK[:, j, :], a1, AF.Exp, [B("g8a")], [B("EK")])
        TT(C, a1, a1, pq[:, 8:16], ALU.subtract, [B("g8a"), B("PCh1")], [B("g8a")])
        ACT(C, EKC[:, j, :], a1, AF.Exp, [B("g8a")], [B("EKC")])
        ACT(C, THR[:, j, :], pq[:, 0:8], AF.Exp, [B("PCh1")], [B("THR")], bias=LN_S)
        ACT(C, DEC[:, j, :], pq[:, 16:32], AF.Exp, [B("PCh1")], [B("DEC")], scale=-1.0)
    if C.stop <= 2:
        return
    alias_bufs([B("PAh0"), B("PAh1"), B("PBh0"), B("PBh1")], [B("PA"), B("PB")])
    slot = 0
    for g in range(4):
        def cons(j, ps, pb, g=g):
            ACT(C, VX[:, j * 8 + 2 * g:j * 8 + 2 * g + 2, 0:256], ps.rearrange("p (a b) -> p a b", b=256), AF.Copy, [pb], [B("VX")])
        tm_group(C, w1, slot, 4, lambda kc, j: C.big[:, kc, j * 128:(j + 1) * 128], lambda kc, j: [C.bigb[kc]], cons)
        slot += 4
    if C.stop <= 3:
        return
    for g in range(4):
        if states_only:
            slot = 48
            break

        def cons_o(j, ps, pb):
            ACT(C, SGO[:, j, :], ps, AF.Sigmoid, [pb], [B("SGO")])
        tm_group(C, w1, slot, 4, lambda kc, j: C.big[:, kc, j * 128:(j + 1) * 128], lambda kc, j: [C.bigb[kc]], cons_o)
        slot += 4

        def cons_z(j, ps, pb, g=g):
            ACT(C, ps, ps, AF.Silu, [pb], [pb])
            TT(C, GT[:, j, g * 512:(g + 1) * 512], ps, SGO[:, j, :], ALU.mult, [pb, B("SGO")], [B("GT")])
        tm_group(C, w1, slot, 4, lambda kc, j: C.big[:, kc, j * 128:(j + 1) * 128], lambda kc, j: [C.bigb[kc]], cons_z)
        slot += 4
    if C.stop <= 4:
        return
    alias_bufs([B("X1"), B("XC1")], [B("SGO")])
    alias_bufs([B("PA"), B("PB")], [B("PAh0"), B("PAh1"), B("PBh0"), B("PBh1")])
    for h in range(8):
        for qk in range(2):
            i = qk * 8 + h
            PS, psb = [(C.PA, B("PA")), (C.PB, B("PB"))][qk]
            inproj_fm(C, w1, slot, PS, psb); slot += 1
            ACT(C, X[:, 3:1027], PS[:], AF.Copy, [psb], [B("X1")])
            CP(C, X[:, 0:3], stv[:, 3 * i:3 * i + 3], [B("stv1")], [B("X1")])
            CP(C, stvo[:, 3 * i:3 * i + 3], X[:, 1024:1027], [B("X1")], [B("stvo1")])
            cw = lambda k: sm[:, 4 * i + k:4 * i + k + 1]
            TS(C, XC[:, 0:1024], X[:, 3:1027], cw(3), sm[:, 64 + i:65 + i], ALU.mult, ALU.add, [B("X1"), B("sm1")], [B("XC1")])
            for k in (2, 1, 0):
                STT(C, XC[:, 0:1024], X[:, k:k + 1024], cw(k), XC[:, 0:1024], ALU.mult, ALU.add, [B("X1"), B("XC1"), B("sm1")], [B("XC1")])
            dst, dbuf = (qT, B("qT1")) if qk == 0 else (kT, B("kT1"))
            ACT(C, dst[:, h, :], XC[:, 0:1024], AF.Silu, [B("XC1")], [dbuf])
    alias_bufs([B("PAh0"), B("PAh1"), B("PBh0"), B("PBh1"), B("PCh0"), B("PCh1")], [B("PA"), B("PB"), B("PC")])
    if C.stop <= 5:
        return
    for h in range(8):
        ACT(C, Cb[:, h, 0:258], Cf[:, h, 0:258], AF.Copy, [B("stf")], [B(f"Cb{h}")])
        if not states_only:
            ACT(C, Ch[:, h, 0:258], Cf[:, h, 0:258], AF.Copy, [B("stf"), B("DEC")], [B(f"Ch{h}")], scale=DEC[:, 0, h:h + 1])
    def FM(j, h, r):
        tk = slice(j * 128, (j + 1) * 128)
        p_sc = C.PC[:, r * 512 + 384:r * 512 + 512]
        b_sc = B(f"PCh{r}")
        if r == 1:
            p_kv = [C.PA[:, 0:258], C.PA[:, 512:770]]
            b_kv = [B("PAh0"), B("PAh1")]
        else:
            p_kv = [C.PB[:, 0:258], C.PB[:, 512:770]]
            b_kv = [B("PBh0"), B("PBh1")]
        p_nd = C.PC[:, r * 512:r * 512 + 258]
        b_nd = B(f"PCh{r}")
        jh = j * 8 + h
        ptk = C.PT[:, r * 1024:r * 1024 + 128]
        TR(C, ptk, kT[:, h, tk], [B("kT1")], [B(f"PT{r}")])
        if not states_only:
            MM(C, p_sc, kT[:, h, tk], qT[:, h, tk], True, True, [B("kT1"), B("qT1")], [b_sc])
        ACT(C, ktok[r][:], ptk, AF.Copy, [B(f"PT{r}"), B("EKC")], [B(f"ktk{r}")], scale=EKC[:, j, h:h + 1])
        if not states_only:
            STT(C, scw[r][:], p_sc, EK[:, j, h:h + 1], maskA, ALU.mult, ALU.mult, [b_sc, B("EK"), B("consts")], [B(f"scw{r}")])
            ACT(C, scw[r][0:64, 64:128], p_sc[0:64, 64:128], AF.Copy, [b_sc, B("EKC")], [B(f"scw{r}")], scale=EKC[0:64, j, h:h + 1])
        for c in range(2):
            rows = slice(64 * c, 64 * c + 64)
            MM(C, p_kv[c], ktok[r][rows, :], VX[rows, jh, 0:258], True, True, [B(f"ktk{r}"), B("VX")], [b_kv[c]])
        if not states_only:
            MM(C, p_nd[0:64, :], qT[:, h, j * 128:j * 128 + 64], Cb[:, h, 0:258], True, False, [B("qT1"), B(f"Cb{h}")], [b_nd])
            MM(C, p_nd[64:128, :], qT[:, h, j * 128 + 64:(j + 1) * 128], Ch[:, h, 0:258], True, False, [B("qT1"), B(f"Ch{h}")], [b_nd])
            MM(C, p_nd, scw[r][:], VX[:, jh, 0:258], False, True, [B(f"scw{r}"), B("VX")], [b_nd])
            ACT(C, ndsb[r][:], p_nd, AF.Copy, [b_nd], [B(f"ndsb{r}")])
        for c in range(2):
            STT(C, Cf[:, h, 0:258], Cf[:, h, 0:258], DEC[:, j, 8 * c + h:8 * c + h + 1], p_kv[c], ALU.mult, ALU.add,
                [B(f"Cf{h}"), B("stf"), B("DEC"), b_kv[c]], [B(f"Cf{h}")])
        if not states_only:
            ACT(C, Cb[:, h, 0:258], Cf[:, h, 0:258], AF.Copy, [B(f"Cf{h}")], [B(f"Cb{h}")])
            if j + 1 < NT:
                ACT(C, Ch[:, h, 0:258], Cf[:, h, 0:258], AF.Copy, [B(f"Cf{h}"), B("DEC")], [B(f"Ch{h}")], scale=DEC[:, j + 1, h:h + 1])

    def KK_(j, h, r):
        nd = ndsb[r]
        b_nd = B(f"ndsb{r}")
        s_ = r8[:, r, :]
        sbn = B(f"r8_{r}")
        ACT(C, s_[:, 0:1], nd[:, 256:257], AF.Abs, [b_nd], [sbn])
        TT(C, s_[:, 0:1], s_[:, 0:1], THR[:, j, h:h + 1], ALU.max, [sbn, B("THR")], [sbn])
        P.emit("dve", lambda e, s_=s_: e.reciprocal(out=s_[:, 1:2], in_=s_[:, 0:1]), reads=[sbn], writes=[sbn])
        ACT(C, jk[:], nd[:, 0:256], AF.Square, [b_nd, sbn], [B("jk"), sbn], scale=s_[:, 1:2], accum_out=s_[:, 2:3])
        ACT(C, s_[:, 3:4], s_[:, 2:3], AF.Sqrt, [sbn], [sbn], scale=1.0 / 256.0, bias=EPS)
        P.emit("dve", lambda e, s_=s_: e.reciprocal(out=s_[:, 4:5], in_=s_[:, 3:4]), reads=[sbn], writes=[sbn])
        TT(C, s_[:, 5:6], s_[:, 4:5], s_[:, 1:2], ALU.mult, [sbn], [sbn])
        STT(C, ycb[:, h * 256:(h + 1) * 256], nd[:, 0:256], s_[:, 5:6], GT[:, j, h * 256:(h + 1) * 256], ALU.mult, ALU.mult,
            [b_nd, sbn, B("GT")], [B("ycb")])

    def YT(j):
        tk = slice(j * 128, (j + 1) * 128)
        for half in range(2):
            ptv = C.PT[:, half * 1024:(half + 1) * 1024]
            pb = B(f"PT{half}")
            for k in range(8):
                kc = half * 8 + k
                TR(C, ptv[:, k * 128:(k + 1) * 128], ycb[:, kc * 128:(kc + 1) * 128], [B("ycb")], [pb])
            for k in range(8):
                kc = half * 8 + k
                ACT(C, C.big[:, kc, tk], ptv[:, k * 128:(k + 1) * 128], AF.Copy, [pb, B("sm1")], [C.bigb[kc]], scale=sm[:, 80 + kc:81 + kc])

    its = [(j, h) for j in range(NT) for h in range(8)]
    if states_only:
        for i, (j, h) in enumerate(its):
            FM(j, h, i % 2)
    else:
        FM(its[0][0], its[0][1], 0)
        for i in range(len(its)):
            if i + 1 < len(its):
                FM(its[i + 1][0], its[i + 1][1], (i + 1) % 2)
            KK_(its[i][0], its[i][1], i % 2)
            if its[i][1] == 7:
                YT(its[i][0])
    if stC_o is not None:
        for q in range(4):
            P.dma("sp", "stC_o", stC_o[:, q * 520:(q + 1) * 520], C.stf[:, q * 520:(q + 1) * 520], reads=[B(f"Cf{h}") for h in range(8)],
                  writes=[B(nm["stC_o"])])
        P.dma("sp", "stv1_o", stv_o, stvo[:], reads=[B("stvo1")], writes=[B(nm["stv_o"])])
    if states_only:
        return
    stage4(C, w1, slot, hin, p1, rpl, g_ple, hout, mix_bufs + [B("ycb"), B("jk"), B("ktk0"), B("ktk1"), B("scw0"), B("scw1"), B("ndsb0"), B("ndsb1")],
           HP, tmps, tb, g_fin, nm)


def _run_layer_unfused(nc, shared, per_core, st_names, out_name):
    zeros = {k: np.zeros(shape, np.float32) for k, (shape, _) in st_names.items()}

    def maps(states):
        ms = []
        for c in range(8):
            m = dict(shared)
            m.update(per_core[c])
            for k in st_names:
                m[k] = states[c][k]
            ms.append(m)
        return ms
    r1 = run_bass_kernel_spmd(nc, maps([zeros] * 8), core_ids=list(range(8)))
    st = []
    for c in range(8):
        if c % 2 == 1:
            st.append({k: np.asarray(r1.results[c - 1][o], np.float32) for k, (_, o) in st_names.items()})
        else:
            st.append(zeros)
    r2 = run_bass_kernel_spmd(nc, maps(st), core_ids=list(range(8)))
    return [np.asarray(r2.results[c][out_name]) for c in range(8)]


def kernel_unfused(**inp):
    inp = {k: np.asarray(v) for k, v in inp.items()}
    x, p = inp["x"], inp["p"]
    s0 = prep_l0(inp)
    s0["msk"] = np.ones((128, 1), np.float32)
    nc0 = build_program([0])
    pc = [{"hin": np.ascontiguousarray(x[c // 2, (c % 2) * T:(c % 2 + 1) * T], dtype=np.float32),
           "p0": np.ascontiguousarray(p[0, c // 2, (c % 2) * T:(c % 2 + 1) * T], dtype=np.float32)} for c in range(8)]
    h2 = _run_layer_unfused(nc0, s0, pc, {"stS": ((128, 1024), "stS_o"), "stv": ((128, 32), "stv_o")}, "hout")
    del s0
    s1 = prep_l1(inp)
    s1["msk"] = np.ones((128, 1), np.float32)
    nc1 = build_program([1])
    pc = [{"hin1": np.ascontiguousarray(h2[c], dtype=np.float32),
           "p1": np.ascontiguousarray(p[1, c // 2, (c % 2) * T:(c % 2 + 1) * T], dtype=np.float32)} for c in range(8)]
    out = _run_layer_unfused(nc1, s1, pc, {"stC": ((128, 8 * 260), "stC_o"), "stv1": ((128, 48), "stv1_o")}, "hout1")
    return np.stack(out).reshape(4, 2 * T, D).astype(np.float32)


def make_in_maps(inp):
    inp = {k: np.asarray(v) for k, v in inp.items()}
    x, p = np.asarray(inp["x"], np.float32), np.asarray(inp["p"], np.float32)
    shared = {}
    shared.update(prep_l0(inp))
    shared.update(prep_l1(inp))
    shared["zS"] = np.zeros((128, 2080), np.float32)
    shared["zv"] = np.zeros((128, 48), np.float32)
    zx = np.zeros((T, D), np.float32)
    zp = np.zeros((T, 256), np.float32)
    maps = []
    for c in range(8):
        b, hf = c // 2, c % 2
        m = dict(shared)
        m["xB"] = np.ascontiguousarray(x[b, hf * T:(hf + 1) * T])
        m["p0B"] = np.ascontiguousarray(p[0, b, hf * T:(hf + 1) * T])
        m["p1B"] = np.ascontiguousarray(p[1, b, hf * T:(hf + 1) * T])
        if hf == 1:
            m["xA"] = np.ascontiguousarray(x[b, 0:T])
            m["p0A"] = np.ascontiguousarray(p[0, b, 0:T])
            m["p1A"] = np.ascontiguousarray(p[1, b, 0:T])
        else:
            m["xA"], m["p0A"], m["p1A"] = zx, zp, zp
        m["msk"] = np.full((128, 1), float(hf), np.float32)
        maps.append(m)
    return maps


def kernel(**inp):
    maps = make_in_maps(inp)
    nc = build_program("fused")
    res = run_bass_kernel_spmd(nc, maps, core_ids=list(range(8)))
    out = [np.asarray(res.results[c]["out"], np.float32) for c in range(8)]
    return np.stack(out).reshape(4, 2 * T, D)
```

```python
import numpy as np
import ml_dtypes
import concourse.bass as bass
import concourse.mybir as mybir
from concourse.bass_utils import run_bass_kernel_spmd

F32 = mybir.dt.float32
BF16 = mybir.dt.bfloat16
AF = mybir.ActivationFunctionType
ALU = mybir.AluOpType
AX = mybir.AxisListType


class Buf:
    __slots__ = ("name", "last_w", "readers")

    def __init__(self, name):
        self.name = name
        self.last_w = None
        self.readers = []


class Op:
    __slots__ = ("eng", "idx", "thunk", "waits", "dma_waits", "signal", "sigval",
                 "is_dma", "dsem", "dval")

    def __init__(self, eng, thunk):
        self.eng = eng
        self.thunk = thunk
        self.waits = {}
        self.dma_waits = {}
        self.signal = False
        self.sigval = 0
        self.is_dma = False
        self.dsem = None
        self.dval = 0


ENGS = ("pe", "act", "dve", "pool", "sp")


class Prog:
    def __init__(self, nc):
        self.nc = nc
        self.ops = {e: [] for e in ENGS}
        self.waited = {e: {} for e in ENGS}
        self.dma_sem_val = {}
        self.dma_last = {}
        self.dma_keys = []

    def eng_obj(self, e):
        nc = self.nc
        return {"pe": nc.tensor, "act": nc.scalar, "dve": nc.vector,
                "pool": nc.gpsimd, "sp": nc.sync}[e]

    def _deps(self, op, reads, writes, acc_ok=()):
        deps = []
        for b in reads:
            if b.last_w is not None:
                deps.append(b.last_w)
        for b in writes:
            if b.last_w is not None:
                if not (b in acc_ok and b.last_w.eng == op.eng):
                    deps.append(b.last_w)
            for r in b.readers:
                deps.append(r)
        e = op.eng
        for d in deps:
            if d is op:
                continue
            if d.is_dma:
                cur = self.waited[e].get(("dma", d.dsem), 0)
                if d.dval > cur:
                    op.dma_waits[d.dsem] = max(op.dma_waits.get(d.dsem, 0), d.dval)
                    self.waited[e][("dma", d.dsem)] = d.dval
            else:
                if d.eng == e and e == "pe":
                    continue
                cur = self.waited[e].get(d.eng, -1)
                if d.idx > cur:
                    prev = op.waits.get(d.eng)
                    if prev is None or d.idx > prev.idx:
                        op.waits[d.eng] = d
        for k, d in op.waits.items():
            d.signal = True
            self.waited[e][k] = max(self.waited[e].get(k, -1), d.idx)
        for b in reads:
            b.readers.append(op)
        for b in writes:
            b.last_w = op
            b.readers = []

    def emit(self, eng, thunk, reads=(), writes=(), acc_ok=()):
        op = Op(eng, thunk)
        op.idx = len(self.ops[eng])
        self._deps(op, reads, writes, acc_ok)
        self.ops[eng].append(op)
        return op

    def dma(self, eng, key, out, in_, reads=(), writes=(), fn=None, **kw):
        if key not in self.dma_sem_val:
            self.dma_sem_val[key] = 0
            self.dma_keys.append(key)
        op = Op(eng, None)
        op.idx = len(self.ops[eng])
        op.is_dma = True
        op.dsem = key
        prev = self.dma_last.get(key)
        self._deps(op, reads, writes)
        if prev is not None:
            cur = self.waited[eng].get(("dma", key), 0)
            if prev.dval > cur:
                op.dma_waits[key] = max(op.dma_waits.get(key, 0), prev.dval)
                self.waited[eng][("dma", key)] = prev.dval
        self.dma_sem_val[key] += 16
        op.dval = self.dma_sem_val[key]
        self.dma_last[key] = op
        op.thunk = (out, in_, kw, fn)
        self.ops[eng].append(op)
        return op

    def finalize(self, sems):
        for e in ENGS:
            c = 0
            for op in self.ops[e]:
                if op.signal:
                    c += 1
                    op.sigval = c

        def run_engine(e, eng):
            for op in self.ops[e]:
                for k, d in op.waits.items():
                    eng.wait_ge(sems[k], d.sigval)
                for k, v in op.dma_waits.items():
                    eng.wait_ge(sems[("dma", k)], v)
                if op.is_dma:
                    out, in_, kw, fn = op.thunk
                    ins = fn(eng) if fn is not None else eng.dma_start(out=out, in_=in_, **kw)
                    ins.then_inc(sems[("dma", op.dsem)], 16)
                    if op.signal:
                        raise RuntimeError("dma op cannot signal engine sem")
                else:
                    ins = op.thunk(eng)
                    if op.signal:
                        ins.then_inc(sems[e], 1)
        return run_engine


def run_prog(nc, prog, final_waits=()):
    from contextlib import ExitStack
    with ExitStack() as st:
        sems = {}
        for e in ENGS:
            sems[e] = st.enter_context(nc.semaphore("s_" + e))
        for k in prog.dma_keys:
            sems[("dma", k)] = st.enter_context(nc.semaphore("d_" + str(k)))
        block = st.enter_context(nc.Block())
        runner = prog.finalize(sems)

        @block.tensor
        def _(eng):
            runner("pe", eng)

        @block.scalar
        def _(eng):
            runner("act", eng)

        @block.vector
        def _(eng):
            runner("dve", eng)

        @block.gpsimd
        def _(eng):
            runner("pool", eng)

        @block.sync
        def _(eng):
            runner("sp", eng)
            for k in final_waits:
                eng.wait_ge(sems[("dma", k)], prog.dma_sem_val[k])


D = 2048
T = 1024
NT = 8
KC = 16
EPS = 1e-6
NW = 6
ARENA = 57600
L0_NSLOT = 40 + 8 + 40


def _fm_slot(W, c0):
    blk = W[:, c0:c0 + 128].reshape(KC, 128, 128)
    return np.ascontiguousarray(blk.transpose(1, 0, 2)).reshape(128, 2048)


def _tm_slots(W, c0):
    out = []
    K = W.shape[0] // 128
    blk = W[:, c0:c0 + 512].reshape(K, 128, 512)
    for kcg in range(K // 4):
        out.append(np.ascontiguousarray(blk[kcg * 4:(kcg + 1) * 4].transpose(1, 0, 2)).reshape(128, 2048))
    return out


def _pl_slot(ple_w, g):
    out = np.zeros((128, 2048), np.float32)
    blk = ple_w[:, g * 512:(g + 1) * 512].reshape(2, 128, 512)
    out[:, 0:1024] = blk.transpose(1, 0, 2).reshape(128, 1024)
    return out


def pack_tail(w_out, gate_w, ple_w):
    slots = []
    for g in range(4):
        slots += _tm_slots(w_out, 512 * g)
    for g in range(4):
        slots.append(_pl_slot(ple_w, g))
    for g in range(4):
        slots += _tm_slots(gate_w, 512 * g)
        slots.append(_pl_slot(ple_w, g))
    return slots


def pack_l0(e_w_in, e_w_out, gate_w, ple_w):
    slots = []
    for n in range(8):
        slots.append(_fm_slot(e_w_in, 4096 + 128 * n))
        slots.append(_fm_slot(e_w_in, 5120 + 128 * n))
    for h in range(8):
        slots.append(_fm_slot(e_w_in, 1024 + 128 * h))
        slots.append(_fm_slot(e_w_in, 128 * h))
        slots.append(_fm_slot(e_w_in, 3072 + 128 * h))
    for g in range(2):
        slots += _tm_slots(e_w_in, 2048 + 512 * g)
    slots += pack_tail(e_w_out, gate_w, ple_w)
    return np.stack(slots).astype(np.float32)


def make_consts():
    c = np.zeros((128, 7, 128), np.float32)
    idx = np.arange(128)
    c[:, 0] = np.eye(128)
    same = (idx[:, None] // 64) == (idx[None, :] // 64)
    c[:, 1] = (same & (idx[:, None] <= idx[None, :])).astype(np.float32)
    c[:, 2] = (idx[:, None] < 64).astype(np.float32) * np.ones((1, 128), np.float32)
    c[:, 3] = (idx[:, None] >= 64).astype(np.float32) * np.ones((1, 128), np.float32)
    c[:, 4] = 1.0 / 128.0
    c[:, 5] = same.astype(np.float32)
    c[:, 6] = (idx[:, None] <= idx[None, :]).astype(np.float32)
    return c.reshape(128, 896)


class Ctx:
    pass


def fence(C):
    P = C.P
    ops = [P.ops[e][-1] for e in ENGS if P.ops[e]] + list(P.dma_last.values())
    C.fence_ops = ops
    for b in C.bufs.values():
        b.readers = b.readers + ops


def build_program(layers, fused=False):
    nc = bass.Bass("TRN2", target_bir_lowering=False)
    from contextlib import ExitStack
    st = ExitStack()
    with st:
        P = Prog(nc)
        C = Ctx()
        C.nc, C.P = nc, P
        C.bufs = {}
        C.fence_ops = []
        C.sbs = {}
        C.drams = {}

        def dram(name, shape, dt=F32, kind="ExternalInput"):
            if name not in C.drams:
                if kind == "Internal":
                    C.drams[name] = nc.dram_tensor(name, list(shape), dt, kind=kind, addr_space="Local").ap()
                else:
                    C.drams[name] = nc.dram_tensor(name, list(shape), dt, kind=kind).ap()
            return C.drams[name]

        def sb(name, shape, dt=F32):
            if name not in C.sbs:
                C.sbs[name] = st.enter_context(nc.sbuf_tensor(name, list(shape), dt))
            return C.sbs[name]

        def B(name):
            if name not in C.bufs:
                b = Buf(name)
                b.readers = list(C.fence_ops)
                C.bufs[name] = b
            return C.bufs[name]

        C.dram, C.sb, C.B = dram, sb, B
        C.PA = st.enter_context(nc.psum_tensor("PA", [128, 1024], F32))
        C.PB = st.enter_context(nc.psum_tensor("PB", [128, 1024], F32))
        C.PC = st.enter_context(nc.psum_tensor("PC", [128, 1024], F32))
        C.PT = st.enter_context(nc.psum_tensor("PT", [128, 2048], BF16))
        C.wring = [sb(f"wr{i}", [128, 2048], BF16) for i in range(NW)]
        C.wslot_n = 0
        C.consts = sb("consts", [128, 896], F32)
        C.ident_bf = sb("ident_bf", [128, 128], BF16)
        C.gain = sb("gain", [128, 2048], F32)
        C.big = sb("big", [128, KC, T], BF16)
        C.bigb = [B(f"big{kc}") for kc in range(KC)]
        C.arena = sb("arena", [128, ARENA], BF16)
        C.small = sb("small", [128, 64], F32)
        C.stf = sb("stf", [128, 8 * 260], F32)
        C.stb = sb("stb", [128, 8 * 260], BF16)
        C.msk = sb("msk_sb", [128, 1], F32)
        consts_d = dram("consts_d", [128, 896])
        P.dma("sp", "c", C.consts[:], consts_d, writes=[B("consts")])
        P.dma("sp", "c", C.msk[:], dram("msk", [128, 1]), writes=[B("msk")])
        P.emit("dve", lambda e: e.tensor_copy(out=C.ident_bf[:], in_=C.consts[:, 0:128]),
               reads=[B("consts")], writes=[B("ident_bf")])
        C.out_keys = []
        import os
        C.stop = int(os.environ.get('STOP', '99'))
        C.rstop = int(os.environ.get('RSTOP', '99'))
        if layers == "fused":
            layer0(C, "A")
            fence(C)
            layer1(C, "A")
            fence(C)
            layer0(C, "B")
            fence(C)
            layer1(C, "B")
        else:
            for li in layers:
                if li == 0:
                    layer0(C, "U")
                else:
                    layer1(C, "U")
        run_prog(nc, P, final_waits=[k for k in C.out_keys if k in P.dma_sem_val])
    return nc


def ACT(C, out, in_, func, R, W, **kw):
    return C.P.emit("act", lambda e: e.activation(out=out, in_=in_, func=func, **kw), reads=R, writes=W)


def TS(C, out, in0, s1, s2, op0, op1, R, W, eng="dve"):
    return C.P.emit(eng, lambda e: e.tensor_scalar(out=out, in0=in0, scalar1=s1, scalar2=s2, op0=op0, op1=op1),
                    reads=R, writes=W)


def TT(C, out, in0, in1, op, R, W, eng="dve"):
    return C.P.emit(eng, lambda e: e.tensor_tensor(out=out, in0=in0, in1=in1, op=op), reads=R, writes=W)


def STT(C, out, in0, scalar, in1, op0, op1, R, W):
    return C.P.emit("dve", lambda e: e.scalar_tensor_tensor(out=out, in0=in0, scalar=scalar, in1=in1, op0=op0, op1=op1),
                    reads=R, writes=W)


def CP(C, out, in_, R, W, eng="dve"):
    return C.P.emit(eng, lambda e: e.tensor_copy(out=out, in_=in_), reads=R, writes=W)


def MM(C, out, lhsT, rhs, start, stop, R, W):
    return C.P.emit("pe", lambda e: e.matmul(out, lhsT=lhsT, rhs=rhs, start=start, stop=stop),
                    reads=R, writes=W, acc_ok=W)


def TR(C, out, in_, R, W):
    return C.P.emit("pe", lambda e: e.transpose(out, in_, C.ident_bf[:]), reads=R + [C.B("ident_bf")], writes=W, acc_ok=W)


def load_w(C, wd, slot):
    i = C.wslot_n % NW
    C.wslot_n += 1
    b = C.B(f"wr{i}")
    C.P.dma("pool", f"w{i}", C.wring[i][:], wd[slot], writes=[b])
    return C.wring[i], b


def rms_to_T(C, src_tile, src_buf, j, gain_ready_buf, dstT, dst_bufs, hb):
    B = C.B
    sm = C.small[:, 16 + 4 * hb:20 + 4 * hb]
    junk = C.hbf[hb]
    ACT(C, junk[:], src_tile, AF.Square, [src_buf], [B(f"hbf{hb}"), B(f"sm_ssq{hb}")], accum_out=sm[:, 0:1])
    ACT(C, sm[:, 1:2], sm[:, 0:1], AF.Sqrt, [B(f"sm_ssq{hb}")], [B(f"sm_sd{hb}")], scale=1.0 / D, bias=EPS)
    C.P.emit("dve", lambda e: e.reciprocal(out=sm[:, 2:3], in_=sm[:, 1:2]), reads=[B(f"sm_sd{hb}")], writes=[B(f"sm_rstd{hb}")])
    STT(C, junk[:], src_tile, sm[:, 2:3], C.gain[:], ALU.mult, ALU.mult,
        [src_buf, B(f"sm_rstd{hb}"), gain_ready_buf], [B(f"hbf{hb}")])
    for half in range(2):
        ptv = C.PT[:, half * 1024:(half + 1) * 1024]
        pb = B(f"PT{half}")
        for k in range(8):
            kc = half * 8 + k
            TR(C, ptv[:, k * 128:(k + 1) * 128], junk[:, kc * 128:(kc + 1) * 128], [B(f"hbf{hb}")], [pb])
        eng = "act" if half == 0 else "dve"
        dst = dstT[:, half * 8:(half + 1) * 8, j * 128:(j + 1) * 128]
        srcv = ptv.rearrange("p (a b) -> p a b", b=128)
        if eng == "act":
            ACT(C, dst, srcv, AF.Copy, [pb], dst_bufs)
        else:
            CP(C, dst, srcv, [pb], dst_bufs)


def alias_bufs(new_bufs, old_bufs):
    ops = []
    for ob in old_bufs:
        ops += ob.readers
        if ob.last_w is not None:
            ops.append(ob.last_w)
    for nb in new_bufs:
        nb.readers = nb.readers + ops


def inproj_fm(C, wd, slot, PS, psb):
    wt, wb = load_w(C, wd, slot)
    for half in range(2):
        for kc in range(KC):
            MM(C, PS[:, half * 512:(half + 1) * 512], wt[:, kc * 128:(kc + 1) * 128],
               C.big[:, kc, half * 512:(half + 1) * 512], kc == 0, kc == KC - 1,
               [wb, C.bigb[kc]], [psb])


def tm_group(C, wd, slot0, K4, lhs_fn, lhs_bufs_fn, consume):
    wts = [load_w(C, wd, slot0 + i) for i in range(K4)]
    for j in range(NT):
        PS, nm = [(C.PA, "PA"), (C.PB, "PB")][(j // 2) % 2]
        ps = PS[:, (j % 2) * 512:(j % 2) * 512 + 512]
        pb = C.B(f"{nm}h{j % 2}")
        nk = K4 * 4
        for kc in range(nk):
            wt, wb = wts[kc // 4]
            MM(C, ps, lhs_fn(kc, j), wt[:, (kc % 4) * 512:(kc % 4) * 512 + 512], kc == 0, kc == nk - 1,
               [wb] + lhs_bufs_fn(kc, j), [pb])
        consume(j, ps, pb)


def layer0(C, seg="U"):
    P, B, nc = C.P, C.B, C.nc
    dram, sb = C.dram, C.sb
    w0 = dram("w0", [L0_NSLOT, 128, 2048])
    wri_d = dram("wri", [128, 16, 128])
    sm_d = dram("sm0", [128, 96])
    g_e = dram("g_e", [D])
    g_ple = dram("g_ple0", [D])
    if seg == "U":
        hin, p0 = dram("hin", [T, D]), dram("p0", [T, 256])
        stS_d, stv_d = dram("stS", [128, 1024]), dram("stv", [128, 32])
        hout = dram("hout", [T, D], kind="ExternalOutput")
        stS_o = dram("stS_o", [128, 1024], kind="ExternalOutput")
        stv_o = dram("stv_o", [128, 32], kind="ExternalOutput")
        nm = {"hin": "d_hin", "hout": "d_hout", "stS": "d_stS", "stv": "d_stv", "stS_o": "d_stS_o", "stv_o": "d_stv_o"}
        C.out_keys += ["hout0", "hout1", "stS_o", "stv_o"]
    elif seg == "A":
        hin, p0 = dram("xA", [T, D]), dram("p0A", [T, 256])
        stS_d, stv_d = dram("zS", [128, 2080])[:, 0:1024], dram("zv", [128, 48])[:, 0:32]
        hout = dram("h2A", [T, D], kind="Internal")
        stS_o = dram("sS0", [128, 1024], kind="Internal")
        stv_o = dram("sv0", [128, 32], kind="Internal")
        nm = {"hin": "d_xA", "hout": "d_h2A", "stS": "d_zS", "stv": "d_zv", "stS_o": "d_sS0", "stv_o": "d_sv0"}
    else:
        hin, p0 = dram("xB", [T, D]), dram("p0B", [T, 256])
        stS_d, stv_d = dram("sS0", [128, 1024], kind="Internal"), dram("sv0", [128, 32], kind="Internal")
        hout = dram("h2B", [T, D], kind="Internal")
        stS_o = stv_o = None
        nm = {"hin": "d_xB", "hout": "d_h2B", "stS": "d_sS0", "stv": "d_sv0"}

    A = C.arena
    def abf(off, a, b):
        return A[:, off:off + a * b].rearrange("p (a b) -> p a b", b=b)
    G1 = abf(0, 8, 1024)
    G2 = abf(8192, 8, 1024)
    Vt = abf(8192, 8, 1024)
    qT = abf(16384, 8, 1024)
    kT = abf(24576, 8, 1024)
    zaT = abf(32768, 8, 1024)
    TB = 40960
    def tmpf(i, w=1032):
        return A[:, TB + i * 2064:TB + (i + 1) * 2064].bitcast(F32)[:, 0:w]
    tmps = [tmpf(i) for i in range(7)]
    tb = [B(f"tmp{i}") for i in range(7)]
    XCB = A[:, TB + 7 * 2064:TB + 7 * 2064 + 1024]
    ZB = A[:, TB + 7 * 2064 + 1024:TB + 7 * 2064 + 2048]
    htile = [A[:, 4096 * i:4096 * (i + 1)].bitcast(F32) for i in range(4)]
    hbf = [A[:, 16384 + 2048 * i:16384 + 2048 * (i + 1)] for i in range(4)]
    C.htile, C.hbf = htile, hbf
    HP = A[:, 0:32768].bitcast(F32).rearrange("p (a b) -> p a b", b=2048)

    wri = sb("wri_sb", [128, 16, 128], BF16)
    sm = sb("sm0_sb", [128, 96], F32)
    smx = sb("smx", [128, 64], F32)
    plast = sb("plast", [128, 8, 17], F32)
    PP = sb("PP", [128, 8, 8], F32)
    khat = [sb(f"khat{i}", [128, 64], BF16) for i in range(2)]
    stv = sb("stv_sb", [128, 32], F32)
    stvo = sb("stvo_sb", [128, 32], F32)
    ktok = [sb(f"ktok{i}", [128, 128], BF16) for i in range(2)]
    attm = [sb(f"attm{i}", [128, 128], BF16) for i in range(2)]
    sqf = [A[:, TB + 256 * i:TB + 256 * (i + 1)].bitcast(F32) for i in range(2)]
    oc = [A[:, TB + 512 + 256 * i:TB + 512 + 256 * (i + 1)].bitcast(F32) for i in range(2)]
    sqb = [A[:, TB + 1024 + 128 * i:TB + 1024 + 128 * (i + 1)] for i in range(2)]
    ones_bf = sb("ones_bf", [128, 128], BF16)
    rpl = sb("rpl", [128, 16], F32)

    P.dma("pool", "wri", wri[:], wri_d, writes=[B("wri")])
    P.dma("sp", "sm", sm[:], sm_d, writes=[B("sm")])
    P.dma("sp", "stv", stv[:], stv_d, reads=[B(nm["stv"])], writes=[B("stv")])
    P.dma("sp", "stS", C.stf[:, 0:1024], stS_d, reads=[B(nm["stS"])], writes=[B("stf")])
    TS(C, stv[:], stv[:], C.msk[:, 0:1], None, ALU.mult, ALU.bypass, [B("stv"), B("msk")], [B("stv")])
    TS(C, C.stf[:, 0:1024], C.stf[:, 0:1024], C.msk[:, 0:1], None, ALU.mult, ALU.bypass, [B("stf"), B("msk")], [B("stf")])
    P.emit("pool", lambda e: e.memset(ZB, 0.0), writes=[B("zb")])
    P.emit("pool", lambda e: e.memset(plast[:], 1.0), writes=[B("plast")])
    TT(C, smx[:, 0:8], sm[:, 0:8], sm[:, 8:16], ALU.subtract, [B("sm")], [B("smx")])
    ACT(C, smx[:, 0:8], smx[:, 0:8], AF.Sigmoid, [B("smx")], [B("smx")])
    TS(C, smx[:, 8:16], smx[:, 0:8], -1.0, 1.0, ALU.mult, ALU.add, [B("smx")], [B("smx")])
    ACT(C, smx[:, 32:40], sm[:, 80:88], AF.Exp, [B("sm")], [B("smx")], scale=-1.0)
    ACT(C, smx[:, 32:40], smx[:, 32:40], AF.Ln, [B("smx")], [B("smx")], bias=1.0)
    TS(C, smx[:, 16:24], smx[:, 32:40], -8.0, None, ALU.mult, ALU.bypass, [B("smx")], [B("smx")])
    TS(C, smx[:, 24:32], smx[:, 32:40], -16.0, None, ALU.mult, ALU.bypass, [B("smx")], [B("smx")])

    P.dma("sp", "g", C.gain[:], g_e.partition_broadcast(128), writes=[B("gain")])
    for j in range(NT):
        hb = j % 4
        P.dma("sp", f"h{hb}", htile[hb], hin[j * 128:(j + 1) * 128, :], reads=[B(nm["hin"])], writes=[B(f"htile{hb}")])
        rms_to_T(C, htile[hb], B(f"htile{hb}"), j, B("gain"), C.big, C.bigb, hb)
    if C.stop <= 1:
        return
    mix_bufs = [B(n) for n in ("G1", "G2", "qT", "kT", "zaT")] + tb + [B("xcb"), B("Vt")]
    alias_bufs(mix_bufs, [B(f"htile{i}") for i in range(4)] + [B(f"hbf{i}") for i in range(4)])

    X, XC, R_, I_, M_, AC, ZS = tmps
    bX, bXC, bR, bI, bM, bAC, bZS = tb
    slot = 0
    inproj_fm(C, w0, slot, C.PA, B("PA")); slot += 1
    for n in range(8):
        ACT(C, X[:, 3:1027], C.PA[:], AF.Copy, [B("PA")], [bX])
        CP(C, X[:, 0:3], stv[:, 8 + 3 * n:11 + 3 * n], [B("stv")], [bX])
        CP(C, stvo[:, 8 + 3 * n:11 + 3 * n], X[:, 1024:1027], [bX], [B("stvo")])
        inproj_fm(C, w0, slot, C.PA, B("PA")); slot += 1
        ACT(C, ZS[:, 0:1024], C.PA[:], AF.Silu, [B("PA")], [bZS])
        if n + 1 < 8:
            inproj_fm(C, w0, slot, C.PA, B("PA")); slot += 1
        cw = lambda k: sm[:, 24 + 4 * n + k:25 + 4 * n + k]
        TS(C, XC[:, 0:1024], X[:, 3:1027], cw(3), sm[:, 56 + n:57 + n], ALU.mult, ALU.add, [bX, B("sm")], [bXC])
        for k in (2, 1, 0):
            STT(C, XC[:, 0:1024], X[:, k:k + 1024], cw(k), XC[:, 0:1024], ALU.mult, ALU.add, [bX, bXC, B("sm")], [bXC])
        ACT(C, XCB, XC[:, 0:1024], AF.Copy, [bXC], [B("xcb")])
        for half in range(2):
            MM(C, C.PB[:, half * 512:(half + 1) * 512], wri[:, n, :], XCB[:, half * 512:(half + 1) * 512], True, True,
               [B("wri"), B("xcb")], [B("PB")])
            MM(C, C.PC[:, half * 512:(half + 1) * 512], wri[:, 8 + n, :], XCB[:, half * 512:(half + 1) * 512], True, True,
               [B("wri"), B("xcb")], [B("PC")])
        ACT(C, R_[:, 0:1024], C.PB[:], AF.Sigmoid, [B("PB"), B("sm")], [bR], bias=sm[:, 64 + n:65 + n])
        ACT(C, I_[:, 0:1024], C.PC[:], AF.Sigmoid, [B("PC"), B("sm")], [bI], bias=sm[:, 72 + n:73 + n])
        ACT(C, M_[:, 0:1024], R_[:, 0:1024], AF.Exp, [bR, B("smx")], [bM], scale=smx[:, 24 + n:25 + n])
        ACT(C, R_[:, 0:1024], R_[:, 0:1024], AF.Exp, [bR, B("smx")], [bR], scale=smx[:, 16 + n:17 + n])
        ACT(C, M_[:, 0:1024], M_[:, 0:1024], AF.Sqrt, [bM], [bM], scale=-1.0, bias=1.0)
        TT(C, I_[:, 0:1024], I_[:, 0:1024], XC[:, 0:1024], ALU.mult, [bI, bXC], [bI])
        TT(C, I_[:, 0:1024], I_[:, 0:1024], M_[:, 0:1024], ALU.mult, [bI, bM], [bI])
        P.emit("dve", lambda e: e.tensor_tensor_scan(out=M_[:, 0:1024], data0=R_[:, 0:1024], data1=I_[:, 0:1024],
                                                     initial=0.0, op0=ALU.mult, op1=ALU.add),
               reads=[bR, bI], writes=[bM])
        P.emit("dve", lambda e: e.tensor_tensor_scan(out=AC[:, 0:1024], data0=R_[:, 0:1024], data1=ZB,
                                                     initial=1.0, op0=ALU.mult, op1=ALU.add),
               reads=[bR, B("zb")], writes=[bAC])
        TT(C, G1[:, n, :], M_[:, 0:1024], ZS[:, 0:1024], ALU.mult, [bM, bZS], [B("G1")])
        TT(C, G2[:, n, :], AC[:, 0:1024], ZS[:, 0:1024], ALU.mult, [bAC, bZS], [B("G2")])
        CP(C, stvo[:, n:n + 1], M_[:, 1023:1024], [bM], [B("stvo")])
    if C.stop <= 2:
        return
    F_, KK, Pc, Rc, Q_ = tmps[0], tmps[1], tmps[2], tmps[3], tmps[4]
    bF, bKK, bPc, bRc, bQ = tb[0], tb[1], tb[2], tb[3], tb[4]
    for h in range(8):
        inproj_fm(C, w0, slot, C.PA, B("PA")); slot += 1
        ACT(C, F_[:, 0:1024], C.PA[:], AF.Sigmoid, [B("PA")], [bF])
        TS(C, F_[:, 0:1024], F_[:, 0:1024], smx[:, 8 + h:9 + h], smx[:, h:h + 1], ALU.mult, ALU.add, [bF, B("smx")], [bF])
        TS(C, KK[:, 0:1024], F_[:, 0:1024], -1.0, 1.0, ALU.mult, ALU.add, [bF], [bKK])
        for c in range(16):
            P.emit("dve", lambda e, c=c: e.tensor_tensor_scan(out=Pc[:, c * 64:(c + 1) * 64], data0=F_[:, c * 64:(c + 1) * 64],
                                                              data1=ZB[:, 0:64], initial=1.0, op0=ALU.mult, op1=ALU.add),
                   reads=[bF, B("zb")], writes=[bPc])
        P.emit("dve", lambda e: e.reciprocal(out=Rc[:, 0:1024], in_=Pc[:, 0:1024]), reads=[bPc], writes=[bRc])
        TT(C, kT[:, h, :], KK[:, 0:1024], Rc[:, 0:1024], ALU.mult, [bKK, bRc], [B("kT")])
        CP(C, plast[:, h, 1:17], Pc[:, 0:1024].rearrange("p (c s) -> p c s", s=64)[:, :, 63], [bPc], [B("plast")])
        plv = plast[:, h, 0:16].rearrange("p (j two) -> p j two", two=2)
        TT(C, PP[:, h, :], plv[:, :, 0], plv[:, :, 1], ALU.mult, [B("plast")], [B("PP")])
        inproj_fm(C, w0, slot, C.PB, B("PB")); slot += 1
        ACT(C, Q_[:, 0:1024], C.PB[:], AF.Silu, [B("PB")], [bQ])
        TT(C, qT[:, h, :], Q_[:, 0:1024], Pc[:, 0:1024], ALU.mult, [bQ, bPc], [B("qT")])
        inproj_fm(C, w0, slot, C.PC, B("PC")); slot += 1
        ACT(C, zaT[:, h, :], C.PC[:], AF.Silu, [B("PC")], [B("zaT")])
    if C.stop <= 3:
        return
    for n in range(8):
        STT(C, G1[:, n, :], G2[:, n, :], stv[:, n:n + 1], G1[:, n, :], ALU.mult, ALU.add, [B("G2"), B("G1"), B("stv")], [B("G1")])
    alias_bufs([B("Vt")], [B("G2")])
    alias_bufs([B("PAh0"), B("PAh1"), B("PBh0"), B("PBh1")], [B("PA"), B("PB")])
    for g in range(2):
        def cons(j, ps, pb, g=g):
            ACT(C, Vt[:, j, g * 512:(g + 1) * 512], ps, AF.Copy, [pb], [B("Vt")])
        tm_group(C, w0, slot, 4, lambda kc, j: C.big[:, kc, j * 128:(j + 1) * 128], lambda kc, j: [C.bigb[kc]], cons)
        slot += 4
    if C.stop <= 4:
        return
    for n in range(8):
        CP(C, C.big[:, 8 + n, :], G1[:, n, :], [B("G1")], [C.bigb[8 + n]], eng="pool")
    alias_bufs([B("PCh0"), B("PCh1")], [B("PC")])
    alias_bufs([B("sqf0"), B("sqf1"), B("oc0"), B("oc1"), B("sqb0"), B("sqb1")], tb)
    CP(C, ones_bf[:], C.consts[:, 512:640], [B("consts")], [B("ones_bf")])
    U = C.stf[:, 0:1024].rearrange("p (h e) -> p h e", e=128)
    Sb = C.stb[:, 0:1024].rearrange("p (h e) -> p h e", e=128)
    Sh = C.stb[:, 1024:2048].rearrange("p (h e) -> p h e", e=128)
    caus = C.consts[:, 768:896]
    ident = C.consts[:, 0:128]
    maskA = C.consts[:, 128:256]
    ones128 = C.consts[:, 512:640]
    for h in range(8):
        ACT(C, Sb[:, h, :], U[:, h, :], AF.Copy, [B("stf")], [B(f"Sb{h}")])
        ACT(C, Sh[:, h, :], U[:, h, :], AF.Copy, [B("stf"), B("PP")], [B(f"Sh{h}")], scale=PP[:, h, 0:1])
    def FM(j, h, r):
        p_att = C.PC[:, r * 512 + 256:r * 512 + 384]
        b_att = B(f"PCh{r}")
        KVP, kvn = [(C.PB, "PB"), (C.PA, "PA")][r]
        p_kv0, p_kv1 = KVP[:, 0:128], KVP[:, 512:640]
        b_kv0, b_kv1 = B(kvn + "h0"), B(kvn + "h1")
        p_o = C.PC[:, r * 512:r * 512 + 128]
        b_o = B(f"PCh{r}")
        tk = slice(j * 128, (j + 1) * 128)
        ptk = C.PT[:, r * 1024:r * 1024 + 128]
        c0 = 2 * j
        t0_, t1_ = slice(j * 128, j * 128 + 64), slice(j * 128 + 64, (j + 1) * 128)
        TR(C, ptk, kT[:, h, tk], [B("kT")], [B(f"PT{r}")])
        MM(C, p_att, kT[:, h, tk], qT[:, h, tk], True, True, [B("kT"), B("qT")], [b_att])
        ACT(C, khat[r][:], kT[:, h, t0_], AF.Copy, [B("kT"), B("plast")], [B(f"khat{r}")], scale=plast[:, h, c0 + 1:c0 + 2])
        MM(C, p_att[0:64, 64:128], khat[r][:], qT[:, h, t1_], True, True, [B(f"khat{r}"), B("qT")], [b_att])
        ACT(C, ktok[r][:], ptk, AF.Copy, [B(f"PT{r}")], [B(f"ktok{r}")])
        TT(C, attm[r][:], p_att, caus, ALU.mult, [b_att, B("consts")], [B(f"attm{r}")])
        MM(C, p_kv0, ktok[r][0:64, :], Vt[0:64, j, h * 128:(h + 1) * 128], True, True, [B(f"ktok{r}"), B("Vt")], [b_kv0])
        MM(C, p_kv1, ktok[r][64:128, :], Vt[64:128, j, h * 128:(h + 1) * 128], True, True, [B(f"ktok{r}"), B("Vt")], [b_kv1])
        MM(C, p_o[:, 0:64], Sb[:, h, :], qT[:, h, t0_], True, False, [B(f"Sb{h}"), B("qT")], [b_o])
        MM(C, p_o[:, 64:128], Sh[:, h, :], qT[:, h, t1_], False, False, [B(f"Sh{h}"), B("qT")], [b_o])
        MM(C, p_o, Vt[:, j, h * 128:(h + 1) * 128], attm[r][:], False, True, [B("Vt"), B(f"attm{r}")], [b_o])
        ACT(C, oc[r][:], p_o, AF.Copy, [b_o], [B(f"oc{r}")])
        ACT(C, sqb[r][:], oc[r][:], AF.Square, [B(f"oc{r}")], [B(f"sqb{r}")])
        p_ms = KVP[:, 128:256]
        STT(C, U[:, h, :], U[:, h, :], plast[:, h, c0:c0 + 1], p_kv0, ALU.mult, ALU.add, [B(f"U{h}"), B("stf"), B("plast"), b_kv0], [B(f"U{h}")])
        STT(C, U[:, h, :], U[:, h, :], plast[:, h, c0 + 1:c0 + 2], p_kv1, ALU.mult, ALU.add, [B(f"U{h}"), B("plast"), b_kv1], [B(f"U{h}")])
        MM(C, p_ms, ones_bf[:], sqb[r][:], True, True, [B("ones_bf"), B(f"sqb{r}")], [b_kv0])
        ACT(C, Sb[:, h, :], U[:, h, :], AF.Copy, [B(f"U{h}"), B("plast")], [B(f"Sb{h}")], scale=plast[:, h, c0 + 2:c0 + 3])
        if j + 1 < NT:
            TS(C, Sh[:, h, :], U[:, h, :], PP[:, h, j + 1:j + 2], None, ALU.mult, ALU.bypass, [B(f"U{h}"), B("PP")], [B(f"Sh{h}")])

    def KK_(j, h, r):
        KVP, kvn = [(C.PB, "PB"), (C.PA, "PA")][r]
        p_ms = KVP[:, 128:256]
        b_ms = B(kvn + "h0")
        tk = slice(j * 128, (j + 1) * 128)
        ACT(C, sqf[r][:], p_ms, AF.Ln, [b_ms], [B(f"sqf{r}")], bias=EPS)
        ACT(C, sqf[r][:], sqf[r][:], AF.Exp, [B(f"sqf{r}")], [B(f"sqf{r}")], scale=-0.5)
        STT(C, sqf[r][:], oc[r][:], sm[:, 16 + h:17 + h], sqf[r][:], ALU.mult, ALU.mult, [B(f"oc{r}"), B("sm"), B(f"sqf{r}")], [B(f"sqf{r}")])
        TT(C, C.big[:, h, tk], sqf[r][:], zaT[:, h, tk], ALU.mult, [B(f"sqf{r}"), B("zaT")], [C.bigb[h]], eng="pool")

    its = [(j, h) for j in range(NT) for h in range(8)]
    FM(its[0][0], its[0][1], 0)
    for i in range(len(its)):
        if i + 1 < len(its):
            FM(its[i + 1][0], its[i + 1][1], (i + 1) % 2)
        KK_(its[i][0], its[i][1], i % 2)
    for h in range(8):
        ACT(C, C.stf[:, h * 128:(h + 1) * 128], U[:, h, :], AF.Copy, [B(f"U{h}"), B("plast")], [B(f"U{h}")], scale=plast[:, h, 16:17])
    if stS_o is not None:
        P.dma("sp", "stS_o", stS_o, C.stf[:, 0:1024], reads=[B(f"U{h}") for h in range(8)], writes=[B(nm["stS_o"])])
        P.dma("sp", "stv_o", stv_o, stvo[:], reads=[B("stvo")], writes=[B(nm["stv_o"])])
    if C.stop <= 5:
        return
    stage4(C, w0, slot, hin, p0, rpl, g_ple, hout, mix_bufs + [B("xcb"), B("zb"), B("sqf0"), B("sqf1"), B("oc0"), B("oc1"), B("sqb0"), B("sqb1")], HP, tmps, tb, None, nm)


def to_T(C, src_bf, src_buf, j, dstT, dst_bufs):
    B = C.B
    for half in range(2):
        ptv = C.PT[:, half * 1024:(half + 1) * 1024]
        pb = B(f"PT{half}")
        for k in range(8):
            kc = half * 8 + k
            TR(C, ptv[:, k * 128:(k + 1) * 128], src_bf[:, kc * 128:(kc + 1) * 128], [src_buf], [pb])
        dst = dstT[:, half * 8:(half + 1) * 8, j * 128:(j + 1) * 128]
        srcv = ptv.rearrange("p (a b) -> p a b", b=128)
        if half == 0:
            ACT(C, dst, srcv, AF.Copy, [pb], dst_bufs[half * 8:(half + 1) * 8])
        else:
            CP(C, dst, srcv, [pb], dst_bufs[half * 8:(half + 1) * 8])


def stage4(C, wd, slot, hin, p_d, rpl, g_ple, hout, old_bufs, HP, tmps, tb, final_gain, names):
    P, B = C.P, C.B
    A = C.arena
    pT_buf = B("pT")
    pT = A[:, 55408:57456].rearrange("p (a b) -> p a b", b=T)
    hpb = [B(f"HP{j}") for j in range(NT)]
    hb2 = [A[:, 32768 + 2048 * i:32768 + 2048 * (i + 1)] for i in range(4)]
    hb2b = [B(f"hb2_{i}") for i in range(4)]
    alias_bufs(hpb + hb2b + tb + [pT_buf], old_bufs)
    alias_bufs([B("PAh0"), B("PAh1"), B("PBh0"), B("PBh1"), B("PCh0"), B("PCh1"), B("PT0"), B("PT1")],
               [B(n) for n in ("PA", "PB", "PC", "PT0", "PT1", "PAh0", "PAh1", "PBh0", "PBh1", "PCh0", "PCh1")])
    st = [tmps[0][:, 0:512], tmps[1][:, 0:512]]
    for j in range(NT):
        i = j % 2
        pst = [tmps[5], tmps[6]][i]
        P.dma("sp", f"hs{i}", pst[:, 0:256], p_d[j * 128:(j + 1) * 128, :], writes=[tb[5 + i]])
        CP(C, hb2[i][:, 0:256], pst[:, 0:256], [tb[5 + i]], [hb2b[i]])
        for kc in range(2):
            TR(C, C.PT[:, kc * 128:(kc + 1) * 128], hb2[i][:, kc * 128:(kc + 1) * 128], [hb2b[i]], [B("PT0")])
        CP(C, pT[:, :, j * 128:(j + 1) * 128], C.PT[:, 0:256].rearrange("p (a b) -> p a b", b=128), [B("PT0")], [pT_buf])
    cnt = [0]
    for g in range(4):
        def cons(j, ps, pb, g=g):
            i = cnt[0] % 2
            cnt[0] += 1
            P.dma("sp", f"hs{i}", st[i], hin[j * 128:(j + 1) * 128, g * 512:(g + 1) * 512], reads=[B(names["hin"])], writes=[tb[i]])
            TT(C, HP[:, j, g * 512:(g + 1) * 512], ps, st[i], ALU.add, [pb, tb[i]], [hpb[j]])
        tm_group(C, wd, slot, 4, lambda kc, j: C.big[:, kc, j * 128:(j + 1) * 128], lambda kc, j: [C.bigb[kc]], cons)
        slot += 4
    if C.stop <= 6:
        return slot
    plw = [load_w(C, wd, slot + i) for i in range(4)]
    slot += 4
    sm = C.small
    junk = tmps[2]
    for j in range(NT):
        i = j % 4
        if j % 2 == 0:
            ACT(C, hb2[i], HP[:, j, :], AF.Copy, [hpb[j]], [hb2b[i]])
        else:
            CP(C, hb2[i], HP[:, j, :], [hpb[j]], [hb2b[i]])
        for g in range(4):
            PS, nm = [(C.PA, "PA"), (C.PB, "PB")][g // 2]
            ps = PS[:, (g % 2) * 512:(g % 2) * 512 + 512]
            pb = B(f"{nm}h{g % 2}")
            wt, wb = plw[g]
            for kc in range(2):
                MM(C, ps, pT[:, kc, j * 128:(j + 1) * 128], wt[:, kc * 512:(kc + 1) * 512], kc == 0, kc == 1, [wb, pT_buf], [pb])
        to_T(C, hb2[i], hb2b[i], j, C.big, C.bigb)
        q = 8 + 4 * (j % 2)
        ACT(C, junk[:, 0:1024], C.PA[:], AF.Square, [B("PAh0"), B("PAh1")], [tb[2], B(f"sm_q0{j % 2}")], accum_out=sm[:, q:q + 1])
        ACT(C, junk[:, 0:1024], C.PB[:], AF.Square, [B("PBh0"), B("PBh1")], [tb[2], B(f"sm_q1{j % 2}")], accum_out=sm[:, q + 1:q + 2])
        TT(C, sm[:, q + 2:q + 3], sm[:, q:q + 1], sm[:, q + 1:q + 2], ALU.add, [B(f"sm_q0{j % 2}"), B(f"sm_q1{j % 2}")], [B(f"sm_q2{j % 2}")])
        ACT(C, sm[:, q + 3:q + 4], sm[:, q + 2:q + 3], AF.Sqrt, [B(f"sm_q2{j % 2}")], [B(f"sm_q3{j % 2}")], scale=1.0 / D, bias=EPS)
        P.emit("dve", lambda e, j=j, q=q: e.reciprocal(out=rpl[:, j:j + 1], in_=sm[:, q + 3:q + 4]), reads=[B(f"sm_q3{j % 2}")], writes=[B("rpl")])
    if C.stop <= 8:
        return slot
    P.dma("sp", "g", C.gain[:], g_ple.partition_broadcast(128), writes=[B("gain")])
    SG, PL = tmps[3], tmps[4]
    bSG, bPL = tb[3], tb[4]
    for g in range(4):
        pw, pwb = None, None

        def cons(j, ps, pb, g=g):
            ps2 = C.PC[:, (j % 2) * 512:(j % 2) * 512 + 512]
            pb2 = B(f"PCh{j % 2}")
            for kc in range(2):
                MM(C, ps2, pT[:, kc, j * 128:(j + 1) * 128], cons.pw[:, kc * 512:(kc + 1) * 512], kc == 0, kc == 1, [cons.pwb, pT_buf], [pb2])
            ACT(C, SG[:, 0:512], ps, AF.Sigmoid, [pb], [bSG])
            STT(C, PL[:, 0:512], ps2, rpl[:, j:j + 1], C.gain[:, g * 512:(g + 1) * 512], ALU.mult, ALU.mult, [pb2, B("rpl"), B("gain")], [bPL])
            TT(C, PL[:, 0:512], PL[:, 0:512], SG[:, 0:512], ALU.mult, [bPL, bSG], [bPL])
            TT(C, HP[:, j, g * 512:(g + 1) * 512], HP[:, j, g * 512:(g + 1) * 512], PL[:, 0:512], ALU.add, [hpb[j], bPL], [hpb[j]])
        cons.pw, cons.pwb = load_w(C, wd, slot + 4)
        tm_group(C, wd, slot, 4, lambda kc, j: C.big[:, kc, j * 128:(j + 1) * 128], lambda kc, j: [C.bigb[kc]], cons)
        slot += 5
    if C.stop <= 9:
        return slot
    if final_gain is not None:
        P.dma("sp", "g", C.gain[:], final_gain.partition_broadcast(128), writes=[B("gain")])
    for j in range(NT):
        i = j % 2
        if final_gain is None:
            for q in range(4):
                P.dma("sp", f"hout{i}", hout[j * 128:(j + 1) * 128, q * 512:(q + 1) * 512], HP[:, j, q * 512:(q + 1) * 512], reads=[hpb[j]], writes=[B(names["hout"])])
        else:
            ot = A[:, 32768 + i * 4096:32768 + (i + 1) * 4096].bitcast(F32)
            ob = B(f"ot{i}")
            if j < 2:
                alias_bufs([ob], hb2b)
            sm2 = C.small
            ACT(C, ot, HP[:, j, :], AF.Square, [hpb[j]], [ob, B("sm_ssq")], accum_out=sm2[:, 0:1])
            ACT(C, sm2[:, 1:2], sm2[:, 0:1], AF.Sqrt, [B("sm_ssq")], [B("sm_sd")], scale=1.0 / D, bias=EPS)
            P.emit("dve", lambda e: e.reciprocal(out=sm2[:, 2:3], in_=sm2[:, 1:2]), reads=[B("sm_sd")], writes=[B("sm_rstd")])
            STT(C, ot, HP[:, j, :], sm2[:, 2:3], C.gain[:], ALU.mult, ALU.mult, [hpb[j], B("sm_rstd"), B("gain")], [ob])
            for q in range(4):
                P.dma("sp", f"hout{i}", hout[j * 128:(j + 1) * 128, q * 512:(q + 1) * 512], ot[:, q * 512:(q + 1) * 512], reads=[ob], writes=[B(names["hout"])])
    return slot


def _pp(v, n):
    return np.ascontiguousarray(np.asarray(v, np.float32).reshape(n, 128).T)


def prep_l0(inp):
    sm = np.zeros((128, 96), np.float32)
    sm[:, 0:8] = _pp(inp["a_lb_logits"][0], 8)
    sm[:, 8:16] = _pp(inp["a_lb_logits"][1], 8)
    sm[:, 16:24] = _pp(inp["a_norm"][0], 8)
    sm[:, 24:56] = np.asarray(inp["b_conv_w"][0], np.float32).reshape(4, 8, 128).transpose(2, 1, 0).reshape(128, 32)
    sm[:, 56:64] = _pp(inp["b_conv_b"][0], 8)
    sm[:, 64:72] = _pp(inp["b_b_r"][0], 8)
    sm[:, 72:80] = _pp(inp["b_b_i"][0], 8)
    sm[:, 80:88] = _pp(inp["b_lambda"][0], 8)
    wri = np.concatenate([np.asarray(inp["b_w_r"][0], np.float32).transpose(1, 0, 2),
                          np.asarray(inp["b_w_i"][0], np.float32).transpose(1, 0, 2)], axis=1)
    return {
        "w0": pack_l0(np.asarray(inp["e_w_in"][0], np.float32), np.asarray(inp["e_w_out"][0], np.float32),
                      np.asarray(inp["ple_gate_w"][0], np.float32), np.asarray(inp["ple_w"][0], np.float32)),
        "wri": np.ascontiguousarray(wri), "sm0": sm,
        "g_e": np.asarray(inp["e_norm"][0], np.float32), "g_ple0": np.asarray(inp["ple_norm"][0], np.float32),
        "consts_d": make_consts(),
    }


def pack_l1(o_w_in, o_w_out, gate_w, ple_w):
    slots = []
    for g in range(4):
        slots += _tm_slots(o_w_in, 2048 + 512 * g)
    for g in range(4):
        slots += _tm_slots(o_w_in, 4096 + 512 * g)
        slots += _tm_slots(o_w_in, 6144 + 512 * g)
    for h in range(8):
        slots.append(_fm_slot(o_w_in, 128 * h))
        slots.append(_fm_slot(o_w_in, 1024 + 128 * h))
    slots += pack_tail(o_w_out, gate_w, ple_w)
    return np.stack(slots).astype(np.float32)


L1_NSLOT = 16 + 32 + 16 + 40


def prep_l1(inp):
    sm = np.zeros((128, 96), np.float32)
    sm[:, 0:64] = np.asarray(inp["c_conv_w"][0], np.float32).reshape(4, 16, 128).transpose(2, 1, 0).reshape(128, 64)
    sm[:, 64:80] = _pp(inp["c_conv_b"][0], 16)
    sm[:, 80:96] = _pp(inp["c_norm"][0], 16)
    wg = np.asarray(inp["o_w_in"][0][:, 8192:8208], np.float32).reshape(16, 128, 16).transpose(1, 0, 2)
    gb = np.concatenate([np.asarray(inp["c_b_i"][0], np.float32), np.asarray(inp["c_b_f"][0], np.float32)])
    return {
        "w1": pack_l1(np.asarray(inp["o_w_in"][0], np.float32), np.asarray(inp["o_w_out"][0], np.float32),
                      np.asarray(inp["ple_gate_w"][1], np.float32), np.asarray(inp["ple_w"][1], np.float32)),
        "wg": np.ascontiguousarray(wg), "sm1": sm, "gb": gb,
        "g_o": np.asarray(inp["o_norm"][0], np.float32), "g_ple1": np.asarray(inp["ple_norm"][1], np.float32),
        "g_fin": np.asarray(inp["final_norm"], np.float32),
        "consts_d": make_consts(),
    }


def layer1(C, seg="U"):
    P, B, nc = C.P, C.B, C.nc
    dram, sb = C.dram, C.sb
    w1 = dram("w1", [L1_NSLOT, 128, 2048])
    wg_d = dram("wg", [128, 16, 16])
    sm_d = dram("sm1", [128, 96])
    gb_d = dram("gb", [16])
    g_o = dram("g_o", [D])
    g_ple = dram("g_ple1", [D])
    g_fin = dram("g_fin", [D])
    states_only = (seg == "A")
    if seg == "U":
        hin, p1 = dram("hin1", [T, D]), dram("p1", [T, 256])
        stC_d, stv_d = dram("stC", [128, 8 * 260]), dram("stv1", [128, 48])
        hout = dram("hout1", [T, D], kind="ExternalOutput")
        stC_o = dram("stC_o", [128, 8 * 260], kind="ExternalOutput")
        stv_o = dram("stv1_o", [128, 48], kind="ExternalOutput")
        nm = {"hin": "d_hin1", "hout": "d_hout1", "stC": "d_stC", "stv": "d_stv1", "stC_o": "d_stC_o", "stv_o": "d_stv1_o"}
        C.out_keys += ["hout0", "hout1", "stC_o", "stv1_o"]
    elif seg == "A":
        hin, p1 = dram("h2A", [T, D], kind="Internal"), dram("p1A", [T, 256])
        stC_d, stv_d = dram("zS", [128, 2080]), dram("zv", [128, 48])
        hout = None
        stC_o = dram("sC1", [128, 8 * 260], kind="Internal")
        stv_o = dram("sv1", [128, 48], kind="Internal")
        nm = {"hin": "d_h2A", "hout": "d_none", "stC": "d_zS", "stv": "d_zv", "stC_o": "d_sC1", "stv_o": "d_sv1"}
    else:
        hin, p1 = dram("h2B", [T, D], kind="Internal"), dram("p1B", [T, 256])
        stC_d, stv_d = dram("sC1", [128, 8 * 260], kind="Internal"), dram("sv1", [128, 48], kind="Internal")
        hout = dram("out", [T, D], kind="ExternalOutput")
        stC_o = stv_o = None
        nm = {"hin": "d_h2B", "hout": "d_out", "stC": "d_sC1", "stv": "d_sv1"}
        C.out_keys += ["hout0", "hout1"]

    A = C.arena

    def abf(off, a, b):
        return A[:, off:off + a * b].rearrange("p (a b) -> p a b", b=b)
    qT = abf(0, 8, 1024)
    kT = abf(8192, 8, 1024)
    VX = abf(16384, 64, 260)
    GT = abf(33024, 8, 2048)
    TB1 = 49408
    SGO = A[:, TB1:TB1 + 8192].bitcast(F32).rearrange("p (a b) -> p a b", b=512)
    X = A[:, TB1:TB1 + 2064].bitcast(F32)
    XC = A[:, TB1 + 2064:TB1 + 4128].bitcast(F32)
    TB = 40960

    def tmpf(i, w=1032):
        return A[:, TB + i * 2064:TB + (i + 1) * 2064].bitcast(F32)[:, 0:w]
    tmps = [tmpf(i) for i in range(7)]
    tb = [B(f"tmp{i}") for i in range(7)]
    htile = [A[:, 4096 * i:4096 * (i + 1)].bitcast(F32) for i in range(4)]
    hbf = [A[:, 16384 + 2048 * i:16384 + 2048 * (i + 1)] for i in range(4)]
    C.htile, C.hbf = htile, hbf
    HP = A[:, 0:32768].bitcast(F32).rearrange("p (a b) -> p a b", b=2048)

    wg = sb("wg_sb", [128, 16, 16], BF16)
    sm = sb("sm1_sb", [128, 96], F32)
    gbb = sb("gbb", [128, 16], F32)
    stv = sb("stv1_sb", [128, 48], F32)
    stvo = sb("stvo1_sb", [128, 48], F32)
    EK = sb("EK", [128, 8, 8], F32)
    EKC = sb("EKC", [128, 8, 8], F32)
    THR = sb("THR", [128, 8, 8], F32)
    DEC = sb("DEC", [128, 8, 16], F32)
    g8 = sb("g8", [128, 4, 8], F32)
    r8 = sb("r8", [128, 2, 8], F32)
    ycb = A[:, TB1:TB1 + 2048]
    jk = A[:, TB1 + 2048:TB1 + 2304]
    ktok = [A[:, TB1 + 2304 + 128 * i:TB1 + 2432 + 128 * i] for i in range(2)]
    scw = [A[:, TB1 + 2560 + 128 * i:TB1 + 2688 + 128 * i] for i in range(2)]
    ndsb = [A[:, TB1 + 2816 + 520 * i:TB1 + 2816 + 520 * i + 516].bitcast(F32) for i in range(2)]
    rpl = sb("rpl1", [128, 16], F32)
    Cf = C.stf[:, :].rearrange("p (h e) -> p h e", e=260)
    Cb = C.stb[:, :].rearrange("p (h e) -> p h e", e=260)
    Chs = sb("Chs", [128, 8 * 260], BF16)
    Ch = Chs[:, :].rearrange("p (h e) -> p h e", e=260)

    P.dma("pool", "wri", wg[:], wg_d, writes=[B("wg")])
    P.dma("sp", "sm", sm[:], sm_d, writes=[B("sm1")])
    P.dma("sp", "sm", gbb[:], gb_d.partition_broadcast(128), writes=[B("gbb")])
    P.dma("sp", "stv", stv[:], stv_d, reads=[B(nm["stv"])], writes=[B("stv1")])
    for q in range(4):
        P.dma("sp", "stS", C.stf[:, q * 520:(q + 1) * 520], stC_d[:, q * 520:(q + 1) * 520], reads=[B(nm["stC"])], writes=[B("stf")])
    TS(C, stv[:], stv[:], C.msk[:, 0:1], None, ALU.mult, ALU.bypass, [B("stv1"), B("msk")], [B("stv1")])
    TS(C, C.stf[:], C.stf[:], C.msk[:, 0:1], None, ALU.mult, ALU.bypass, [B("stf"), B("msk")], [B("stf")])
    P.dma("sp", "g", C.gain[:], g_o.partition_broadcast(128), writes=[B("gain")])
    for j in range(NT):
        hb = j % 4
        P.dma("sp", f"h{hb}", htile[hb], hin[j * 128:(j + 1) * 128, :], reads=[B(nm["hin"])], writes=[B(f"htile{hb}")])
        rms_to_T(C, htile[hb], B(f"htile{hb}"), j, B("gain"), C.big, C.bigb, hb)
    if C.stop <= 1:
        return
    mix_bufs = [B(n) for n in ("qT1", "kT1", "VX", "GT", "SGO", "X1", "XC1")]
    alias_bufs(mix_bufs, [B(f"htile{i}") for i in range(4)] + [B(f"hbf{i}") for i in range(4)])
    P.emit("pool", lambda e: e.memset(VX[:, :, 256:260], 0.0), writes=[B("VX")])
    P.emit("pool", lambda e: e.memset(VX[:, :, 256:257], 1.0), writes=[B("VX")])
    maskA = C.consts[:, 128:256]
    H0, H1 = C.consts[:, 256:384], C.consts[:, 384:512]
    SAME = C.consts[:, 640:768]
    LN_S = float(np.log(np.sqrt(128.0)))
    for j in range(NT):
        tk = slice(j * 128, (j + 1) * 128)
        pg = C.PC[:, 0:16]
        for kc in range(KC):
            MM(C, pg, C.big[:, kc, tk], wg[:, kc, :], kc == 0, kc == KC - 1, [B("wg"), C.bigb[kc]], [B("PCh0")])
        li, nlf, a1, t1 = g8[:, 0, :], g8[:, 1, :], g8[:, 2, :], g8[:, 3, :]
        TT(C, t1, pg[:, 8:16], gbb[:, 8:16], ALU.add, [B("PCh0"), B("gbb")], [B("g8t")])
        TT(C, li, pg[:, 0:8], gbb[:, 0:8], ALU.add, [B("PCh0"), B("gbb")], [B("g8l")])
        ACT(C, t1, t1, AF.Exp, [B("g8t")], [B("g8t")], scale=-1.0)
        ACT(C, nlf, t1, AF.Ln, [B("g8t")], [B("g8n")], bias=1.0)
        pq = C.PC[:, 512:544]
        MM(C, pq[:, 0:8], maskA, nlf, True, True, [B("consts"), B("g8n")], [B("PCh1")])
        MM(C, pq[:, 8:16], SAME, nlf, True, True, [B("consts"), B("g8n")], [B("PCh1")])
        MM(C, pq[:, 16:24], H0, nlf, True, True, [B("consts"), B("g8n")], [B("PCh1")])
        MM(C, pq[:, 24:32], H1, nlf, True, True, [B("consts"), B("g8n")], [B("PCh1")])
        TT(C, a1, li, pq[:, 0:8], ALU.add, [B("g8l"), B("PCh1")], [B("g8a")])
        ACT(C, EK[:, j, :], a1, AF.Exp, [B("g8a")], [B("EK")])
        TT(C, a1, a1, pq[:, 8:16], ALU.subtract, [B("g8a"), B("PCh1")], [B("g8a")])
        ACT(C, EKC[:, j, :], a1, AF.Exp, [B("g8a")], [B("EKC")])
        ACT(C, THR[:, j, :], pq[:, 0:8], AF.Exp, [B("PCh1")], [B("THR")], bias=LN_S)
        ACT(C, DEC[:, j, :], pq[:, 16:32], AF.Exp, [B("PCh1")], [B("DEC")], scale=-1.0)
    if C.stop <= 2:
        return
    alias_bufs([B("PAh0"), B("PAh1"), B("PBh0"), B("PBh1")], [B("PA"), B("PB")])
    slot = 0
    for g in range(4):
        def cons(j, ps, pb, g=g):
            ACT(C, VX[:, j * 8 + 2 * g:j * 8 + 2 * g + 2, 0:256], ps.rearrange("p (a b) -> p a b", b=256), AF.Copy, [pb], [B("VX")])
        tm_group(C, w1, slot, 4, lambda kc, j: C.big[:, kc, j * 128:(j + 1) * 128], lambda kc, j: [C.bigb[kc]], cons)
        slot += 4
    if C.stop <= 3:
        return
    for g in range(4):
        if states_only:
            slot = 48
            break

        def cons_o(j, ps, pb):
            ACT(C, SGO[:, j, :], ps, AF.Sigmoid, [pb], [B("SGO")])
        tm_group(C, w1, slot, 4, lambda kc, j: C.big[:, kc, j * 128:(j + 1) * 128], lambda kc, j: [C.bigb[kc]], cons_o)
        slot += 4

        def cons_z(j, ps, pb, g=g):
            ACT(C, ps, ps, AF.Silu, [pb], [pb])
            TT(C, GT[:, j, g * 512:(g + 1) * 512], ps, SGO[:, j, :], ALU.mult, [pb, B("SGO")], [B("GT")])
        tm_group(C, w1, slot, 4, lambda kc, j: C.big[:, kc, j * 128:(j + 1) * 128], lambda kc, j: [C.bigb[kc]], cons_z)
        slot += 4
    if C.stop <= 4:
        return
    alias_bufs([B("X1"), B("XC1")], [B("SGO")])
    alias_bufs([B("PA"), B("PB")], [B("PAh0"), B("PAh1"), B("PBh0"), B("PBh1")])
    for h in range(8):
        for qk in range(2):
            i = qk * 8 + h
            PS, psb = [(C.PA, B("PA")), (C.PB, B("PB"))][qk]
            inproj_fm(C, w1, slot, PS, psb); slot += 1
            ACT(C, X[:, 3:1027], PS[:], AF.Copy, [psb], [B("X1")])
            CP(C, X[:, 0:3], stv[:, 3 * i:3 * i + 3], [B("stv1")], [B("X1")])
            CP(C, stvo[:, 3 * i:3 * i + 3], X[:, 1024:1027], [B("X1")], [B("stvo1")])
            cw = lambda k: sm[:, 4 * i + k:4 * i + k + 1]
            TS(C, XC[:, 0:1024], X[:, 3:1027], cw(3), sm[:, 64 + i:65 + i], ALU.mult, ALU.add, [B("X1"), B("sm1")], [B("XC1")])
            for k in (2, 1, 0):
                STT(C, XC[:, 0:1024], X[:, k:k + 1024], cw(k), XC[:, 0:1024], ALU.mult, ALU.add, [B("X1"), B("XC1"), B("sm1")], [B("XC1")])
            dst, dbuf = (qT, B("qT1")) if qk == 0 else (kT, B("kT1"))
            ACT(C, dst[:, h, :], XC[:, 0:1024], AF.Silu, [B("XC1")], [dbuf])
    alias_bufs([B("PAh0"), B("PAh1"), B("PBh0"), B("PBh1"), B("PCh0"), B("PCh1")], [B("PA"), B("PB"), B("PC")])
    if C.stop <= 5:
        return
    for h in range(8):
        ACT(C, Cb[:, h, 0:258], Cf[:, h, 0:258], AF.Copy, [B("stf")], [B(f"Cb{h}")])
        if not states_only:
            ACT(C, Ch[:, h, 0:258], Cf[:, h, 0:258], AF.Copy, [B("stf"), B("DEC")], [B(f"Ch{h}")], scale=DEC[:, 0, h:h + 1])
    def FM(j, h, r):
        tk = slice(j * 128, (j + 1) * 128)
        p_sc = C.PC[:, r * 512 + 384:r * 512 + 512]
        b_sc = B(f"PCh{r}")
        if r == 1:
            p_kv = [C.PA[:, 0:258], C.PA[:, 512:770]]
            b_kv = [B("PAh0"), B("PAh1")]
        else:
            p_kv = [C.PB[:, 0:258], C.PB[:, 512:770]]
            b_kv = [B("PBh0"), B("PBh1")]
        p_nd = C.PC[:, r * 512:r * 512 + 258]
        b_nd = B(f"PCh{r}")
        jh = j * 8 + h
        ptk = C.PT[:, r * 1024:r * 1024 + 128]
        TR(C, ptk, kT[:, h, tk], [B("kT1")], [B(f"PT{r}")])
        if not states_only:
            MM(C, p_sc, kT[:, h, tk], qT[:, h, tk], True, True, [B("kT1"), B("qT1")], [b_sc])
        ACT(C, ktok[r][:], ptk, AF.Copy, [B(f"PT{r}"), B("EKC")], [B(f"ktk{r}")], scale=EKC[:, j, h:h + 1])
        if not states_only:
            STT(C, scw[r][:], p_sc, EK[:, j, h:h + 1], maskA, ALU.mult, ALU.mult, [b_sc, B("EK"), B("consts")], [B(f"scw{r}")])
            ACT(C, scw[r][0:64, 64:128], p_sc[0:64, 64:128], AF.Copy, [b_sc, B("EKC")], [B(f"scw{r}")], scale=EKC[0:64, j, h:h + 1])
        for c in range(2):
            rows = slice(64 * c, 64 * c + 64)
            MM(C, p_kv[c], ktok[r][rows, :], VX[rows, jh, 0:258], True, True, [B(f"ktk{r}"), B("VX")], [b_kv[c]])
        if not states_only:
            MM(C, p_nd[0:64, :], qT[:, h, j * 128:j * 128 + 64], Cb[:, h, 0:258], True, False, [B("qT1"), B(f"Cb{h}")], [b_nd])
            MM(C, p_nd[64:128, :], qT[:, h, j * 128 + 64:(j + 1) * 128], Ch[:, h, 0:258], True, False, [B("qT1"), B(f"Ch{h}")], [b_nd])
            MM(C, p_nd, scw[r][:], VX[:, jh, 0:258], False, True, [B(f"scw{r}"), B("VX")], [b_nd])
            ACT(C, ndsb[r][:], p_nd, AF.Copy, [b_nd], [B(f"ndsb{r}")])
        for c in range(2):
            STT(C, Cf[:, h, 0:258], Cf[:, h, 0:258], DEC[:, j, 8 * c + h:8 * c + h + 1], p_kv[c], ALU.mult, ALU.add,
                [B(f"Cf{h}"), B("stf"), B("DEC"), b_kv[c]], [B(f"Cf{h}")])
        if not states_only:
            ACT(C, Cb[:, h, 0:258], Cf[:, h, 0:258], AF.Copy, [B(f"Cf{h}")], [B(f"Cb{h}")])
            if j + 1 < NT:
                TS(C, Ch[:, h, 0:258], Cf[:, h, 0:258], DEC[:, j + 1, h:h + 1], None, ALU.mult, ALU.bypass, [B(f"Cf{h}"), B("DEC")], [B(f"Ch{h}")])

    def KK_(j, h, r):
        nd = ndsb[r]
        b_nd = B(f"ndsb{r}")
        s_ = r8[:, r, :]
        sbn = B(f"r8_{r}")
        ACT(C, s_[:, 0:1], nd[:, 256:257], AF.Abs, [b_nd], [sbn])
        TT(C, s_[:, 0:1], s_[:, 0:1], THR[:, j, h:h + 1], ALU.max, [sbn, B("THR")], [sbn])
        P.emit("dve", lambda e, s_=s_: e.reciprocal(out=s_[:, 1:2], in_=s_[:, 0:1]), reads=[sbn], writes=[sbn])
        ACT(C, jk[:], nd[:, 0:256], AF.Square, [b_nd, sbn], [B("jk"), sbn], scale=s_[:, 1:2], accum_out=s_[:, 2:3])
        ACT(C, s_[:, 3:4], s_[:, 2:3], AF.Sqrt, [sbn], [sbn], scale=1.0 / 256.0, bias=EPS)
        P.emit("dve", lambda e, s_=s_: e.reciprocal(out=s_[:, 4:5], in_=s_[:, 3:4]), reads=[sbn], writes=[sbn])
        TT(C, s_[:, 5:6], s_[:, 4:5], s_[:, 1:2], ALU.mult, [sbn], [sbn])
        STT(C, ycb[:, h * 256:(h + 1) * 256], nd[:, 0:256], s_[:, 5:6], GT[:, j, h * 256:(h + 1) * 256], ALU.mult, ALU.mult,
            [b_nd, sbn, B("GT")], [B("ycb")])

    def YT(j):
        tk = slice(j * 128, (j + 1) * 128)
        for half in range(2):
            ptv = C.PT[:, half * 1024:(half + 1) * 1024]
            pb = B(f"PT{half}")
            for k in range(8):
                kc = half * 8 + k
                TR(C, ptv[:, k * 128:(k + 1) * 128], ycb[:, kc * 128:(kc + 1) * 128], [B("ycb")], [pb])
            for k in range(8):
                kc = half * 8 + k
                ACT(C, C.big[:, kc, tk], ptv[:, k * 128:(k + 1) * 128], AF.Copy, [pb, B("sm1")], [C.bigb[kc]], scale=sm[:, 80 + kc:81 + kc])

    its = [(j, h) for j in range(NT) for h in range(8)]
    if states_only:
        for i, (j, h) in enumerate(its):
            FM(j, h, i % 2)
    else:
        FM(its[0][0], its[0][1], 0)
        for i in range(len(its)):
            if i + 1 < len(its):
                FM(its[i + 1][0], its[i + 1][1], (i + 1) % 2)
            KK_(its[i][0], its[i][1], i % 2)
            if its[i][1] == 7:
                YT(its[i][0])
    if stC_o is not None:
        for q in range(4):
            P.dma("sp", "stC_o", stC_o[:, q * 520:(q + 1) * 520], C.stf[:, q * 520:(q + 1) * 520], reads=[B(f"Cf{h}") for h in range(8)],
                  writes=[B(nm["stC_o"])])
        P.dma("sp", "stv1_o", stv_o, stvo[:], reads=[B("stvo1")], writes=[B(nm["stv_o"])])
    if states_only:
        return
    stage4(C, w1, slot, hin, p1, rpl, g_ple, hout, mix_bufs + [B("ycb"), B("jk"), B("ktk0"), B("ktk1"), B("scw0"), B("scw1"), B("ndsb0"), B("ndsb1")],
           HP, tmps, tb, g_fin, nm)


def _run_layer_unfused(nc, shared, per_core, st_names, out_name):
    zeros = {k: np.zeros(shape, np.float32) for k, (shape, _) in st_names.items()}

    def maps(states):
        ms = []
        for c in range(8):
            m = dict(shared)
            m.update(per_core[c])
            for k in st_names:
                m[k] = states[c][k]
            ms.append(m)
        return ms
    r1 = run_bass_kernel_spmd(nc, maps([zeros] * 8), core_ids=list(range(8)))
    st = []
    for c in range(8):
        if c % 2 == 1:
            st.append({k: np.asarray(r1.results[c - 1][o], np.float32) for k, (_, o) in st_names.items()})
        else:
            st.append(zeros)
    r2 = run_bass_kernel_spmd(nc, maps(st), core_ids=list(range(8)))
    return [np.asarray(r2.results[c][out_name]) for c in range(8)]


def kernel_unfused(**inp):
    inp = {k: np.asarray(v) for k, v in inp.items()}
    x, p = inp["x"], inp["p"]
    s0 = prep_l0(inp)
    s0["msk"] = np.ones((128, 1), np.float32)
    nc0 = build_program([0])
    pc = [{"hin": np.ascontiguousarray(x[c // 2, (c % 2) * T:(c % 2 + 1) * T], dtype=np.float32),
           "p0": np.ascontiguousarray(p[0, c // 2, (c % 2) * T:(c % 2 + 1) * T], dtype=np.float32)} for c in range(8)]
    h2 = _run_layer_unfused(nc0, s0, pc, {"stS": ((128, 1024), "stS_o"), "stv": ((128, 32), "stv_o")}, "hout")
    del s0
    s1 = prep_l1(inp)
    s1["msk"] = np.ones((128, 1), np.float32)
    nc1 = build_program([1])
    pc = [{"hin1": np.ascontiguousarray(h2[c], dtype=np.float32),
           "p1": np.ascontiguousarray(p[1, c // 2, (c % 2) * T:(c % 2 + 1) * T], dtype=np.float32)} for c in range(8)]
    out = _run_layer_unfused(nc1, s1, pc, {"stC": ((128, 8 * 260), "stC_o"), "stv1": ((128, 48), "stv1_o")}, "hout1")
    return np.stack(out).reshape(4, 2 * T, D).astype(np.float32)


def make_in_maps(inp):
    inp = {k: np.asarray(v) for k, v in inp.items()}
    x, p = np.asarray(inp["x"], np.float32), np.asarray(inp["p"], np.float32)
    shared = {}
    shared.update(prep_l0(inp))
    shared.update(prep_l1(inp))
    shared["zS"] = np.zeros((128, 2080), np.float32)
    shared["zv"] = np.zeros((128, 48), np.float32)
    zx = np.zeros((T, D), np.float32)
    zp = np.zeros((T, 256), np.float32)
    maps = []
    for c in range(8):
        b, hf = c // 2, c % 2
        m = dict(shared)
        m["xB"] = np.ascontiguousarray(x[b, hf * T:(hf + 1) * T])
        m["p0B"] = np.ascontiguousarray(p[0, b, hf * T:(hf + 1) * T])
        m["p1B"] = np.ascontiguousarray(p[1, b, hf * T:(hf + 1) * T])
        if hf == 1:
            m["xA"] = np.ascontiguousarray(x[b, 0:T])
            m["p0A"] = np.ascontiguousarray(p[0, b, 0:T])
            m["p1A"] = np.ascontiguousarray(p[1, b, 0:T])
        else:
            m["xA"], m["p0A"], m["p1A"] = zx, zp, zp
        m["msk"] = np.full((128, 1), float(hf), np.float32)
        maps.append(m)
    return maps


def kernel(**inp):
    maps = make_in_maps(inp)
    nc = build_program("fused")
    res = run_bass_kernel_spmd(nc, maps, core_ids=list(range(8)))
    out = [np.asarray(res.results[c]["out"], np.float32) for c in range(8)]
    return np.stack(out).reshape(4, 2 * T, D)
```

```python
import numpy as np
import ml_dtypes
import concourse.bass as bass
import concourse.mybir as mybir
from concourse.bass_utils import run_bass_kernel_spmd

F32 = mybir.dt.float32
BF16 = mybir.dt.bfloat16
AF = mybir.ActivationFunctionType
ALU = mybir.AluOpType
AX = mybir.AxisListType


class Buf:
    __slots__ = ("name", "last_w", "readers")

    def __init__(self, name):
        self.name = name
        self.last_w = None
        self.readers = []


class Op:
    __slots__ = ("eng", "idx", "thunk", "waits", "dma_waits", "signal", "sigval",
                 "is_dma", "dsem", "dval")

    def __init__(self, eng, thunk):
        self.eng = eng
        self.thunk = thunk
        self.waits = {}
        self.dma_waits = {}
        self.signal = False
        self.sigval = 0
        self.is_dma = False
        self.dsem = None
        self.dval = 0


ENGS = ("pe", "act", "dve", "pool", "sp")


class Prog:
    def __init__(self, nc):
        self.nc = nc
        self.ops = {e: [] for e in ENGS}
        self.waited = {e: {} for e in ENGS}
        self.dma_sem_val = {}
        self.dma_last = {}
        self.dma_keys = []

    def eng_obj(self, e):
        nc = self.nc
        return {"pe": nc.tensor, "act": nc.scalar, "dve": nc.vector,
                "pool": nc.gpsimd, "sp": nc.sync}[e]

    def _deps(self, op, reads, writes, acc_ok=()):
        deps = []
        for b in reads:
            if b.last_w is not None:
                deps.append(b.last_w)
        for b in writes:
            if b.last_w is not None:
                if not (b in acc_ok and b.last_w.eng == op.eng):
                    deps.append(b.last_w)
            for r in b.readers:
                deps.append(r)
        e = op.eng
        for d in deps:
            if d is op:
                continue
            if d.is_dma:
                cur = self.waited[e].get(("dma", d.dsem), 0)
                if d.dval > cur:
                    op.dma_waits[d.dsem] = max(op.dma_waits.get(d.dsem, 0), d.dval)
                    self.waited[e][("dma", d.dsem)] = d.dval
            else:
                if d.eng == e and e == "pe":
                    continue
                cur = self.waited[e].get(d.eng, -1)
                if d.idx > cur:
                    prev = op.waits.get(d.eng)
                    if prev is None or d.idx > prev.idx:
                        op.waits[d.eng] = d
        for k, d in op.waits.items():
            d.signal = True
            self.waited[e][k] = max(self.waited[e].get(k, -1), d.idx)
        for b in reads:
            b.readers.append(op)
        for b in writes:
            b.last_w = op
            b.readers = []

    def emit(self, eng, thunk, reads=(), writes=(), acc_ok=()):
        op = Op(eng, thunk)
        op.idx = len(self.ops[eng])
        self._deps(op, reads, writes, acc_ok)
        self.ops[eng].append(op)
        return op

    def dma(self, eng, key, out, in_, reads=(), writes=(), fn=None, **kw):
        if key not in self.dma_sem_val:
            self.dma_sem_val[key] = 0
            self.dma_keys.append(key)
        op = Op(eng, None)
        op.idx = len(self.ops[eng])
        op.is_dma = True
        op.dsem = key
        prev = self.dma_last.get(key)
        self._deps(op, reads, writes)
        if prev is not None:
            cur = self.waited[eng].get(("dma", key), 0)
            if prev.dval > cur:
                op.dma_waits[key] = max(op.dma_waits.get(key, 0), prev.dval)
                self.waited[eng][("dma", key)] = prev.dval
        self.dma_sem_val[key] += 16
        op.dval = self.dma_sem_val[key]
        self.dma_last[key] = op
        op.thunk = (out, in_, kw, fn)
        self.ops[eng].append(op)
        return op

    def finalize(self, sems):
        for e in ENGS:
            c = 0
            for op in self.ops[e]:
                if op.signal:
                    c += 1
                    op.sigval = c

        def run_engine(e, eng):
            for op in self.ops[e]:
                for k, d in op.waits.items():
                    eng.wait_ge(sems[k], d.sigval)
                for k, v in op.dma_waits.items():
                    eng.wait_ge(sems[("dma", k)], v)
                if op.is_dma:
                    out, in_, kw, fn = op.thunk
                    ins = fn(eng) if fn is not None else eng.dma_start(out=out, in_=in_, **kw)
                    ins.then_inc(sems[("dma", op.dsem)], 16)
                    if op.signal:
                        raise RuntimeError("dma op cannot signal engine sem")
                else:
                    ins = op.thunk(eng)
                    if op.signal:
                        ins.then_inc(sems[e], 1)
        return run_engine


def run_prog(nc, prog, final_waits=()):
    from contextlib import ExitStack
    with ExitStack() as st:
        sems = {}
        for e in ENGS:
            sems[e] = st.enter_context(nc.semaphore("s_" + e))
        for k in prog.dma_keys:
            sems[("dma", k)] = st.enter_context(nc.semaphore("d_" + str(k)))
        block = st.enter_context(nc.Block())
        runner = prog.finalize(sems)

        @block.tensor
        def _(eng):
            runner("pe", eng)

        @block.scalar
        def _(eng):
            runner("act", eng)

        @block.vector
        def _(eng):
            runner("dve", eng)

        @block.gpsimd
        def _(eng):
            runner("pool", eng)

        @block.sync
        def _(eng):
            runner("sp", eng)
            for k in final_waits:
                eng.wait_ge(sems[("dma", k)], prog.dma_sem_val[k])


D = 2048
T = 1024
NT = 8
KC = 16
EPS = 1e-6
NW = 6
ARENA = 57600
L0_NSLOT = 40 + 8 + 40


def _fm_slot(W, c0):
    blk = W[:, c0:c0 + 128].reshape(KC, 128, 128)
    return np.ascontiguousarray(blk.transpose(1, 0, 2)).reshape(128, 2048)


def _tm_slots(W, c0):
    out = []
    K = W.shape[0] // 128
    blk = W[:, c0:c0 + 512].reshape(K, 128, 512)
    for kcg in range(K // 4):
        out.append(np.ascontiguousarray(blk[kcg * 4:(kcg + 1) * 4].transpose(1, 0, 2)).reshape(128, 2048))
    return out


def _pl_slot(ple_w, g):
    out = np.zeros((128, 2048), np.float32)
    blk = ple_w[:, g * 512:(g + 1) * 512].reshape(2, 128, 512)
    out[:, 0:1024] = blk.transpose(1, 0, 2).reshape(128, 1024)
    return out


def pack_tail(w_out, gate_w, ple_w):
    slots = []
    for g in range(4):
        slots += _tm_slots(w_out, 512 * g)
    for g in range(4):
        slots.append(_pl_slot(ple_w, g))
    for g in range(4):
        slots += _tm_slots(gate_w, 512 * g)
        slots.append(_pl_slot(ple_w, g))
    return slots


def pack_l0(e_w_in, e_w_out, gate_w, ple_w):
    slots = []
    for n in range(8):
        slots.append(_fm_slot(e_w_in, 4096 + 128 * n))
        slots.append(_fm_slot(e_w_in, 5120 + 128 * n))
    for h in range(8):
        slots.append(_fm_slot(e_w_in, 1024 + 128 * h))
        slots.append(_fm_slot(e_w_in, 128 * h))
        slots.append(_fm_slot(e_w_in, 3072 + 128 * h))
    for g in range(2):
        slots += _tm_slots(e_w_in, 2048 + 512 * g)
    slots += pack_tail(e_w_out, gate_w, ple_w)
    return np.stack(slots).astype(np.float32)


def make_consts():
    c = np.zeros((128, 7, 128), np.float32)
    idx = np.arange(128)
    c[:, 0] = np.eye(128)
    same = (idx[:, None] // 64) == (idx[None, :] // 64)
    c[:, 1] = (same & (idx[:, None] <= idx[None, :])).astype(np.float32)
    c[:, 2] = (idx[:, None] < 64).astype(np.float32) * np.ones((1, 128), np.float32)
    c[:, 3] = (idx[:, None] >= 64).astype(np.float32) * np.ones((1, 128), np.float32)
    c[:, 4] = 1.0 / 128.0
    c[:, 5] = same.astype(np.float32)
    c[:, 6] = (idx[:, None] <= idx[None, :]).astype(np.float32)
    return c.reshape(128, 896)


class Ctx:
    pass


def fence(C):
    P = C.P
    ops = [P.ops[e][-1] for e in ENGS if P.ops[e]] + list(P.dma_last.values())
    C.fence_ops = ops
    for b in C.bufs.values():
        b.readers = b.readers + ops


def build_program(layers, fused=False):
    nc = bass.Bass("TRN2", target_bir_lowering=False)
    from contextlib import ExitStack
    st = ExitStack()
    with st:
        P = Prog(nc)
        C = Ctx()
        C.nc, C.P = nc, P
        C.bufs = {}
        C.fence_ops = []
        C.sbs = {}
        C.drams = {}

        def dram(name, shape, dt=F32, kind="ExternalInput"):
            if name not in C.drams:
                if kind == "Internal":
                    C.drams[name] = nc.dram_tensor(name, list(shape), dt, kind=kind, addr_space="Local").ap()
                else:
                    C.drams[name] = nc.dram_tensor(name, list(shape), dt, kind=kind).ap()
            return C.drams[name]

        def sb(name, shape, dt=F32):
            if name not in C.sbs:
                C.sbs[name] = st.enter_context(nc.sbuf_tensor(name, list(shape), dt))
            return C.sbs[name]

        def B(name):
            if name not in C.bufs:
                b = Buf(name)
                b.readers = list(C.fence_ops)
                C.bufs[name] = b
            return C.bufs[name]

        C.dram, C.sb, C.B = dram, sb, B
        C.PA = st.enter_context(nc.psum_tensor("PA", [128, 1024], F32))
        C.PB = st.enter_context(nc.psum_tensor("PB", [128, 1024], F32))
        C.PC = st.enter_context(nc.psum_tensor("PC", [128, 1024], F32))
        C.PT = st.enter_context(nc.psum_tensor("PT", [128, 2048], BF16))
        C.wring = [sb(f"wr{i}", [128, 2048], BF16) for i in range(NW)]
        C.wslot_n = 0
        C.consts = sb("consts", [128, 896], F32)
        C.ident_bf = sb("ident_bf", [128, 128], BF16)
        C.gain = sb("gain", [128, 2048], F32)
        C.big = sb("big", [128, KC, T], BF16)
        C.bigb = [B(f"big{kc}") for kc in range(KC)]
        C.arena = sb("arena", [128, ARENA], BF16)
        C.small = sb("small", [128, 64], F32)
        C.stf = sb("stf", [128, 8 * 260], F32)
        C.stb = sb("stb", [128, 8 * 260], BF16)
        C.msk = sb("msk_sb", [128, 1], F32)
        consts_d = dram("consts_d", [128, 896])
        P.dma("sp", "c", C.consts[:], consts_d, writes=[B("consts")])
        P.dma("sp", "c", C.msk[:], dram("msk", [128, 1]), writes=[B("msk")])
        P.emit("dve", lambda e: e.tensor_copy(out=C.ident_bf[:], in_=C.consts[:, 0:128]),
               reads=[B("consts")], writes=[B("ident_bf")])
        C.out_keys = []
        import os
        C.stop = int(os.environ.get('STOP', '99'))
        C.rstop = int(os.environ.get('RSTOP', '99'))
        if layers == "fused":
            layer0(C, "A")
            fence(C)
            layer1(C, "A")
            fence(C)
            layer0(C, "B")
            fence(C)
            layer1(C, "B")
        else:
            for li in layers:
                if li == 0:
                    layer0(C, "U")
                else:
                    layer1(C, "U")
        run_prog(nc, P, final_waits=[k for k in C.out_keys if k in P.dma_sem_val])
    return nc


def ACT(C, out, in_, func, R, W, **kw):
    return C.P.emit("act", lambda e: e.activation(out=out, in_=in_, func=func, **kw), reads=R, writes=W)


def TS(C, out, in0, s1, s2, op0, op1, R, W, eng="dve"):
    return C.P.emit(eng, lambda e: e.tensor_scalar(out=out, in0=in0, scalar1=s1, scalar2=s2, op0=op0, op1=op1),
                    reads=R, writes=W)


def TT(C, out, in0, in1, op, R, W, eng="dve"):
    return C.P.emit(eng, lambda e: e.tensor_tensor(out=out, in0=in0, in1=in1, op=op), reads=R, writes=W)


def STT(C, out, in0, scalar, in1, op0, op1, R, W):
    return C.P.emit("dve", lambda e: e.scalar_tensor_tensor(out=out, in0=in0, scalar=scalar, in1=in1, op0=op0, op1=op1),
                    reads=R, writes=W)


def CP(C, out, in_, R, W, eng="dve"):
    return C.P.emit(eng, lambda e: e.tensor_copy(out=out, in_=in_), reads=R, writes=W)


def MM(C, out, lhsT, rhs, start, stop, R, W):
    return C.P.emit("pe", lambda e: e.matmul(out, lhsT=lhsT, rhs=rhs, start=start, stop=stop),
                    reads=R, writes=W, acc_ok=W)


def TR(C, out, in_, R, W):
    return C.P.emit("pe", lambda e: e.transpose(out, in_, C.ident_bf[:]), reads=R + [C.B("ident_bf")], writes=W, acc_ok=W)


def load_w(C, wd, slot):
    i = C.wslot_n % NW
    C.wslot_n += 1
    b = C.B(f"wr{i}")
    C.P.dma("pool", f"w{i}", C.wring[i][:], wd[slot], writes=[b])
    return C.wring[i], b


def rms_to_T(C, src_tile, src_buf, j, gain_ready_buf, dstT, dst_bufs, hb):
    B = C.B
    sm = C.small[:, 16 + 4 * hb:20 + 4 * hb]
    junk = C.hbf[hb]
    ACT(C, junk[:], src_tile, AF.Square, [src_buf], [B(f"hbf{hb}"), B(f"sm_ssq{hb}")], accum_out=sm[:, 0:1])
    ACT(C, sm[:, 1:2], sm[:, 0:1], AF.Sqrt, [B(f"sm_ssq{hb}")], [B(f"sm_sd{hb}")], scale=1.0 / D, bias=EPS)
    C.P.emit("dve", lambda e: e.reciprocal(out=sm[:, 2:3], in_=sm[:, 1:2]), reads=[B(f"sm_sd{hb}")], writes=[B(f"sm_rstd{hb}")])
    STT(C, junk[:], src_tile, sm[:, 2:3], C.gain[:], ALU.mult, ALU.mult,
        [src_buf, B(f"sm_rstd{hb}"), gain_ready_buf], [B(f"hbf{hb}")])
    for half in range(2):
        ptv = C.PT[:, half * 1024:(half + 1) * 1024]
        pb = B(f"PT{half}")
        for k in range(8):
            kc = half * 8 + k
            TR(C, ptv[:, k * 128:(k + 1) * 128], junk[:, kc * 128:(kc + 1) * 128], [B(f"hbf{hb}")], [pb])
        eng = "act" if half == 0 else "dve"
        dst = dstT[:, half * 8:(half + 1) * 8, j * 128:(j + 1) * 128]
        srcv = ptv.rearrange("p (a b) -> p a b", b=128)
        if eng == "act":
            ACT(C, dst, srcv, AF.Copy, [pb], dst_bufs)
        else:
            CP(C, dst, srcv, [pb], dst_bufs)


def alias_bufs(new_bufs, old_bufs):
    ops = []
    for ob in old_bufs:
        ops += ob.readers
        if ob.last_w is not None:
            ops.append(ob.last_w)
    for nb in new_bufs:
        nb.readers = nb.readers + ops


def inproj_fm(C, wd, slot, PS, psb):
    wt, wb = load_w(C, wd, slot)
    for half in range(2):
        for kc in range(KC):
            MM(C, PS[:, half * 512:(half + 1) * 512], wt[:, kc * 128:(kc + 1) * 128],
               C.big[:, kc, half * 512:(half + 1) * 512], kc == 0, kc == KC - 1,
               [wb, C.bigb[kc]], [psb])


def tm_group(C, wd, slot0, K4, lhs_fn, lhs_bufs_fn, consume):
    wts = [load_w(C, wd, slot0 + i) for i in range(K4)]
    for j in range(NT):
        PS, nm = [(C.PA, "PA"), (C.PB, "PB")][(j // 2) % 2]
        ps = PS[:, (j % 2) * 512:(j % 2) * 512 + 512]
        pb = C.B(f"{nm}h{j % 2}")
        nk = K4 * 4
        for kc in range(nk):
            wt, wb = wts[kc // 4]
            MM(C, ps, lhs_fn(kc, j), wt[:, (kc % 4) * 512:(kc % 4) * 512 + 512], kc == 0, kc == nk - 1,
               [wb] + lhs_bufs_fn(kc, j), [pb])
        consume(j, ps, pb)


def layer0(C, seg="U"):
    P, B, nc = C.P, C.B, C.nc
    dram, sb = C.dram, C.sb
    w0 = dram("w0", [L0_NSLOT, 128, 2048])
    wri_d = dram("wri", [128, 16, 128])
    sm_d = dram("sm0", [128, 96])
    g_e = dram("g_e", [D])
    g_ple = dram("g_ple0", [D])
    if seg == "U":
        hin, p0 = dram("hin", [T, D]), dram("p0", [T, 256])
        stS_d, stv_d = dram("stS", [128, 1024]), dram("stv", [128, 32])
        hout = dram("hout", [T, D], kind="ExternalOutput")
        stS_o = dram("stS_o", [128, 1024], kind="ExternalOutput")
        stv_o = dram("stv_o", [128, 32], kind="ExternalOutput")
        nm = {"hin": "d_hin", "hout": "d_hout", "stS": "d_stS", "stv": "d_stv", "stS_o": "d_stS_o", "stv_o": "d_stv_o"}
        C.out_keys += ["hout0", "hout1", "stS_o", "stv_o"]
    elif seg == "A":
        hin, p0 = dram("xA", [T, D]), dram("p0A", [T, 256])
        stS_d, stv_d = dram("zS", [128, 2080])[:, 0:1024], dram("zv", [128, 48])[:, 0:32]
        hout = dram("h2A", [T, D], kind="Internal")
        stS_o = dram("sS0", [128, 1024], kind="Internal")
        stv_o = dram("sv0", [128, 32], kind="Internal")
        nm = {"hin": "d_xA", "hout": "d_h2A", "stS": "d_zS", "stv": "d_zv", "stS_o": "d_sS0", "stv_o": "d_sv0"}
    else:
        hin, p0 = dram("xB", [T, D]), dram("p0B", [T, 256])
        stS_d, stv_d = dram("sS0", [128, 1024], kind="Internal"), dram("sv0", [128, 32], kind="Internal")
        hout = dram("h2B", [T, D], kind="Internal")
        stS_o = stv_o = None
        nm = {"hin": "d_xB", "hout": "d_h2B", "stS": "d_sS0", "stv": "d_sv0"}

    A = C.arena
    def abf(off, a, b):
        return A[:, off:off + a * b].rearrange("p (a b) -> p a b", b=b)
    G1 = abf(0, 8, 1024)
    G2 = abf(8192, 8, 1024)
    Vt = abf(8192, 8, 1024)
    qT = abf(16384, 8, 1024)
    kT = abf(24576, 8, 1024)
    zaT = abf(32768, 8, 1024)
    TB = 40960
    def tmpf(i, w=1032):
        return A[:, TB + i * 2064:TB + (i + 1) * 2064].bitcast(F32)[:, 0:w]
    tmps = [tmpf(i) for i in range(7)]
    tb = [B(f"tmp{i}") for i in range(7)]
    XCB = A[:, TB + 7 * 2064:TB + 7 * 2064 + 1024]
    ZB = A[:, TB + 7 * 2064 + 1024:TB + 7 * 2064 + 2048]
    htile = [A[:, 4096 * i:4096 * (i + 1)].bitcast(F32) for i in range(4)]
    hbf = [A[:, 16384 + 2048 * i:16384 + 2048 * (i + 1)] for i in range(4)]
    C.htile, C.hbf = htile, hbf
    HP = A[:, 0:32768].bitcast(F32).rearrange("p (a b) -> p a b", b=2048)

    wri = sb("wri_sb", [128, 16, 128], BF16)
    sm = sb("sm0_sb", [128, 96], F32)
    smx = sb("smx", [128, 64], F32)
    plast = sb("plast", [128, 8, 17], F32)
    PP = sb("PP", [128, 8, 8], F32)
    khat = [sb(f"khat{i}", [128, 64], BF16) for i in range(2)]
    stv = sb("stv_sb", [128, 32], F32)
    stvo = sb("stvo_sb", [128, 32], F32)
    ktok = [sb(f"ktok{i}", [128, 128], BF16) for i in range(2)]
    attm = [sb(f"attm{i}", [128, 128], BF16) for i in range(2)]
    sqf = [A[:, TB + 256 * i:TB + 256 * (i + 1)].bitcast(F32) for i in range(2)]
    oc = [A[:, TB + 512 + 256 * i:TB + 512 + 256 * (i + 1)].bitcast(F32) for i in range(2)]
    sqb = [A[:, TB + 1024 + 128 * i:TB + 1024 + 128 * (i + 1)] for i in range(2)]
    ones_bf = sb("ones_bf", [128, 128], BF16)
    rpl = sb("rpl", [128, 16], F32)

    P.dma("pool", "wri", wri[:], wri_d, writes=[B("wri")])
    P.dma("sp", "sm", sm[:], sm_d, writes=[B("sm")])
    P.dma("sp", "stv", stv[:], stv_d, reads=[B(nm["stv"])], writes=[B("stv")])
    P.dma("sp", "stS", C.stf[:, 0:1024], stS_d, reads=[B(nm["stS"])], writes=[B("stf")])
    TS(C, stv[:], stv[:], C.msk[:, 0:1], None, ALU.mult, ALU.bypass, [B("stv"), B("msk")], [B("stv")])
    TS(C, C.stf[:, 0:1024], C.stf[:, 0:1024], C.msk[:, 0:1], None, ALU.mult, ALU.bypass, [B("stf"), B("msk")], [B("stf")])
    P.emit("pool", lambda e: e.memset(ZB, 0.0), writes=[B("zb")])
    P.emit("pool", lambda e: e.memset(plast[:], 1.0), writes=[B("plast")])
    TT(C, smx[:, 0:8], sm[:, 0:8], sm[:, 8:16], ALU.subtract, [B("sm")], [B("smx")])
    ACT(C, smx[:, 0:8], smx[:, 0:8], AF.Sigmoid, [B("smx")], [B("smx")])
    TS(C, smx[:, 8:16], smx[:, 0:8], -1.0, 1.0, ALU.mult, ALU.add, [B("smx")], [B("smx")])
    ACT(C, smx[:, 32:40], sm[:, 80:88], AF.Exp, [B("sm")], [B("smx")], scale=-1.0)
    ACT(C, smx[:, 32:40], smx[:, 32:40], AF.Ln, [B("smx")], [B("smx")], bias=1.0)
    TS(C, smx[:, 16:24], smx[:, 32:40], -8.0, None, ALU.mult, ALU.bypass, [B("smx")], [B("smx")])
    TS(C, smx[:, 24:32], smx[:, 32:40], -16.0, None, ALU.mult, ALU.bypass, [B("smx")], [B("smx")])

    P.dma("sp", "g", C.gain[:], g_e.partition_broadcast(128), writes=[B("gain")])
    for j in range(NT):
        hb = j % 4
        P.dma("sp", f"h{hb}", htile[hb], hin[j * 128:(j + 1) * 128, :], reads=[B(nm["hin"])], writes=[B(f"htile{hb}")])
        rms_to_T(C, htile[hb], B(f"htile{hb}"), j, B("gain"), C.big, C.bigb, hb)
    if C.stop <= 1:
        return
    mix_bufs = [B(n) for n in ("G1", "G2", "qT", "kT", "zaT")] + tb + [B("xcb"), B("Vt")]
    alias_bufs(mix_bufs, [B(f"htile{i}") for i in range(4)] + [B(f"hbf{i}") for i in range(4)])

    X, XC, R_, I_, M_, AC, ZS = tmps
    bX, bXC, bR, bI, bM, bAC, bZS = tb
    slot = 0
    inproj_fm(C, w0, slot, C.PA, B("PA")); slot += 1
    for n in range(8):
        ACT(C, X[:, 3:1027], C.PA[:], AF.Copy, [B("PA")], [bX])
        CP(C, X[:, 0:3], stv[:, 8 + 3 * n:11 + 3 * n], [B("stv")], [bX])
        CP(C, stvo[:, 8 + 3 * n:11 + 3 * n], X[:, 1024:1027], [bX], [B("stvo")])
        inproj_fm(C, w0, slot, C.PA, B("PA")); slot += 1
        ACT(C, ZS[:, 0:1024], C.PA[:], AF.Silu, [B("PA")], [bZS])
        if n + 1 < 8:
            inproj_fm(C, w0, slot, C.PA, B("PA")); slot += 1
        cw = lambda k: sm[:, 24 + 4 * n + k:25 + 4 * n + k]
        TS(C, XC[:, 0:1024], X[:, 3:1027], cw(3), sm[:, 56 + n:57 + n], ALU.mult, ALU.add, [bX, B("sm")], [bXC])
        for k in (2, 1, 0):
            STT(C, XC[:, 0:1024], X[:, k:k + 1024], cw(k), XC[:, 0:1024], ALU.mult, ALU.add, [bX, bXC, B("sm")], [bXC])
        ACT(C, XCB, XC[:, 0:1024], AF.Copy, [bXC], [B("xcb")])
        for half in range(2):
            MM(C, C.PB[:, half * 512:(half + 1) * 512], wri[:, n, :], XCB[:, half * 512:(half + 1) * 512], True, True,
               [B("wri"), B("xcb")], [B("PB")])
            MM(C, C.PC[:, half * 512:(half + 1) * 512], wri[:, 8 + n, :], XCB[:, half * 512:(half + 1) * 512], True, True,
               [B("wri"), B("xcb")], [B("PC")])
        ACT(C, R_[:, 0:1024], C.PB[:], AF.Sigmoid, [B("PB"), B("sm")], [bR], bias=sm[:, 64 + n:65 + n])
        ACT(C, I_[:, 0:1024], C.PC[:], AF.Sigmoid, [B("PC"), B("sm")], [bI], bias=sm[:, 72 + n:73 + n])
        ACT(C, M_[:, 0:1024], R_[:, 0:1024], AF.Exp, [bR, B("smx")], [bM], scale=smx[:, 24 + n:25 + n])
        ACT(C, R_[:, 0:1024], R_[:, 0:1024], AF.Exp, [bR, B("smx")], [bR], scale=smx[:, 16 + n:17 + n])
        ACT(C, M_[:, 0:1024], M_[:, 0:1024], AF.Sqrt, [bM], [bM], scale=-1.0, bias=1.0)
        TT(C, I_[:, 0:1024], I_[:, 0:1024], XC[:, 0:1024], ALU.mult, [bI, bXC], [bI])
        TT(C, I_[:, 0:1024], I_[:, 0:1024], M_[:, 0:1024], ALU.mult, [bI, bM], [bI])
        P.emit("dve", lambda e: e.tensor_tensor_scan(out=M_[:, 0:1024], data0=R_[:, 0:1024], data1=I_[:, 0:1024],
                                                     initial=0.0, op0=ALU.mult, op1=ALU.add),
               reads=[bR, bI], writes=[bM])
        P.emit("dve", lambda e: e.tensor_tensor_scan(out=AC[:, 0:1024], data0=R_[:, 0:1024], data1=ZB,
                                                     initial=1.0, op0=ALU.mult, op1=ALU.add),
               reads=[bR, B("zb")], writes=[bAC])
        TT(C, G1[:, n, :], M_[:, 0:1024], ZS[:, 0:1024], ALU.mult, [bM, bZS], [B("G1")])
        TT(C, G2[:, n, :], AC[:, 0:1024], ZS[:, 0:1024], ALU.mult, [bAC, bZS], [B("G2")])
        CP(C, stvo[:, n:n + 1], M_[:, 1023:1024], [bM], [B("stvo")])
    if C.stop <= 2:
        return
    F_, KK, Pc, Rc, Q_ = tmps[0], tmps[1], tmps[2], tmps[3], tmps[4]
    bF, bKK, bPc, bRc, bQ = tb[0], tb[1], tb[2], tb[3], tb[4]
    for h in range(8):
        inproj_fm(C, w0, slot, C.PA, B("PA")); slot += 1
        ACT(C, F_[:, 0:1024], C.PA[:], AF.Sigmoid, [B("PA")], [bF])
        TS(C, F_[:, 0:1024], F_[:, 0:1024], smx[:, 8 + h:9 + h], smx[:, h:h + 1], ALU.mult, ALU.add, [bF, B("smx")], [bF])
        TS(C, KK[:, 0:1024], F_[:, 0:1024], -1.0, 1.0, ALU.mult, ALU.add, [bF], [bKK])
        for c in range(16):
            P.emit("dve", lambda e, c=c: e.tensor_tensor_scan(out=Pc[:, c * 64:(c + 1) * 64], data0=F_[:, c * 64:(c + 1) * 64],
                                                              data1=ZB[:, 0:64], initial=1.0, op0=ALU.mult, op1=ALU.add),
                   reads=[bF, B("zb")], writes=[bPc])
        P.emit("dve", lambda e: e.reciprocal(out=Rc[:, 0:1024], in_=Pc[:, 0:1024]), reads=[bPc], writes=[bRc])
        TT(C, kT[:, h, :], KK[:, 0:1024], Rc[:, 0:1024], ALU.mult, [bKK, bRc], [B("kT")])
        CP(C, plast[:, h, 1:17], Pc[:, 0:1024].rearrange("p (c s) -> p c s", s=64)[:, :, 63], [bPc], [B("plast")])
        plv = plast[:, h, 0:16].rearrange("p (j two) -> p j two", two=2)
        TT(C, PP[:, h, :], plv[:, :, 0], plv[:, :, 1], ALU.mult, [B("plast")], [B("PP")])
        inproj_fm(C, w0, slot, C.PB, B("PB")); slot += 1
        ACT(C, Q_[:, 0:1024], C.PB[:], AF.Silu, [B("PB")], [bQ])
        TT(C, qT[:, h, :], Q_[:, 0:1024], Pc[:, 0:1024], ALU.mult, [bQ, bPc], [B("qT")])
        inproj_fm(C, w0, slot, C.PC, B("PC")); slot += 1
        ACT(C, zaT[:, h, :], C.PC[:], AF.Silu, [B("PC")], [B("zaT")])
    if C.stop <= 3:
        return
    for n in range(8):
        STT(C, G1[:, n, :], G2[:, n, :], stv[:, n:n + 1], G1[:, n, :], ALU.mult, ALU.add, [B("G2"), B("G1"), B("stv")], [B("G1")])
    alias_bufs([B("Vt")], [B("G2")])
    alias_bufs([B("PAh0"), B("PAh1"), B("PBh0"), B("PBh1")], [B("PA"), B("PB")])
    for g in range(2):
        def cons(j, ps, pb, g=g):
            ACT(C, Vt[:, j, g * 512:(g + 1) * 512], ps, AF.Copy, [pb], [B("Vt")])
        tm_group(C, w0, slot, 4, lambda kc, j: C.big[:, kc, j * 128:(j + 1) * 128], lambda kc, j: [C.bigb[kc]], cons)
        slot += 4
    if C.stop <= 4:
        return
    for n in range(8):
        CP(C, C.big[:, 8 + n, :], G1[:, n, :], [B("G1")], [C.bigb[8 + n]], eng="pool")
    alias_bufs([B("PCh0"), B("PCh1")], [B("PC")])
    alias_bufs([B("sqf0"), B("sqf1"), B("oc0"), B("oc1"), B("sqb0"), B("sqb1")], tb)
    CP(C, ones_bf[:], C.consts[:, 512:640], [B("consts")], [B("ones_bf")])
    U = C.stf[:, 0:1024].rearrange("p (h e) -> p h e", e=128)
    Sb = C.stb[:, 0:1024].rearrange("p (h e) -> p h e", e=128)
    Sh = C.stb[:, 1024:2048].rearrange("p (h e) -> p h e", e=128)
    caus = C.consts[:, 768:896]
    ident = C.consts[:, 0:128]
    maskA = C.consts[:, 128:256]
    ones128 = C.consts[:, 512:640]
    for h in range(8):
        ACT(C, Sb[:, h, :], U[:, h, :], AF.Copy, [B("stf")], [B(f"Sb{h}")])
        ACT(C, Sh[:, h, :], U[:, h, :], AF.Copy, [B("stf"), B("PP")], [B(f"Sh{h}")], scale=PP[:, h, 0:1])
    def FM(j, h, r):
        p_att = C.PC[:, r * 512 + 256:r * 512 + 384]
        b_att = B(f"PCh{r}")
        KVP, kvn = [(C.PB, "PB"), (C.PA, "PA")][r]
        p_kv0, p_kv1 = KVP[:, 0:128], KVP[:, 512:640]
        b_kv0, b_kv1 = B(kvn + "h0"), B(kvn + "h1")
        p_o = C.PC[:, r * 512:r * 512 + 128]
        b_o = B(f"PCh{r}")
        tk = slice(j * 128, (j + 1) * 128)
        ptk = C.PT[:, r * 1024:r * 1024 + 128]
        c0 = 2 * j
        t0_, t1_ = slice(j * 128, j * 128 + 64), slice(j * 128 + 64, (j + 1) * 128)
        TR(C, ptk, kT[:, h, tk], [B("kT")], [B(f"PT{r}")])
        MM(C, p_att, kT[:, h, tk], qT[:, h, tk], True, True, [B("kT"), B("qT")], [b_att])
        ACT(C, khat[r][:], kT[:, h, t0_], AF.Copy, [B("kT"), B("plast")], [B(f"khat{r}")], scale=plast[:, h, c0 + 1:c0 + 2])
        MM(C, p_att[0:64, 64:128], khat[r][:], qT[:, h, t1_], True, True, [B(f"khat{r}"), B("qT")], [b_att])
        ACT(C, ktok[r][:], ptk, AF.Copy, [B(f"PT{r}")], [B(f"ktok{r}")])
        TT(C, attm[r][:], p_att, caus, ALU.mult, [b_att, B("consts")], [B(f"attm{r}")])
        MM(C, p_kv0, ktok[r][0:64, :], Vt[0:64, j, h * 128:(h + 1) * 128], True, True, [B(f"ktok{r}"), B("Vt")], [b_kv0])
        MM(C, p_kv1, ktok[r][64:128, :], Vt[64:128, j, h * 128:(h + 1) * 128], True, True, [B(f"ktok{r}"), B("Vt")], [b_kv1])
        MM(C, p_o[:, 0:64], Sb[:, h, :], qT[:, h, t0_], True, False, [B(f"Sb{h}"), B("qT")], [b_o])
        MM(C, p_o[:, 64:128], Sh[:, h, :], qT[:, h, t1_], False, False, [B(f"Sh{h}"), B("qT")], [b_o])
        MM(C, p_o, Vt[:, j, h * 128:(h + 1) * 128], attm[r][:], False, True, [B("Vt"), B(f"attm{r}")], [b_o])
        ACT(C, oc[r][:], p_o, AF.Copy, [b_o], [B(f"oc{r}")])
        ACT(C, sqb[r][:], oc[r][:], AF.Square, [B(f"oc{r}")], [B(f"sqb{r}")])
        p_ms = KVP[:, 128:256]
        STT(C, U[:, h, :], U[:, h, :], plast[:, h, c0:c0 + 1], p_kv0, ALU.mult, ALU.add, [B(f"U{h}"), B("stf"), B("plast"), b_kv0], [B(f"U{h}")])
        STT(C, U[:, h, :], U[:, h, :], plast[:, h, c0 + 1:c0 + 2], p_kv1, ALU.mult, ALU.add, [B(f"U{h}"), B("plast"), b_kv1], [B(f"U{h}")])
        MM(C, p_ms, ones_bf[:], sqb[r][:], True, True, [B("ones_bf"), B(f"sqb{r}")], [b_kv0])
        TS(C, Sb[:, h, :], U[:, h, :], plast[:, h, c0 + 2:c0 + 3], None, ALU.mult, ALU.bypass, [B(f"U{h}"), B("plast")], [B(f"Sb{h}")])
        if j + 1 < NT:
            TS(C, Sh[:, h, :], U[:, h, :], PP[:, h, j + 1:j + 2], None, ALU.mult, ALU.bypass, [B(f"U{h}"), B("PP")], [B(f"Sh{h}")])

    def KK_(j, h, r):
        KVP, kvn = [(C.PB, "PB"), (C.PA, "PA")][r]
        p_ms = KVP[:, 128:256]
        b_ms = B(kvn + "h0")
        tk = slice(j * 128, (j + 1) * 128)
        ACT(C, sqf[r][:], p_ms, AF.Ln, [b_ms], [B(f"sqf{r}")], bias=EPS)
        ACT(C, sqf[r][:], sqf[r][:], AF.Exp, [B(f"sqf{r}")], [B(f"sqf{r}")], scale=-0.5)
        STT(C, sqf[r][:], oc[r][:], sm[:, 16 + h:17 + h], sqf[r][:], ALU.mult, ALU.mult, [B(f"oc{r}"), B("sm"), B(f"sqf{r}")], [B(f"sqf{r}")])
        TT(C, C.big[:, h, tk], sqf[r][:], zaT[:, h, tk], ALU.mult, [B(f"sqf{r}"), B("zaT")], [C.bigb[h]], eng="pool")

    its = [(j, h) for j in range(NT) for h in range(8)]
    FM(its[0][0], its[0][1], 0)
    for i in range(len(its)):
        if i + 1 < len(its):
            FM(its[i + 1][0], its[i + 1][1], (i + 1) % 2)
        KK_(its[i][0], its[i][1], i % 2)
    for h in range(8):
        ACT(C, C.stf[:, h * 128:(h + 1) * 128], U[:, h, :], AF.Copy, [B(f"U{h}"), B("plast")], [B(f"U{h}")], scale=plast[:, h, 16:17])
    if stS_o is not None:
        P.dma("sp", "stS_o", stS_o, C.stf[:, 0:1024], reads=[B(f"U{h}") for h in range(8)], writes=[B(nm["stS_o"])])
        P.dma("sp", "stv_o", stv_o, stvo[:], reads=[B("stvo")], writes=[B(nm["stv_o"])])
    if C.stop <= 5:
        return
    stage4(C, w0, slot, hin, p0, rpl, g_ple, hout, mix_bufs + [B("xcb"), B("zb"), B("sqf0"), B("sqf1"), B("oc0"), B("oc1"), B("sqb0"), B("sqb1")], HP, tmps, tb, None, nm)


def to_T(C, src_bf, src_buf, j, dstT, dst_bufs):
    B = C.B
    for half in range(2):
        ptv = C.PT[:, half * 1024:(half + 1) * 1024]
        pb = B(f"PT{half}")
        for k in range(8):
            kc = half * 8 + k
            TR(C, ptv[:, k * 128:(k + 1) * 128], src_bf[:, kc * 128:(kc + 1) * 128], [src_buf], [pb])
        dst = dstT[:, half * 8:(half + 1) * 8, j * 128:(j + 1) * 128]
        srcv = ptv.rearrange("p (a b) -> p a b", b=128)
        if half == 0:
            ACT(C, dst, srcv, AF.Copy, [pb], dst_bufs[half * 8:(half + 1) * 8])
        else:
            CP(C, dst, srcv, [pb], dst_bufs[half * 8:(half + 1) * 8])


def stage4(C, wd, slot, hin, p_d, rpl, g_ple, hout, old_bufs, HP, tmps, tb, final_gain, names):
    P, B = C.P, C.B
    A = C.arena
    pT_buf = B("pT")
    pT = A[:, 55408:57456].rearrange("p (a b) -> p a b", b=T)
    hpb = [B(f"HP{j}") for j in range(NT)]
    hb2 = [A[:, 32768 + 2048 * i:32768 + 2048 * (i + 1)] for i in range(4)]
    hb2b = [B(f"hb2_{i}") for i in range(4)]
    alias_bufs(hpb + hb2b + tb + [pT_buf], old_bufs)
    alias_bufs([B("PAh0"), B("PAh1"), B("PBh0"), B("PBh1"), B("PCh0"), B("PCh1"), B("PT0"), B("PT1")],
               [B(n) for n in ("PA", "PB", "PC", "PT0", "PT1", "PAh0", "PAh1", "PBh0", "PBh1", "PCh0", "PCh1")])
    st = [tmps[0][:, 0:512], tmps[1][:, 0:512]]
    for j in range(NT):
        i = j % 2
        pst = [tmps[5], tmps[6]][i]
        P.dma("sp", f"hs{i}", pst[:, 0:256], p_d[j * 128:(j + 1) * 128, :], writes=[tb[5 + i]])
        CP(C, hb2[i][:, 0:256], pst[:, 0:256], [tb[5 + i]], [hb2b[i]])
        for kc in range(2):
            TR(C, C.PT[:, kc * 128:(kc + 1) * 128], hb2[i][:, kc * 128:(kc + 1) * 128], [hb2b[i]], [B("PT0")])
        CP(C, pT[:, :, j * 128:(j + 1) * 128], C.PT[:, 0:256].rearrange("p (a b) -> p a b", b=128), [B("PT0")], [pT_buf])
    cnt = [0]
    for g in range(4):
        def cons(j, ps, pb, g=g):
            i = cnt[0] % 2
            cnt[0] += 1
            P.dma("sp", f"hs{i}", st[i], hin[j * 128:(j + 1) * 128, g * 512:(g + 1) * 512], reads=[B(names["hin"])], writes=[tb[i]])
            TT(C, HP[:, j, g * 512:(g + 1) * 512], ps, st[i], ALU.add, [pb, tb[i]], [hpb[j]])
        tm_group(C, wd, slot, 4, lambda kc, j: C.big[:, kc, j * 128:(j + 1) * 128], lambda kc, j: [C.bigb[kc]], cons)
        slot += 4
    if C.stop <= 6:
        return slot
    plw = [load_w(C, wd, slot + i) for i in range(4)]
    slot += 4
    sm = C.small
    junk = tmps[2]
    for j in range(NT):
        i = j % 4
        if j % 2 == 0:
            ACT(C, hb2[i], HP[:, j, :], AF.Copy, [hpb[j]], [hb2b[i]])
        else:
            CP(C, hb2[i], HP[:, j, :], [hpb[j]], [hb2b[i]])
        for g in range(4):
            PS, nm = [(C.PA, "PA"), (C.PB, "PB")][g // 2]
            ps = PS[:, (g % 2) * 512:(g % 2) * 512 + 512]
            pb = B(f"{nm}h{g % 2}")
            wt, wb = plw[g]
            for kc in range(2):
                MM(C, ps, pT[:, kc, j * 128:(j + 1) * 128], wt[:, kc * 512:(kc + 1) * 512], kc == 0, kc == 1, [wb, pT_buf], [pb])
        to_T(C, hb2[i], hb2b[i], j, C.big, C.bigb)
        q = 8 + 4 * (j % 2)
        ACT(C, junk[:, 0:1024], C.PA[:], AF.Square, [B("PAh0"), B("PAh1")], [tb[2], B(f"sm_q0{j % 2}")], accum_out=sm[:, q:q + 1])
        ACT(C, junk[:, 0:1024], C.PB[:], AF.Square, [B("PBh0"), B("PBh1")], [tb[2], B(f"sm_q1{j % 2}")], accum_out=sm[:, q + 1:q + 2])
        TT(C, sm[:, q + 2:q + 3], sm[:, q:q + 1], sm[:, q + 1:q + 2], ALU.add, [B(f"sm_q0{j % 2}"), B(f"sm_q1{j % 2}")], [B(f"sm_q2{j % 2}")])
        ACT(C, sm[:, q + 3:q + 4], sm[:, q + 2:q + 3], AF.Sqrt, [B(f"sm_q2{j % 2}")], [B(f"sm_q3{j % 2}")], scale=1.0 / D, bias=EPS)
        P.emit("dve", lambda e, j=j, q=q: e.reciprocal(out=rpl[:, j:j + 1], in_=sm[:, q + 3:q + 4]), reads=[B(f"sm_q3{j % 2}")], writes=[B("rpl")])
    if C.stop <= 8:
        return slot
    P.dma("sp", "g", C.gain[:], g_ple.partition_broadcast(128), writes=[B("gain")])
    SG, PL = tmps[3], tmps[4]
    bSG, bPL = tb[3], tb[4]
    for g in range(4):
        pw, pwb = None, None

        def cons(j, ps, pb, g=g):
            ps2 = C.PC[:, (j % 2) * 512:(j % 2) * 512 + 512]
            pb2 = B(f"PCh{j % 2}")
            for kc in range(2):
                MM(C, ps2, pT[:, kc, j * 128:(j + 1) * 128], cons.pw[:, kc * 512:(kc + 1) * 512], kc == 0, kc == 1, [cons.pwb, pT_buf], [pb2])
            ACT(C, SG[:, 0:512], ps, AF.Sigmoid, [pb], [bSG])
            STT(C, PL[:, 0:512], ps2, rpl[:, j:j + 1], C.gain[:, g * 512:(g + 1) * 512], ALU.mult, ALU.mult, [pb2, B("rpl"), B("gain")], [bPL])
            TT(C, PL[:, 0:512], PL[:, 0:512], SG[:, 0:512], ALU.mult, [bPL, bSG], [bPL])
            TT(C, HP[:, j, g * 512:(g + 1) * 512], HP[:, j, g * 512:(g + 1) * 512], PL[:, 0:512], ALU.add, [hpb[j], bPL], [hpb[j]])
        cons.pw, cons.pwb = load_w(C, wd, slot + 4)
        tm_group(C, wd, slot, 4, lambda kc, j: C.big[:, kc, j * 128:(j + 1) * 128], lambda kc, j: [C.bigb[kc]], cons)
        slot += 5
    if C.stop <= 9:
        return slot
    if final_gain is not None:
        P.dma("sp", "g", C.gain[:], final_gain.partition_broadcast(128), writes=[B("gain")])
    for j in range(NT):
        i = j % 2
        if final_gain is None:
            for q in range(4):
                P.dma("sp", f"hout{i}", hout[j * 128:(j + 1) * 128, q * 512:(q + 1) * 512], HP[:, j, q * 512:(q + 1) * 512], reads=[hpb[j]], writes=[B(names["hout"])])
        else:
            ot = A[:, 32768 + i * 4096:32768 + (i + 1) * 4096].bitcast(F32)
            ob = B(f"ot{i}")
            if j < 2:
                alias_bufs([ob], hb2b)
            sm2 = C.small
            ACT(C, ot, HP[:, j, :], AF.Square, [hpb[j]], [ob, B("sm_ssq")], accum_out=sm2[:, 0:1])
            ACT(C, sm2[:, 1:2], sm2[:, 0:1], AF.Sqrt, [B("sm_ssq")], [B("sm_sd")], scale=1.0 / D, bias=EPS)
            P.emit("dve", lambda e: e.reciprocal(out=sm2[:, 2:3], in_=sm2[:, 1:2]), reads=[B("sm_sd")], writes=[B("sm_rstd")])
            STT(C, ot, HP[:, j, :], sm2[:, 2:3], C.gain[:], ALU.mult, ALU.mult, [hpb[j], B("sm_rstd"), B("gain")], [ob])
            for q in range(4):
                P.dma("sp", f"hout{i}", hout[j * 128:(j + 1) * 128, q * 512:(q + 1) * 512], ot[:, q * 512:(q + 1) * 512], reads=[ob], writes=[B(names["hout"])])
    return slot


def _pp(v, n):
    return np.ascontiguousarray(np.asarray(v, np.float32).reshape(n, 128).T)


def prep_l0(inp):
    sm = np.zeros((128, 96), np.float32)
    sm[:, 0:8] = _pp(inp["a_lb_logits"][0], 8)
    sm[:, 8:16] = _pp(inp["a_lb_logits"][1], 8)
    sm[:, 16:24] = _pp(inp["a_norm"][0], 8)
    sm[:, 24:56] = np.asarray(inp["b_conv_w"][0], np.float32).reshape(4, 8, 128).transpose(2, 1, 0).reshape(128, 32)
    sm[:, 56:64] = _pp(inp["b_conv_b"][0], 8)
    sm[:, 64:72] = _pp(inp["b_b_r"][0], 8)
    sm[:, 72:80] = _pp(inp["b_b_i"][0], 8)
    sm[:, 80:88] = _pp(inp["b_lambda"][0], 8)
    wri = np.concatenate([np.asarray(inp["b_w_r"][0], np.float32).transpose(1, 0, 2),
                          np.asarray(inp["b_w_i"][0], np.float32).transpose(1, 0, 2)], axis=1)
    return {
        "w0": pack_l0(np.asarray(inp["e_w_in"][0], np.float32), np.asarray(inp["e_w_out"][0], np.float32),
                      np.asarray(inp["ple_gate_w"][0], np.float32), np.asarray(inp["ple_w"][0], np.float32)),
        "wri": np.ascontiguousarray(wri), "sm0": sm,
        "g_e": np.asarray(inp["e_norm"][0], np.float32), "g_ple0": np.asarray(inp["ple_norm"][0], np.float32),
        "consts_d": make_consts(),
    }


def pack_l1(o_w_in, o_w_out, gate_w, ple_w):
    slots = []
    for g in range(4):
        slots += _tm_slots(o_w_in, 2048 + 512 * g)
    for g in range(4):
        slots += _tm_slots(o_w_in, 4096 + 512 * g)
        slots += _tm_slots(o_w_in, 6144 + 512 * g)
    for h in range(8):
        slots.append(_fm_slot(o_w_in, 128 * h))
        slots.append(_fm_slot(o_w_in, 1024 + 128 * h))
    slots += pack_tail(o_w_out, gate_w, ple_w)
    return np.stack(slots).astype(np.float32)


L1_NSLOT = 16 + 32 + 16 + 40


def prep_l1(inp):
    sm = np.zeros((128, 96), np.float32)
    sm[:, 0:64] = np.asarray(inp["c_conv_w"][0], np.float32).reshape(4, 16, 128).transpose(2, 1, 0).reshape(128, 64)
    sm[:, 64:80] = _pp(inp["c_conv_b"][0], 16)
    sm[:, 80:96] = _pp(inp["c_norm"][0], 16)
    wg = np.asarray(inp["o_w_in"][0][:, 8192:8208], np.float32).reshape(16, 128, 16).transpose(1, 0, 2)
    gb = np.concatenate([np.asarray(inp["c_b_i"][0], np.float32), np.asarray(inp["c_b_f"][0], np.float32)])
    return {
        "w1": pack_l1(np.asarray(inp["o_w_in"][0], np.float32), np.asarray(inp["o_w_out"][0], np.float32),
                      np.asarray(inp["ple_gate_w"][1], np.float32), np.asarray(inp["ple_w"][1], np.float32)),
        "wg": np.ascontiguousarray(wg), "sm1": sm, "gb": gb,
        "g_o": np.asarray(inp["o_norm"][0], np.float32), "g_ple1": np.asarray(inp["ple_norm"][1], np.float32),
        "g_fin": np.asarray(inp["final_norm"], np.float32),
        "consts_d": make_consts(),
    }


def layer1(C, seg="U"):
    P, B, nc = C.P, C.B, C.nc
    dram, sb = C.dram, C.sb
    w1 = dram("w1", [L1_NSLOT, 128, 2048])
    wg_d = dram("wg", [128, 16, 16])
    sm_d = dram("sm1", [128, 96])
    gb_d = dram("gb", [16])
    g_o = dram("g_o", [D])
    g_ple = dram("g_ple1", [D])
    g_fin = dram("g_fin", [D])
    states_only = (seg == "A")
    if seg == "U":
        hin, p1 = dram("hin1", [T, D]), dram("p1", [T, 256])
        stC_d, stv_d = dram("stC", [128, 8 * 260]), dram("stv1", [128, 48])
        hout = dram("hout1", [T, D], kind="ExternalOutput")
        stC_o = dram("stC_o", [128, 8 * 260], kind="ExternalOutput")
        stv_o = dram("stv1_o", [128, 48], kind="ExternalOutput")
        nm = {"hin": "d_hin1", "hout": "d_hout1", "stC": "d_stC", "stv": "d_stv1", "stC_o": "d_stC_o", "stv_o": "d_stv1_o"}
        C.out_keys += ["hout0", "hout1", "stC_o", "stv1_o"]
    elif seg == "A":
        hin, p1 = dram("h2A", [T, D], kind="Internal"), dram("p1A", [T, 256])
        stC_d, stv_d = dram("zS", [128, 2080]), dram("zv", [128, 48])
        hout = None
        stC_o = dram("sC1", [128, 8 * 260], kind="Internal")
        stv_o = dram("sv1", [128, 48], kind="Internal")
        nm = {"hin": "d_h2A", "hout": "d_none", "stC": "d_zS", "stv": "d_zv", "stC_o": "d_sC1", "stv_o": "d_sv1"}
    else:
        hin, p1 = dram("h2B", [T, D], kind="Internal"), dram("p1B", [T, 256])
        stC_d, stv_d = dram("sC1", [128, 8 * 260], kind="Internal"), dram("sv1", [128, 48], kind="Internal")
        hout = dram("out", [T, D], kind="ExternalOutput")
        stC_o = stv_o = None
        nm = {"hin": "d_h2B", "hout": "d_out", "stC": "d_sC1", "stv": "d_sv1"}
        C.out_keys += ["hout0", "hout1"]

    A = C.arena

    def abf(off, a, b):
        return A[:, off:off + a * b].rearrange("p (a b) -> p a b", b=b)
    qT = abf(0, 8, 1024)
    kT = abf(8192, 8, 1024)
    VX = abf(16384, 64, 260)
    GT = abf(33024, 8, 2048)
    TB1 = 49408
    SGO = A[:, TB1:TB1 + 8192].bitcast(F32).rearrange("p (a b) -> p a b", b=512)
    X = A[:, TB1:TB1 + 2064].bitcast(F32)
    XC = A[:, TB1 + 2064:TB1 + 4128].bitcast(F32)
    TB = 40960

    def tmpf(i, w=1032):
        return A[:, TB + i * 2064:TB + (i + 1) * 2064].bitcast(F32)[:, 0:w]
    tmps = [tmpf(i) for i in range(7)]
    tb = [B(f"tmp{i}") for i in range(7)]
    htile = [A[:, 4096 * i:4096 * (i + 1)].bitcast(F32) for i in range(4)]
    hbf = [A[:, 16384 + 2048 * i:16384 + 2048 * (i + 1)] for i in range(4)]
    C.htile, C.hbf = htile, hbf
    HP = A[:, 0:32768].bitcast(F32).rearrange("p (a b) -> p a b", b=2048)

    wg = sb("wg_sb", [128, 16, 16], BF16)
    sm = sb("sm1_sb", [128, 96], F32)
    gbb = sb("gbb", [128, 16], F32)
    stv = sb("stv1_sb", [128, 48], F32)
    stvo = sb("stvo1_sb", [128, 48], F32)
    EK = sb("EK", [128, 8, 8], F32)
    EKC = sb("EKC", [128, 8, 8], F32)
    THR = sb("THR", [128, 8, 8], F32)
    DEC = sb("DEC", [128, 8, 16], F32)
    g8 = sb("g8", [128, 4, 8], F32)
    r8 = sb("r8", [128, 2, 8], F32)
    ycb = A[:, TB1:TB1 + 2048]
    jk = A[:, TB1 + 2048:TB1 + 2304]
    ktok = [A[:, TB1 + 2304 + 128 * i:TB1 + 2432 + 128 * i] for i in range(2)]
    scw = [A[:, TB1 + 2560 + 128 * i:TB1 + 2688 + 128 * i] for i in range(2)]
    ndsb = [A[:, TB1 + 2816 + 520 * i:TB1 + 2816 + 520 * i + 516].bitcast(F32) for i in range(2)]
    rpl = sb("rpl1", [128, 16], F32)
    Cf = C.stf[:, :].rearrange("p (h e) -> p h e", e=260)
    Cb = C.stb[:, :].rearrange("p (h e) -> p h e", e=260)
    Chs = sb("Chs", [128, 8 * 260], BF16)
    Ch = Chs[:, :].rearrange("p (h e) -> p h e", e=260)

    P.dma("pool", "wri", wg[:], wg_d, writes=[B("wg")])
    P.dma("sp", "sm", sm[:], sm_d, writes=[B("sm1")])
    P.dma("sp", "sm", gbb[:], gb_d.partition_broadcast(128), writes=[B("gbb")])
    P.dma("sp", "stv", stv[:], stv_d, reads=[B(nm["stv"])], writes=[B("stv1")])
    for q in range(4):
        P.dma("sp", "stS", C.stf[:, q * 520:(q + 1) * 520], stC_d[:, q * 520:(q + 1) * 520], reads=[B(nm["stC"])], writes=[B("stf")])
    TS(C, stv[:], stv[:], C.msk[:, 0:1], None, ALU.mult, ALU.bypass, [B("stv1"), B("msk")], [B("stv1")])
    TS(C, C.stf[:], C.stf[:], C.msk[:, 0:1], None, ALU.mult, ALU.bypass, [B("stf"), B("msk")], [B("stf")])
    P.dma("sp", "g", C.gain[:], g_o.partition_broadcast(128), writes=[B("gain")])
    for j in range(NT):
        hb = j % 4
        P.dma("sp", f"h{hb}", htile[hb], hin[j * 128:(j + 1) * 128, :], reads=[B(nm["hin"])], writes=[B(f"htile{hb}")])
        rms_to_T(C, htile[hb], B(f"htile{hb}"), j, B("gain"), C.big, C.bigb, hb)
    if C.stop <= 1:
        return
    mix_bufs = [B(n) for n in ("qT1", "kT1", "VX", "GT", "SGO", "X1", "XC1")]
    alias_bufs(mix_bufs, [B(f"htile{i}") for i in range(4)] + [B(f"hbf{i}") for i in range(4)])
    P.emit("pool", lambda e: e.memset(VX[:, :, 256:260], 0.0), writes=[B("VX")])
    P.emit("pool", lambda e: e.memset(VX[:, :, 256:257], 1.0), writes=[B("VX")])
    maskA = C.consts[:, 128:256]
    H0, H1 = C.consts[:, 256:384], C.consts[:, 384:512]
    SAME = C.consts[:, 640:768]
    LN_S = float(np.log(np.sqrt(128.0)))
    for j in range(NT):
        tk = slice(j * 128, (j + 1) * 128)
        pg = C.PC[:, 0:16]
        for kc in range(KC):
            MM(C, pg, C.big[:, kc, tk], wg[:, kc, :], kc == 0, kc == KC - 1, [B("wg"), C.bigb[kc]], [B("PCh0")])
        li, nlf, a1, t1 = g8[:, 0, :], g8[:, 1, :], g8[:, 2, :], g8[:, 3, :]
        TT(C, t1, pg[:, 8:16], gbb[:, 8:16], ALU.add, [B("PCh0"), B("gbb")], [B("g8t")])
        TT(C, li, pg[:, 0:8], gbb[:, 0:8], ALU.add, [B("PCh0"), B("gbb")], [B("g8l")])
        ACT(C, t1, t1, AF.Exp, [B("g8t")], [B("g8t")], scale=-1.0)
        ACT(C, nlf, t1, AF.Ln, [B("g8t")], [B("g8n")], bias=1.0)
        pq = C.PC[:, 512:544]
        MM(C, pq[:, 0:8], maskA, nlf, True, True, [B("consts"), B("g8n")], [B("PCh1")])
        MM(C, pq[:, 8:16], SAME, nlf, True, True, [B("consts"), B("g8n")], [B("PCh1")])
        MM(C, pq[:, 16:24], H0, nlf, True, True, [B("consts"), B("g8n")], [B("PCh1")])
        MM(C, pq[:, 24:32], H1, nlf, True, True, [B("consts"), B("g8n")], [B("PCh1")])
        TT(C, a1, li, pq[:, 0:8], ALU.add, [B("g8l"), B("PCh1")], [B("g8a")])
        ACT(C, EK[:, j, :], a1, AF.Exp, [B("g8a")], [B("EK")])
        TT(C, a1, a1, pq[:, 8:16], ALU.subtract, [B("g8a"), B("PCh1")], [B("g8a")])
        ACT(C, EKC[:, j, :], a1, AF.Exp, [B("g8a")], [B("EKC")])
        ACT(C, THR[:, j, :], pq[:, 0:8], AF.Exp, [B("PCh1")], [B("THR")], bias=LN_S)
        ACT(C, DEC[:, j, :], pq[:, 16:32], AF.Exp, [B("PCh1")], [B("DEC")], scale=-1.0)
    if C.stop <= 2:
        return
    alias_bufs([B("PAh0"), B("PAh1"), B("PBh0"), B("PBh1")], [B("PA"), B("PB")])
    slot = 0
    for g in range(4):
        def cons(j, ps, pb, g=g):
            ACT(C, VX[:, j * 8 + 2 * g:j * 8 + 2 * g + 2, 0:256], ps.rearrange("p (a b) -> p a b", b=256), AF.Copy, [pb], [B("VX")])
        tm_group(C, w1, slot, 4, lambda kc, j: C.big[:, kc, j * 128:(j + 1) * 128], lambda kc, j: [C.bigb[kc]], cons)
        slot += 4
    if C.stop <= 3:
        return
    for g in range(4):
        if states_only:
            slot = 48
            break

        def cons_o(j, ps, pb):
            ACT(C, SGO[:, j, :], ps, AF.Sigmoid, [pb], [B("SGO")])
        tm_group(C, w1, slot, 4, lambda kc, j: C.big[:, kc, j * 128:(j + 1) * 128], lambda kc, j: [C.bigb[kc]], cons_o)
        slot += 4

        def cons_z(j, ps, pb, g=g):
            ACT(C, ps, ps, AF.Silu, [pb], [pb])
            TT(C, GT[:, j, g * 512:(g + 1) * 512], ps, SGO[:, j, :], ALU.mult, [pb, B("SGO")], [B("GT")])
        tm_group(C, w1, slot, 4, lambda kc, j: C.big[:, kc, j * 128:(j + 1) * 128], lambda kc, j: [C.bigb[kc]], cons_z)
        slot += 4
    if C.stop <= 4:
        return
    alias_bufs([B("X1"), B("XC1")], [B("SGO")])
    alias_bufs([B("PA"), B("PB")], [B("PAh0"), B("PAh1"), B("PBh0"), B("PBh1")])
    for h in range(8):
        for qk in range(2):
            i = qk * 8 + h
            PS, psb = [(C.PA, B("PA")), (C.PB, B("PB"))][qk]
            inproj_fm(C, w1, slot, PS, psb); slot += 1
            ACT(C, X[:, 3:1027], PS[:], AF.Copy, [psb], [B("X1")])
            CP(C, X[:, 0:3], stv[:, 3 * i:3 * i + 3], [B("stv1")], [B("X1")])
            CP(C, stvo[:, 3 * i:3 * i + 3], X[:, 1024:1027], [B("X1")], [B("stvo1")])
            cw = lambda k: sm[:, 4 * i + k:4 * i + k + 1]
            TS(C, XC[:, 0:1024], X[:, 3:1027], cw(3), sm[:, 64 + i:65 + i], ALU.mult, ALU.add, [B("X1"), B("sm1")], [B("XC1")])
            for k in (2, 1, 0):
                STT(C, XC[:, 0:1024], X[:, k:k + 1024], cw(k), XC[:, 0:1024], ALU.mult, ALU.add, [B("X1"), B("XC1"), B("sm1")], [B("XC1")])
            dst, dbuf = (qT, B("qT1")) if qk == 0 else (kT, B("kT1"))
            ACT(C, dst[:, h, :], XC[:, 0:1024], AF.Silu, [B("XC1")], [dbuf])
    alias_bufs([B("PAh0"), B("PAh1"), B("PBh0"), B("PBh1"), B("PCh0"), B("PCh1")], [B("PA"), B("PB"), B("PC")])
    if C.stop <= 5:
        return
    for h in range(8):
        ACT(C, Cb[:, h, 0:258], Cf[:, h, 0:258], AF.Copy, [B("stf")], [B(f"Cb{h}")])
        if not states_only:
            ACT(C, Ch[:, h, 0:258], Cf[:, h, 0:258], AF.Copy, [B("stf"), B("DEC")], [B(f"Ch{h}")], scale=DEC[:, 0, h:h + 1])
    def FM(j, h, r):
        tk = slice(j * 128, (j + 1) * 128)
        p_sc = C.PC[:, r * 512 + 384:r * 512 + 512]
        b_sc = B(f"PCh{r}")
        if r == 1:
            p_kv = [C.PA[:, 0:258], C.PA[:, 512:770]]
            b_kv = [B("PAh0"), B("PAh1")]
        else:
            p_kv = [C.PB[:, 0:258], C.PB[:, 512:770]]
            b_kv = [B("PBh0"), B("PBh1")]
        p_nd = C.PC[:, r * 512:r * 512 + 258]
        b_nd = B(f"PCh{r}")
        jh = j * 8 + h
        ptk = C.PT[:, r * 1024:r * 1024 + 128]
        TR(C, ptk, kT[:, h, tk], [B("kT1")], [B(f"PT{r}")])
        if not states_only:
            MM(C, p_sc, kT[:, h, tk], qT[:, h, tk], True, True, [B("kT1"), B("qT1")], [b_sc])
        ACT(C, ktok[r][:], ptk, AF.Copy, [B(f"PT{r}"), B("EKC")], [B(f"ktk{r}")], scale=EKC[:, j, h:h + 1])
        if not states_only:
            STT(C, scw[r][:], p_sc, EK[:, j, h:h + 1], maskA, ALU.mult, ALU.mult, [b_sc, B("EK"), B("consts")], [B(f"scw{r}")])
            ACT(C, scw[r][0:64, 64:128], p_sc[0:64, 64:128], AF.Copy, [b_sc, B("EKC")], [B(f"scw{r}")], scale=EKC[0:64, j, h:h + 1])
        for c in range(2):
            rows = slice(64 * c, 64 * c + 64)
            MM(C, p_kv[c], ktok[r][rows, :], VX[rows, jh, 0:258], True, True, [B(f"ktk{r}"), B("VX")], [b_kv[c]])
        if not states_only:
            MM(C, p_nd[0:64, :], qT[:, h, j * 128:j * 128 + 64], Cb[:, h, 0:258], True, False, [B("qT1"), B(f"Cb{h}")], [b_nd])
            MM(C, p_nd[64:128, :], qT[:, h, j * 128 + 64:(j + 1) * 128], Ch[:, h, 0:258], True, False, [B("qT1"), B(f"Ch{h}")], [b_nd])
            MM(C, p_nd, scw[r][:], VX[:, jh, 0:258], False, True, [B(f"scw{r}"), B("VX")], [b_nd])
            ACT(C, ndsb[r][:], p_nd, AF.Copy, [b_nd], [B(f"ndsb{r}")])
        for c in range(2):
            STT(C, Cf[:, h, 0:258], Cf[:, h, 0:258], DEC[:, j, 8 * c + h:8 * c + h + 1], p_kv[c], ALU.mult, ALU.add,
                [B(f"Cf{h}"), B("stf"), B("DEC"), b_kv[c]], [B(f"Cf{h}")])
        if not states_only:
            ACT(C, Cb[:, h, 0:258], Cf[:, h, 0:258], AF.Copy, [B(f"Cf{h}")], [B(f"Cb{h}")])
            if j + 1 < NT:
                TS(C, Ch[:, h, 0:258], Cf[:, h, 0:258], DEC[:, j + 1, h:h + 1], None, ALU.mult, ALU.bypass, [B(f"Cf{h}"), B("DEC")], [B(f"Ch{h}")])

    def KK_(j, h, r):
        nd = ndsb[r]
        b_nd = B(f"ndsb{r}")
        s_ = r8[:, r, :]
        sbn = B(f"r8_{r}")
        ACT(C, s_[:, 0:1], nd[:, 256:257], AF.Abs, [b_nd], [sbn])
        TT(C, s_[:, 0:1], s_[:, 0:1], THR[:, j, h:h + 1], ALU.max, [sbn, B("THR")], [sbn])
        P.emit("dve", lambda e, s_=s_: e.reciprocal(out=s_[:, 1:2], in_=s_[:, 0:1]), reads=[sbn], writes=[sbn])
        ACT(C, jk[:], nd[:, 0:256], AF.Square, [b_nd, sbn], [B("jk"), sbn], scale=s_[:, 1:2], accum_out=s_[:, 2:3])
        ACT(C, s_[:, 3:4], s_[:, 2:3], AF.Sqrt, [sbn], [sbn], scale=1.0 / 256.0, bias=EPS)
        P.emit("dve", lambda e, s_=s_: e.reciprocal(out=s_[:, 4:5], in_=s_[:, 3:4]), reads=[sbn], writes=[sbn])
        TT(C, s_[:, 5:6], s_[:, 4:5], s_[:, 1:2], ALU.mult, [sbn], [sbn])
        STT(C, ycb[:, h * 256:(h + 1) * 256], nd[:, 0:256], s_[:, 5:6], GT[:, j, h * 256:(h + 1) * 256], ALU.mult, ALU.mult,
            [b_nd, sbn, B("GT")], [B("ycb")])

    def YT(j):
        tk = slice(j * 128, (j + 1) * 128)
        for half in range(2):
            ptv = C.PT[:, half * 1024:(half + 1) * 1024]
            pb = B(f"PT{half}")
            for k in range(8):
                kc = half * 8 + k
                TR(C, ptv[:, k * 128:(k + 1) * 128], ycb[:, kc * 128:(kc + 1) * 128], [B("ycb")], [pb])
            for k in range(8):
                kc = half * 8 + k
                ACT(C, C.big[:, kc, tk], ptv[:, k * 128:(k + 1) * 128], AF.Copy, [pb, B("sm1")], [C.bigb[kc]], scale=sm[:, 80 + kc:81 + kc])

    its = [(j, h) for j in range(NT) for h in range(8)]
    if states_only:
        for i, (j, h) in enumerate(its):
            FM(j, h, i % 2)
    else:
        FM(its[0][0], its[0][1], 0)
        for i in range(len(its)):
            if i + 1 < len(its):
                FM(its[i + 1][0], its[i + 1][1], (i + 1) % 2)
            KK_(its[i][0], its[i][1], i % 2)
            if its[i][1] == 7:
                YT(its[i][0])
    if stC_o is not None:
        for q in range(4):
            P.dma("sp", "stC_o", stC_o[:, q * 520:(q + 1) * 520], C.stf[:, q * 520:(q + 1) * 520], reads=[B(f"Cf{h}") for h in range(8)],
                  writes=[B(nm["stC_o"])])
        P.dma("sp", "stv1_o", stv_o, stvo[:], reads=[B("stvo1")], writes=[B(nm["stv_o"])])
    if states_only:
        return
    stage4(C, w1, slot, hin, p1, rpl, g_ple, hout, mix_bufs + [B("ycb"), B("jk"), B("ktk0"), B("ktk1"), B("scw0"), B("scw1"), B("ndsb0"), B("ndsb1")],
           HP, tmps, tb, g_fin, nm)


def _run_layer_unfused(nc, shared, per_core, st_names, out_name):
    zeros = {k: np.zeros(shape, np.float32) for k, (shape, _) in st_names.items()}

    def maps(states):
        ms = []
        for c in range(8):
            m = dict(shared)
            m.update(per_core[c])
            for k in st_names:
                m[k] = states[c][k]
            ms.append(m)
        return ms
    r1 = run_bass_kernel_spmd(nc, maps([zeros] * 8), core_ids=list(range(8)))
    st = []
    for c in range(8):
        if c % 2 == 1:
            st.append({k: np.asarray(r1.results[c - 1][o], np.float32) for k, (_, o) in st_names.items()})
        else:
            st.append(zeros)
    r2 = run_bass_kernel_spmd(nc, maps(st), core_ids=list(range(8)))
    return [np.asarray(r2.results[c][out_name]) for c in range(8)]


def kernel_unfused(**inp):
    inp = {k: np.asarray(v) for k, v in inp.items()}
    x, p = inp["x"], inp["p"]
    s0 = prep_l0(inp)
    s0["msk"] = np.ones((128, 1), np.float32)
    nc0 = build_program([0])
    pc = [{"hin": np.ascontiguousarray(x[c // 2, (c % 2) * T:(c % 2 + 1) * T], dtype=np.float32),
           "p0": np.ascontiguousarray(p[0, c // 2, (c % 2) * T:(c % 2 + 1) * T], dtype=np.float32)} for c in range(8)]
    h2 = _run_layer_unfused(nc0, s0, pc, {"stS": ((128, 1024), "stS_o"), "stv": ((128, 32), "stv_o")}, "hout")
    del s0
    s1 = prep_l1(inp)
    s1["msk"] = np.ones((128, 1), np.float32)
    nc1 = build_program([1])
    pc = [{"hin1": np.ascontiguousarray(h2[c], dtype=np.float32),
           "p1": np.ascontiguousarray(p[1, c // 2, (c % 2) * T:(c % 2 + 1) * T], dtype=np.float32)} for c in range(8)]
    out = _run_layer_unfused(nc1, s1, pc, {"stC": ((128, 8 * 260), "stC_o"), "stv1": ((128, 48), "stv1_o")}, "hout1")
    return np.stack(out).reshape(4, 2 * T, D).astype(np.float32)


def make_in_maps(inp):
    inp = {k: np.asarray(v) for k, v in inp.items()}
    x, p = np.asarray(inp["x"], np.float32), np.asarray(inp["p"], np.float32)
    shared = {}
    shared.update(prep_l0(inp))
    shared.update(prep_l1(inp))
    shared["zS"] = np.zeros((128, 2080), np.float32)
    shared["zv"] = np.zeros((128, 48), np.float32)
    zx = np.zeros((T, D), np.float32)
    zp = np.zeros((T, 256), np.float32)
    maps = []
    for c in range(8):
        b, hf = c // 2, c % 2
        m = dict(shared)
        m["xB"] = np.ascontiguousarray(x[b, hf * T:(hf + 1) * T])
        m["p0B"] = np.ascontiguousarray(p[0, b, hf * T:(hf + 1) * T])
        m["p1B"] = np.ascontiguousarray(p[1, b, hf * T:(hf + 1) * T])
        if hf == 1:
            m["xA"] = np.ascontiguousarray(x[b, 0:T])
            m["p0A"] = np.ascontiguousarray(p[0, b, 0:T])
            m["p1A"] = np.ascontiguousarray(p[1, b, 0:T])
        else:
            m["xA"], m["p0A"], m["p1A"] = zx, zp, zp
        m["msk"] = np.full((128, 1), float(hf), np.float32)
        maps.append(m)
    return maps


def kernel(**inp):
    maps = make_in_maps(inp)
    nc = build_program("fused")
    res = run_bass_kernel_spmd(nc, maps, core_ids=list(range(8)))
    out = [np.asarray(res.results[c]["out"], np.float32) for c in range(8)]
    return np.stack(out).reshape(4, 2 * T, D)
```

```python
import numpy as np
import ml_dtypes
import concourse.bass as bass
import concourse.mybir as mybir
from concourse.bass_utils import run_bass_kernel_spmd

F32 = mybir.dt.float32
BF16 = mybir.dt.bfloat16
AF = mybir.ActivationFunctionType
ALU = mybir.AluOpType
AX = mybir.AxisListType


class Buf:
    __slots__ = ("name", "last_w", "readers")

    def __init__(self, name):
        self.name = name
        self.last_w = None
        self.readers = []


class Op:
    __slots__ = ("eng", "idx", "thunk", "waits", "dma_waits", "signal", "sigval",
                 "is_dma", "dsem", "dval")

    def __init__(self, eng, thunk):
        self.eng = eng
        self.thunk = thunk
        self.waits = {}
        self.dma_waits = {}
        self.signal = False
        self.sigval = 0
        self.is_dma = False
        self.dsem = None
        self.dval = 0


ENGS = ("pe", "act", "dve", "pool", "sp")


class Prog:
    def __init__(self, nc):
        self.nc = nc
        self.ops = {e: [] for e in ENGS}
        self.waited = {e: {} for e in ENGS}
        self.dma_sem_val = {}
        self.dma_last = {}
        self.dma_keys = []

    def eng_obj(self, e):
        nc = self.nc
        return {"pe": nc.tensor, "act": nc.scalar, "dve": nc.vector,
                "pool": nc.gpsimd, "sp": nc.sync}[e]

    def _deps(self, op, reads, writes, acc_ok=()):
        deps = []
        for b in reads:
            if b.last_w is not None:
                deps.append(b.last_w)
        for b in writes:
            if b.last_w is not None:
                if not (b in acc_ok and b.last_w.eng == op.eng):
                    deps.append(b.last_w)
            for r in b.readers:
                deps.append(r)
        e = op.eng
        for d in deps:
            if d is op:
                continue
            if d.is_dma:
                cur = self.waited[e].get(("dma", d.dsem), 0)
                if d.dval > cur:
                    op.dma_waits[d.dsem] = max(op.dma_waits.get(d.dsem, 0), d.dval)
                    self.waited[e][("dma", d.dsem)] = d.dval
            else:
                if d.eng == e and e == "pe":
                    continue
                cur = self.waited[e].get(d.eng, -1)
                if d.idx > cur:
                    prev = op.waits.get(d.eng)
                    if prev is None or d.idx > prev.idx:
                        op.waits[d.eng] = d
        for k, d in op.waits.items():
            d.signal = True
            self.waited[e][k] = max(self.waited[e].get(k, -1), d.idx)
        for b in reads:
            b.readers.append(op)
        for b in writes:
            b.last_w = op
            b.readers = []

    def emit(self, eng, thunk, reads=(), writes=(), acc_ok=()):
        op = Op(eng, thunk)
        op.idx = len(self.ops[eng])
        self._deps(op, reads, writes, acc_ok)
        self.ops[eng].append(op)
        return op

    def dma(self, eng, key, out, in_, reads=(), writes=(), fn=None, **kw):
        if key not in self.dma_sem_val:
            self.dma_sem_val[key] = 0
            self.dma_keys.append(key)
        op = Op(eng, None)
        op.idx = len(self.ops[eng])
        op.is_dma = True
        op.dsem = key
        prev = self.dma_last.get(key)
        self._deps(op, reads, writes)
        if prev is not None:
            cur = self.waited[eng].get(("dma", key), 0)
            if prev.dval > cur:
                op.dma_waits[key] = max(op.dma_waits.get(key, 0), prev.dval)
                self.waited[eng][("dma", key)] = prev.dval
        self.dma_sem_val[key] += 16
        op.dval = self.dma_sem_val[key]
        self.dma_last[key] = op
        op.thunk = (out, in_, kw, fn)
        self.ops[eng].append(op)
        return op

    def finalize(self, sems):
        for e in ENGS:
            c = 0
            for op in self.ops[e]:
                if op.signal:
                    c += 1
                    op.sigval = c

        def run_engine(e, eng):
            for op in self.ops[e]:
                for k, d in op.waits.items():
                    eng.wait_ge(sems[k], d.sigval)
                for k, v in op.dma_waits.items():
                    eng.wait_ge(sems[("dma", k)], v)
                if op.is_dma:
                    out, in_, kw, fn = op.thunk
                    ins = fn(eng) if fn is not None else eng.dma_start(out=out, in_=in_, **kw)
                    ins.then_inc(sems[("dma", op.dsem)], 16)
                    if op.signal:
                        raise RuntimeError("dma op cannot signal engine sem")
                else:
                    ins = op.thunk(eng)
                    if op.signal:
                        ins.then_inc(sems[e], 1)
        return run_engine


def run_prog(nc, prog, final_waits=()):
    from contextlib import ExitStack
    with ExitStack() as st:
        sems = {}
        for e in ENGS:
            sems[e] = st.enter_context(nc.semaphore("s_" + e))
        for k in prog.dma_keys:
            sems[("dma", k)] = st.enter_context(nc.semaphore("d_" + str(k)))
        block = st.enter_context(nc.Block())
        runner = prog.finalize(sems)

        @block.tensor
        def _(eng):
            runner("pe", eng)

        @block.scalar
        def _(eng):
            runner("act", eng)

        @block.vector
        def _(eng):
            runner("dve", eng)

        @block.gpsimd
        def _(eng):
            runner("pool", eng)

        @block.sync
        def _(eng):
            runner("sp", eng)
            for k in final_waits:
                eng.wait_ge(sems[("dma", k)], prog.dma_sem_val[k])


D = 2048
T = 1024
NT = 8
KC = 16
EPS = 1e-6
NW = 6
ARENA = 57600
L0_NSLOT = 40 + 8 + 40


def _fm_slot(W, c0):
    blk = W[:, c0:c0 + 128].reshape(KC, 128, 128)
    return np.ascontiguousarray(blk.transpose(1, 0, 2)).reshape(128, 2048)


def _tm_slots(W, c0):
    out = []
    K = W.shape[0] // 128
    blk = W[:, c0:c0 + 512].reshape(K, 128, 512)
    for kcg in range(K // 4):
        out.append(np.ascontiguousarray(blk[kcg * 4:(kcg + 1) * 4].transpose(1, 0, 2)).reshape(128, 2048))
    return out


def _pl_slot(ple_w, g):
    out = np.zeros((128, 2048), np.float32)
    blk = ple_w[:, g * 512:(g + 1) * 512].reshape(2, 128, 512)
    out[:, 0:1024] = blk.transpose(1, 0, 2).reshape(128, 1024)
    return out


def pack_tail(w_out, gate_w, ple_w):
    slots = []
    for g in range(4):
        slots += _tm_slots(w_out, 512 * g)
    for g in range(4):
        slots.append(_pl_slot(ple_w, g))
    for g in range(4):
        slots += _tm_slots(gate_w, 512 * g)
        slots.append(_pl_slot(ple_w, g))
    return slots


def pack_l0(e_w_in, e_w_out, gate_w, ple_w):
    slots = []
    for n in range(8):
        slots.append(_fm_slot(e_w_in, 4096 + 128 * n))
        slots.append(_fm_slot(e_w_in, 5120 + 128 * n))
    for h in range(8):
        slots.append(_fm_slot(e_w_in, 1024 + 128 * h))
        slots.append(_fm_slot(e_w_in, 128 * h))
        slots.append(_fm_slot(e_w_in, 3072 + 128 * h))
    for g in range(2):
        slots += _tm_slots(e_w_in, 2048 + 512 * g)
    slots += pack_tail(e_w_out, gate_w, ple_w)
    return np.stack(slots).astype(np.float32)


def make_consts():
    c = np.zeros((128, 7, 128), np.float32)
    idx = np.arange(128)
    c[:, 0] = np.eye(128)
    same = (idx[:, None] // 64) == (idx[None, :] // 64)
    c[:, 1] = (same & (idx[:, None] <= idx[None, :])).astype(np.float32)
    c[:, 2] = (idx[:, None] < 64).astype(np.float32) * np.ones((1, 128), np.float32)
    c[:, 3] = (idx[:, None] >= 64).astype(np.float32) * np.ones((1, 128), np.float32)
    c[:, 4] = 1.0 / 128.0
    c[:, 5] = same.astype(np.float32)
    c[:, 6] = (idx[:, None] <= idx[None, :]).astype(np.float32)
    return c.reshape(128, 896)


class Ctx:
    pass


def fence(C):
    P = C.P
    ops = [P.ops[e][-1] for e in ENGS if P.ops[e]] + list(P.dma_last.values())
    C.fence_ops = ops
    for b in C.bufs.values():
        b.readers = b.readers + ops


def build_program(layers, fused=False):
    nc = bass.Bass("TRN2", target_bir_lowering=False)
    from contextlib import ExitStack
    st = ExitStack()
    with st:
        P = Prog(nc)
        C = Ctx()
        C.nc, C.P = nc, P
        C.bufs = {}
        C.fence_ops = []
        C.sbs = {}
        C.drams = {}

        def dram(name, shape, dt=F32, kind="ExternalInput"):
            if name not in C.drams:
                if kind == "Internal":
                    C.drams[name] = nc.dram_tensor(name, list(shape), dt, kind=kind, addr_space="Local").ap()
                else:
                    C.drams[name] = nc.dram_tensor(name, list(shape), dt, kind=kind).ap()
            return C.drams[name]

        def sb(name, shape, dt=F32):
            if name not in C.sbs:
                C.sbs[name] = st.enter_context(nc.sbuf_tensor(name, list(shape), dt))
            return C.sbs[name]

        def B(name):
            if name not in C.bufs:
                b = Buf(name)
                b.readers = list(C.fence_ops)
                C.bufs[name] = b
            return C.bufs[name]

        C.dram, C.sb, C.B = dram, sb, B
        C.PA = st.enter_context(nc.psum_tensor("PA", [128, 1024], F32))
        C.PB = st.enter_context(nc.psum_tensor("PB", [128, 1024], F32))
        C.PC = st.enter_context(nc.psum_tensor("PC", [128, 1024], F32))
        C.PT = st.enter_context(nc.psum_tensor("PT", [128, 2048], BF16))
        C.wring = [sb(f"wr{i}", [128, 2048], BF16) for i in range(NW)]
        C.wslot_n = 0
        C.consts = sb("consts", [128, 896], F32)
        C.ident_bf = sb("ident_bf", [128, 128], BF16)
        C.gain = sb("gain", [128, 2048], F32)
        C.big = sb("big", [128, KC, T], BF16)
        C.bigb = [B(f"big{kc}") for kc in range(KC)]
        C.arena = sb("arena", [128, ARENA], BF16)
        C.small = sb("small", [128, 64], F32)
        C.stf = sb("stf", [128, 8 * 260], F32)
        C.stb = sb("stb", [128, 8 * 260], BF16)
        C.msk = sb("msk_sb", [128, 1], F32)
        consts_d = dram("consts_d", [128, 896])
        P.dma("sp", "c", C.consts[:], consts_d, writes=[B("consts")])
        P.dma("sp", "c", C.msk[:], dram("msk", [128, 1]), writes=[B("msk")])
        P.emit("dve", lambda e: e.tensor_copy(out=C.ident_bf[:], in_=C.consts[:, 0:128]),
               reads=[B("consts")], writes=[B("ident_bf")])
        C.out_keys = []
        import os
        C.stop = int(os.environ.get('STOP', '99'))
        C.rstop = int(os.environ.get('RSTOP', '99'))
        if layers == "fused":
            layer0(C, "A")
            fence(C)
            layer1(C, "A")
            fence(C)
            layer0(C, "B")
            fence(C)
            layer1(C, "B")
        else:
            for li in layers:
                if li == 0:
                    layer0(C, "U")
                else:
                    layer1(C, "U")
        run_prog(nc, P, final_waits=[k for k in C.out_keys if k in P.dma_sem_val])
    return nc


def ACT(C, out, in_, func, R, W, **kw):
    return C.P.emit("act", lambda e: e.activation(out=out, in_=in_, func=func, **kw), reads=R, writes=W)


def TS(C, out, in0, s1, s2, op0, op1, R, W, eng="dve"):
    return C.P.emit(eng, lambda e: e.tensor_scalar(out=out, in0=in0, scalar1=s1, scalar2=s2, op0=op0, op1=op1),
                    reads=R, writes=W)


def TT(C, out, in0, in1, op, R, W, eng="dve"):
    return C.P.emit(eng, lambda e: e.tensor_tensor(out=out, in0=in0, in1=in1, op=op), reads=R, writes=W)


def STT(C, out, in0, scalar, in1, op0, op1, R, W):
    return C.P.emit("dve", lambda e: e.scalar_tensor_tensor(out=out, in0=in0, scalar=scalar, in1=in1, op0=op0, op1=op1),
                    reads=R, writes=W)


def CP(C, out, in_, R, W, eng="dve"):
    return C.P.emit(eng, lambda e: e.tensor_copy(out=out, in_=in_), reads=R, writes=W)


def MM(C, out, lhsT, rhs, start, stop, R, W):
    return C.P.emit("pe", lambda e: e.matmul(out, lhsT=lhsT, rhs=rhs, start=start, stop=stop),
                    reads=R, writes=W, acc_ok=W)


def TR(C, out, in_, R, W):
    return C.P.emit("pe", lambda e: e.transpose(out, in_, C.ident_bf[:]), reads=R + [C.B("ident_bf")], writes=W, acc_ok=W)


def load_w(C, wd, slot):
    i = C.wslot_n % NW
    C.wslot_n += 1
    b = C.B(f"wr{i}")
    C.P.dma("pool", f"w{i}", C.wring[i][:], wd[slot], writes=[b])
    return C.wring[i], b


def rms_to_T(C, src_tile, src_buf, j, gain_ready_buf, dstT, dst_bufs, hb):
    B = C.B
    sm = C.small[:, 16 + 4 * hb:20 + 4 * hb]
    junk = C.hbf[hb]
    ACT(C, junk[:], src_tile, AF.Square, [src_buf], [B(f"hbf{hb}"), B(f"sm_ssq{hb}")], accum_out=sm[:, 0:1])
    ACT(C, sm[:, 1:2], sm[:, 0:1], AF.Sqrt, [B(f"sm_ssq{hb}")], [B(f"sm_sd{hb}")], scale=1.0 / D, bias=EPS)
    C.P.emit("dve", lambda e: e.reciprocal(out=sm[:, 2:3], in_=sm[:, 1:2]), reads=[B(f"sm_sd{hb}")], writes=[B(f"sm_rstd{hb}")])
    STT(C, junk[:], src_tile, sm[:, 2:3], C.gain[:], ALU.mult, ALU.mult,
        [src_buf, B(f"sm_rstd{hb}"), gain_ready_buf], [B(f"hbf{hb}")])
    for half in range(2):
        ptv = C.PT[:, half * 1024:(half + 1) * 1024]
        pb = B(f"PT{half}")
        for k in range(8):
            kc = half * 8 + k
            TR(C, ptv[:, k * 128:(k + 1) * 128], junk[:, kc * 128:(kc + 1) * 128], [B(f"hbf{hb}")], [pb])
        eng = "act" if half == 0 else "dve"
        dst = dstT[:, half * 8:(half + 1) * 8, j * 128:(j + 1) * 128]
        srcv = ptv.rearrange("p (a b) -> p a b", b=128)
        if eng == "act":
            ACT(C, dst, srcv, AF.Copy, [pb], dst_bufs)
        else:
            CP(C, dst, srcv, [pb], dst_bufs)


def alias_bufs(new_bufs, old_bufs):
    ops = []
    for ob in old_bufs:
        ops += ob.readers
        if ob.last_w is not None:
            ops.append(ob.last_w)
    for nb in new_bufs:
        nb.readers = nb.readers + ops


def inproj_fm(C, wd, slot, PS, psb):
    wt, wb = load_w(C, wd, slot)
    for half in range(2):
        for kc in range(KC):
            MM(C, PS[:, half * 512:(half + 1) * 512], wt[:, kc * 128:(kc + 1) * 128],
               C.big[:, kc, half * 512:(half + 1) * 512], kc == 0, kc == KC - 1,
               [wb, C.bigb[kc]], [psb])


def tm_group(C, wd, slot0, K4, lhs_fn, lhs_bufs_fn, consume):
    wts = [load_w(C, wd, slot0 + i) for i in range(K4)]
    for j in range(NT):
        PS, nm = [(C.PA, "PA"), (C.PB, "PB")][(j // 2) % 2]
        ps = PS[:, (j % 2) * 512:(j % 2) * 512 + 512]
        pb = C.B(f"{nm}h{j % 2}")
        nk = K4 * 4
        for kc in range(nk):
            wt, wb = wts[kc // 4]
            MM(C, ps, lhs_fn(kc, j), wt[:, (kc % 4) * 512:(kc % 4) * 512 + 512], kc == 0, kc == nk - 1,
               [wb] + lhs_bufs_fn(kc, j), [pb])
        consume(j, ps, pb)


def layer0(C, seg="U"):
    P, B, nc = C.P, C.B, C.nc
    dram, sb = C.dram, C.sb
    w0 = dram("w0", [L0_NSLOT, 128, 2048])
    wri_d = dram("wri", [128, 16, 128])
    sm_d = dram("sm0", [128, 96])
    g_e = dram("g_e", [D])
    g_ple = dram("g_ple0", [D])
    if seg == "U":
        hin, p0 = dram("hin", [T, D]), dram("p0", [T, 256])
        stS_d, stv_d = dram("stS", [128, 1024]), dram("stv", [128, 32])
        hout = dram("hout", [T, D], kind="ExternalOutput")
        stS_o = dram("stS_o", [128, 1024], kind="ExternalOutput")
        stv_o = dram("stv_o", [128, 32], kind="ExternalOutput")
        nm = {"hin": "d_hin", "hout": "d_hout", "stS": "d_stS", "stv": "d_stv", "stS_o": "d_stS_o", "stv_o": "d_stv_o"}
        C.out_keys += ["hout0", "hout1", "stS_o", "stv_o"]
    elif seg == "A":
        hin, p0 = dram("xA", [T, D]), dram("p0A", [T, 256])
        stS_d, stv_d = dram("zS", [128, 2080])[:, 0:1024], dram("zv", [128, 48])[:, 0:32]
        hout = dram("h2A", [T, D], kind="Internal")
        stS_o = dram("sS0", [128, 1024], kind="Internal")
        stv_o = dram("sv0", [128, 32], kind="Internal")
        nm = {"hin": "d_xA", "hout": "d_h2A", "stS": "d_zS", "stv": "d_zv", "stS_o": "d_sS0", "stv_o": "d_sv0"}
    else:
        hin, p0 = dram("xB", [T, D]), dram("p0B", [T, 256])
        stS_d, stv_d = dram("sS0", [128, 1024], kind="Internal"), dram("sv0", [128, 32], kind="Internal")
        hout = dram("h2B", [T, D], kind="Internal")
        stS_o = stv_o = None
        nm = {"hin": "d_xB", "hout": "d_h2B", "stS": "d_sS0", "stv": "d_sv0"}

    A = C.arena
    def abf(off, a, b):
        return A[:, off:off + a * b].rearrange("p (a b) -> p a b", b=b)
    G1 = abf(0, 8, 1024)
    G2 = abf(8192, 8, 1024)
    Vt = abf(8192, 8, 1024)
    qT = abf(16384, 8, 1024)
    kT = abf(24576, 8, 1024)
    zaT = abf(32768, 8, 1024)
    TB = 40960
    def tmpf(i, w=1032):
        return A[:, TB + i * 2064:TB + (i + 1) * 2064].bitcast(F32)[:, 0:w]
    tmps = [tmpf(i) for i in range(7)]
    tb = [B(f"tmp{i}") for i in range(7)]
    XCB = A[:, TB + 7 * 2064:TB + 7 * 2064 + 1024]
    ZB = A[:, TB + 7 * 2064 + 1024:TB + 7 * 2064 + 2048]
    htile = [A[:, 4096 * i:4096 * (i + 1)].bitcast(F32) for i in range(4)]
    hbf = [A[:, 16384 + 2048 * i:16384 + 2048 * (i + 1)] for i in range(4)]
    C.htile, C.hbf = htile, hbf
    HP = A[:, 0:32768].bitcast(F32).rearrange("p (a b) -> p a b", b=2048)

    wri = sb("wri_sb", [128, 16, 128], BF16)
    sm = sb("sm0_sb", [128, 96], F32)
    smx = sb("smx", [128, 64], F32)
    plast = sb("plast", [128, 8, 17], F32)
    PP = sb("PP", [128, 8, 8], F32)
    khat = [sb(f"khat{i}", [128, 64], BF16) for i in range(2)]
    stv = sb("stv_sb", [128, 32], F32)
    stvo = sb("stvo_sb", [128, 32], F32)
    ktok = [sb(f"ktok{i}", [128, 128], BF16) for i in range(2)]
    attm = [sb(f"attm{i}", [128, 128], BF16) for i in range(2)]
    sqf = [A[:, TB + 256 * i:TB + 256 * (i + 1)].bitcast(F32) for i in range(2)]
    oc = [A[:, TB + 512 + 256 * i:TB + 512 + 256 * (i + 1)].bitcast(F32) for i in range(2)]
    sqb = [A[:, TB + 1024 + 128 * i:TB + 1024 + 128 * (i + 1)] for i in range(2)]
    ones_bf = sb("ones_bf", [128, 128], BF16)
    rpl = sb("rpl", [128, 16], F32)

    P.dma("pool", "wri", wri[:], wri_d, writes=[B("wri")])
    P.dma("sp", "sm", sm[:], sm_d, writes=[B("sm")])
    P.dma("sp", "stv", stv[:], stv_d, reads=[B(nm["stv"])], writes=[B("stv")])
    P.dma("sp", "stS", C.stf[:, 0:1024], stS_d, reads=[B(nm["stS"])], writes=[B("stf")])
    TS(C, stv[:], stv[:], C.msk[:, 0:1], None, ALU.mult, ALU.bypass, [B("stv"), B("msk")], [B("stv")])
    TS(C, C.stf[:, 0:1024], C.stf[:, 0:1024], C.msk[:, 0:1], None, ALU.mult, ALU.bypass, [B("stf"), B("msk")], [B("stf")])
    P.emit("pool", lambda e: e.memset(ZB, 0.0), writes=[B("zb")])
    P.emit("pool", lambda e: e.memset(plast[:], 1.0), writes=[B("plast")])
    TT(C, smx[:, 0:8], sm[:, 0:8], sm[:, 8:16], ALU.subtract, [B("sm")], [B("smx")])
    ACT(C, smx[:, 0:8], smx[:, 0:8], AF.Sigmoid, [B("smx")], [B("smx")])
    TS(C, smx[:, 8:16], smx[:, 0:8], -1.0, 1.0, ALU.mult, ALU.add, [B("smx")], [B("smx")])
    ACT(C, smx[:, 32:40], sm[:, 80:88], AF.Exp, [B("sm")], [B("smx")], scale=-1.0)
    ACT(C, smx[:, 32:40], smx[:, 32:40], AF.Ln, [B("smx")], [B("smx")], bias=1.0)
    TS(C, smx[:, 16:24], smx[:, 32:40], -8.0, None, ALU.mult, ALU.bypass, [B("smx")], [B("smx")])
    TS(C, smx[:, 24:32], smx[:, 32:40], -16.0, None, ALU.mult, ALU.bypass, [B("smx")], [B("smx")])

    P.dma("sp", "g", C.gain[:], g_e.partition_broadcast(128), writes=[B("gain")])
    for j in range(NT):
        hb = j % 4
        P.dma("sp", f"h{hb}", htile[hb], hin[j * 128:(j + 1) * 128, :], reads=[B(nm["hin"])], writes=[B(f"htile{hb}")])
        rms_to_T(C, htile[hb], B(f"htile{hb}"), j, B("gain"), C.big, C.bigb, hb)
    if C.stop <= 1:
        return
    mix_bufs = [B(n) for n in ("G1", "G2", "qT", "kT", "zaT")] + tb + [B("xcb"), B("Vt")]
    alias_bufs(mix_bufs, [B(f"htile{i}") for i in range(4)] + [B(f"hbf{i}") for i in range(4)])

    X, XC, R_, I_, M_, AC, ZS = tmps
    bX, bXC, bR, bI, bM, bAC, bZS = tb
    slot = 0
    inproj_fm(C, w0, slot, C.PA, B("PA")); slot += 1
    for n in range(8):
        ACT(C, X[:, 3:1027], C.PA[:], AF.Copy, [B("PA")], [bX])
        CP(C, X[:, 0:3], stv[:, 8 + 3 * n:11 + 3 * n], [B("stv")], [bX])
        CP(C, stvo[:, 8 + 3 * n:11 + 3 * n], X[:, 1024:1027], [bX], [B("stvo")])
        inproj_fm(C, w0, slot, C.PA, B("PA")); slot += 1
        ACT(C, ZS[:, 0:1024], C.PA[:], AF.Silu, [B("PA")], [bZS])
        if n + 1 < 8:
            inproj_fm(C, w0, slot, C.PA, B("PA")); slot += 1
        cw = lambda k: sm[:, 24 + 4 * n + k:25 + 4 * n + k]
        TS(C, XC[:, 0:1024], X[:, 3:1027], cw(3), sm[:, 56 + n:57 + n], ALU.mult, ALU.add, [bX, B("sm")], [bXC])
        for k in (2, 1, 0):
            STT(C, XC[:, 0:1024], X[:, k:k + 1024], cw(k), XC[:, 0:1024], ALU.mult, ALU.add, [bX, bXC, B("sm")], [bXC])
        ACT(C, XCB, XC[:, 0:1024], AF.Copy, [bXC], [B("xcb")])
        for half in range(2):
            MM(C, C.PB[:, half * 512:(half + 1) * 512], wri[:, n, :], XCB[:, half * 512:(half + 1) * 512], True, True,
               [B("wri"), B("xcb")], [B("PB")])
            MM(C, C.PC[:, half * 512:(half + 1) * 512], wri[:, 8 + n, :], XCB[:, half * 512:(half + 1) * 512], True, True,
               [B("wri"), B("xcb")], [B("PC")])
        ACT(C, R_[:, 0:1024], C.PB[:], AF.Sigmoid, [B("PB"), B("sm")], [bR], bias=sm[:, 64 + n:65 + n])
        ACT(C, I_[:, 0:1024], C.PC[:], AF.Sigmoid, [B("PC"), B("sm")], [bI], bias=sm[:, 72 + n:73 + n])
        ACT(C, M_[:, 0:1024], R_[:, 0:1024], AF.Exp, [bR, B("smx")], [bM], scale=smx[:, 24 + n:25 + n])
        ACT(C, R_[:, 0:1024], R_[:, 0:1024], AF.Exp, [bR, B("smx")], [bR], scale=smx[:, 16 + n:17 + n])
        ACT(C, M_[:, 0:1024], M_[:, 0:1024], AF.Sqrt, [bM], [bM], scale=-1.0, bias=1.0)
        TT(C, I_[:, 0:1024], I_[:, 0:1024], XC[:, 0:1024], ALU.mult, [bI, bXC], [bI])
        TT(C, I_[:, 0:1024], I_[:, 0:1024], M_[:, 0:1024], ALU.mult, [bI, bM], [bI])
        P.emit("dve", lambda e: e.tensor_tensor_scan(out=M_[:, 0:1024], data0=R_[:, 0:1024], data1=I_[:, 0:1024],
                                                     initial=0.0, op0=ALU.mult, op1=ALU.add),
               reads=[bR, bI], writes=[bM])
        P.emit("dve", lambda e: e.tensor_tensor_scan(out=AC[:, 0:1024], data0=R_[:, 0:1024], data1=ZB,
                                                     initial=1.0, op0=ALU.mult, op1=ALU.add),
               reads=[bR, B("zb")], writes=[bAC])
        TT(C, G1[:, n, :], M_[:, 0:1024], ZS[:, 0:1024], ALU.mult, [bM, bZS], [B("G1")])
        TT(C, G2[:, n, :], AC[:, 0:1024], ZS[:, 0:1024], ALU.mult, [bAC, bZS], [B("G2")])
        CP(C, stvo[:, n:n + 1], M_[:, 1023:1024], [bM], [B("stvo")])
    if C.stop <= 2:
        return
    F_, KK, Pc, Rc, Q_ = tmps[0], tmps[1], tmps[2], tmps[3], tmps[4]
    bF, bKK, bPc, bRc, bQ = tb[0], tb[1], tb[2], tb[3], tb[4]
    for h in range(8):
        inproj_fm(C, w0, slot, C.PA, B("PA")); slot += 1
        ACT(C, F_[:, 0:1024], C.PA[:], AF.Sigmoid, [B("PA")], [bF])
        TS(C, F_[:, 0:1024], F_[:, 0:1024], smx[:, 8 + h:9 + h], smx[:, h:h + 1], ALU.mult, ALU.add, [bF, B("smx")], [bF])
        TS(C, KK[:, 0:1024], F_[:, 0:1024], -1.0, 1.0, ALU.mult, ALU.add, [bF], [bKK])
        for c in range(16):
            P.emit("dve", lambda e, c=c: e.tensor_tensor_scan(out=Pc[:, c * 64:(c + 1) * 64], data0=F_[:, c * 64:(c + 1) * 64],
                                                              data1=ZB[:, 0:64], initial=1.0, op0=ALU.mult, op1=ALU.add),
                   reads=[bF, B("zb")], writes=[bPc])
        P.emit("dve", lambda e: e.reciprocal(out=Rc[:, 0:1024], in_=Pc[:, 0:1024]), reads=[bPc], writes=[bRc])
        TT(C, kT[:, h, :], KK[:, 0:1024], Rc[:, 0:1024], ALU.mult, [bKK, bRc], [B("kT")])
        CP(C, plast[:, h, 1:17], Pc[:, 0:1024].rearrange("p (c s) -> p c s", s=64)[:, :, 63], [bPc], [B("plast")])
        plv = plast[:, h, 0:16].rearrange("p (j two) -> p j two", two=2)
        TT(C, PP[:, h, :], plv[:, :, 0], plv[:, :, 1], ALU.mult, [B("plast")], [B("PP")])
        inproj_fm(C, w0, slot, C.PB, B("PB")); slot += 1
        ACT(C, Q_[:, 0:1024], C.PB[:], AF.Silu, [B("PB")], [bQ])
        TT(C, qT[:, h, :], Q_[:, 0:1024], Pc[:, 0:1024], ALU.mult, [bQ, bPc], [B("qT")])
        inproj_fm(C, w0, slot, C.PC, B("PC")); slot += 1
        ACT(C, zaT[:, h, :], C.PC[:], AF.Silu, [B("PC")], [B("zaT")])
    if C.stop <= 3:
        return
    for n in range(8):
        STT(C, G1[:, n, :], G2[:, n, :], stv[:, n:n + 1], G1[:, n, :], ALU.mult, ALU.add, [B("G2"), B("G1"), B("stv")], [B("G1")])
    alias_bufs([B("Vt")], [B("G2")])
    alias_bufs([B("PAh0"), B("PAh1"), B("PBh0"), B("PBh1")], [B("PA"), B("PB")])
    for g in range(2):
        def cons(j, ps, pb, g=g):
            ACT(C, Vt[:, j, g * 512:(g + 1) * 512], ps, AF.Copy, [pb], [B("Vt")])
        tm_group(C, w0, slot, 4, lambda kc, j: C.big[:, kc, j * 128:(j + 1) * 128], lambda kc, j: [C.bigb[kc]], cons)
        slot += 4
    if C.stop <= 4:
        return
    for n in range(8):
        CP(C, C.big[:, 8 + n, :], G1[:, n, :], [B("G1")], [C.bigb[8 + n]], eng="pool")
    alias_bufs([B("PCh0"), B("PCh1")], [B("PC")])
    alias_bufs([B("sqf0"), B("sqf1"), B("oc0"), B("oc1"), B("sqb0"), B("sqb1")], tb)
    CP(C, ones_bf[:], C.consts[:, 512:640], [B("consts")], [B("ones_bf")])
    U = C.stf[:, 0:1024].rearrange("p (h e) -> p h e", e=128)
    Sb = C.stb[:, 0:1024].rearrange("p (h e) -> p h e", e=128)
    Sh = C.stb[:, 1024:2048].rearrange("p (h e) -> p h e", e=128)
    caus = C.consts[:, 768:896]
    ident = C.consts[:, 0:128]
    maskA = C.consts[:, 128:256]
    ones128 = C.consts[:, 512:640]
    for h in range(8):
        ACT(C, Sb[:, h, :], U[:, h, :], AF.Copy, [B("stf")], [B(f"Sb{h}")])
        ACT(C, Sh[:, h, :], U[:, h, :], AF.Copy, [B("stf"), B("PP")], [B(f"Sh{h}")], scale=PP[:, h, 0:1])
    def FM(j, h, r):
        p_att = C.PC[:, r * 512 + 256:r * 512 + 384]
        b_att = B(f"PCh{r}")
        KVP, kvn = [(C.PB, "PB"), (C.PA, "PA")][r]
        p_kv0, p_kv1 = KVP[:, 0:128], KVP[:, 512:640]
        b_kv0, b_kv1 = B(kvn + "h0"), B(kvn + "h1")
        p_o = C.PC[:, r * 512:r * 512 + 128]
        b_o = B(f"PCh{r}")
        tk = slice(j * 128, (j + 1) * 128)
        ptk = C.PT[:, r * 1024:r * 1024 + 128]
        c0 = 2 * j
        t0_, t1_ = slice(j * 128, j * 128 + 64), slice(j * 128 + 64, (j + 1) * 128)
        TR(C, ptk, kT[:, h, tk], [B("kT")], [B(f"PT{r}")])
        MM(C, p_att, kT[:, h, tk], qT[:, h, tk], True, True, [B("kT"), B("qT")], [b_att])
        ACT(C, khat[r][:], kT[:, h, t0_], AF.Copy, [B("kT"), B("plast")], [B(f"khat{r}")], scale=plast[:, h, c0 + 1:c0 + 2])
        MM(C, p_att[0:64, 64:128], khat[r][:], qT[:, h, t1_], True, True, [B(f"khat{r}"), B("qT")], [b_att])
        ACT(C, ktok[r][:], ptk, AF.Copy, [B(f"PT{r}")], [B(f"ktok{r}")])
        TT(C, attm[r][:], p_att, caus, ALU.mult, [b_att, B("consts")], [B(f"attm{r}")])
        MM(C, p_kv0, ktok[r][0:64, :], Vt[0:64, j, h * 128:(h + 1) * 128], True, True, [B(f"ktok{r}"), B("Vt")], [b_kv0])
        MM(C, p_kv1, ktok[r][64:128, :], Vt[64:128, j, h * 128:(h + 1) * 128], True, True, [B(f"ktok{r}"), B("Vt")], [b_kv1])
        MM(C, p_o[:, 0:64], Sb[:, h, :], qT[:, h, t0_], True, False, [B(f"Sb{h}"), B("qT")], [b_o])
        MM(C, p_o[:, 64:128], Sh[:, h, :], qT[:, h, t1_], False, False, [B(f"Sh{h}"), B("qT")], [b_o])
        MM(C, p_o, Vt[:, j, h * 128:(h + 1) * 128], attm[r][:], False, True, [B("Vt"), B(f"attm{r}")], [b_o])
        ACT(C, oc[r][:], p_o, AF.Copy, [b_o], [B(f"oc{r}")])
        ACT(C, sqb[r][:], oc[r][:], AF.Square, [B(f"oc{r}")], [B(f"sqb{r}")])
        p_ms = KVP[:, 128:256]
        STT(C, U[:, h, :], U[:, h, :], plast[:, h, c0:c0 + 1], p_kv0, ALU.mult, ALU.add, [B(f"U{h}"), B("stf"), B("plast"), b_kv0], [B(f"U{h}")])
        STT(C, U[:, h, :], U[:, h, :], plast[:, h, c0 + 1:c0 + 2], p_kv1, ALU.mult, ALU.add, [B(f"U{h}"), B("plast"), b_kv1], [B(f"U{h}")])
        MM(C, p_ms, ones_bf[:], sqb[r][:], True, True, [B("ones_bf"), B(f"sqb{r}")], [b_kv0])
        TS(C, Sb[:, h, :], U[:, h, :], plast[:, h, c0 + 2:c0 + 3], None, ALU.mult, ALU.bypass, [B(f"U{h}"), B("plast")], [B(f"Sb{h}")])
        if j + 1 < NT:
            TS(C, Sh[:, h, :], U[:, h, :], PP[:, h, j + 1:j + 2], None, ALU.mult, ALU.bypass, [B(f"U{h}"), B("PP")], [B(f"Sh{h}")])

    def KK_(j, h, r):
        KVP, kvn = [(C.PB, "PB"), (C.PA, "PA")][r]
        p_ms = KVP[:, 128:256]
        b_ms = B(kvn + "h0")
        tk = slice(j * 128, (j + 1) * 128)
        ACT(C, sqf[r][:], p_ms, AF.Ln, [b_ms], [B(f"sqf{r}")], bias=EPS)
        ACT(C, sqf[r][:], sqf[r][:], AF.Exp, [B(f"sqf{r}")], [B(f"sqf{r}")], scale=-0.5)
        STT(C, sqf[r][:], oc[r][:], sm[:, 16 + h:17 + h], sqf[r][:], ALU.mult, ALU.mult, [B(f"oc{r}"), B("sm"), B(f"sqf{r}")], [B(f"sqf{r}")])
        TT(C, C.big[:, h, tk], sqf[r][:], zaT[:, h, tk], ALU.mult, [B(f"sqf{r}"), B("zaT")], [C.bigb[h]], eng="pool")

    its = [(j, h) for j in range(NT) for h in range(8)]
    FM(its[0][0], its[0][1], 0)
    for i in range(len(its)):
        if i + 1 < len(its):
            FM(its[i + 1][0], its[i + 1][1], (i + 1) % 2)
        KK_(its[i][0], its[i][1], i % 2)
    for h in range(8):
        ACT(C, C.stf[:, h * 128:(h + 1) * 128], U[:, h, :], AF.Copy, [B(f"U{h}"), B("plast")], [B(f"U{h}")], scale=plast[:, h, 16:17])
    if stS_o is not None:
        P.dma("sp", "stS_o", stS_o, C.stf[:, 0:1024], reads=[B(f"U{h}") for h in range(8)], writes=[B(nm["stS_o"])])
        P.dma("sp", "stv_o", stv_o, stvo[:], reads=[B("stvo")], writes=[B(nm["stv_o"])])
    if C.stop <= 5:
        return
    stage4(C, w0, slot, hin, p0, rpl, g_ple, hout, mix_bufs + [B("xcb"), B("zb"), B("sqf0"), B("sqf1"), B("oc0"), B("oc1"), B("sqb0"), B("sqb1")], HP, tmps, tb, None, nm)


def to_T(C, src_bf, src_buf, j, dstT, dst_bufs):
    B = C.B
    for half in range(2):
        ptv = C.PT[:, half * 1024:(half + 1) * 1024]
        pb = B(f"PT{half}")
        for k in range(8):
            kc = half * 8 + k
            TR(C, ptv[:, k * 128:(k + 1) * 128], src_bf[:, kc * 128:(kc + 1) * 128], [src_buf], [pb])
        dst = dstT[:, half * 8:(half + 1) * 8, j * 128:(j + 1) * 128]
        srcv = ptv.rearrange("p (a b) -> p a b", b=128)
        if half == 0:
            ACT(C, dst, srcv, AF.Copy, [pb], dst_bufs[half * 8:(half + 1) * 8])
        else:
            CP(C, dst, srcv, [pb], dst_bufs[half * 8:(half + 1) * 8])


def stage4(C, wd, slot, hin, p_d, rpl, g_ple, hout, old_bufs, HP, tmps, tb, final_gain, names):
    P, B = C.P, C.B
    A = C.arena
    pT_buf = B("pT")
    pT = A[:, 55408:57456].rearrange("p (a b) -> p a b", b=T)
    hpb = [B(f"HP{j}") for j in range(NT)]
    hb2 = [A[:, 32768 + 2048 * i:32768 + 2048 * (i + 1)] for i in range(4)]
    hb2b = [B(f"hb2_{i}") for i in range(4)]
    alias_bufs(hpb + hb2b + tb + [pT_buf], old_bufs)
    alias_bufs([B("PAh0"), B("PAh1"), B("PBh0"), B("PBh1"), B("PCh0"), B("PCh1"), B("PT0"), B("PT1")],
               [B(n) for n in ("PA", "PB", "PC", "PT0", "PT1", "PAh0", "PAh1", "PBh0", "PBh1", "PCh0", "PCh1")])
    st = [tmps[0][:, 0:512], tmps[1][:, 0:512]]
    for j in range(NT):
        i = j % 2
        pst = [tmps[5], tmps[6]][i]
        P.dma("sp", f"hs{i}", pst[:, 0:256], p_d[j * 128:(j + 1) * 128, :], writes=[tb[5 + i]])
        CP(C, hb2[i][:, 0:256], pst[:, 0:256], [tb[5 + i]], [hb2b[i]])
        for kc in range(2):
            TR(C, C.PT[:, kc * 128:(kc + 1) * 128], hb2[i][:, kc * 128:(kc + 1) * 128], [hb2b[i]], [B("PT0")])
        CP(C, pT[:, :, j * 128:(j + 1) * 128], C.PT[:, 0:256].rearrange("p (a b) -> p a b", b=128), [B("PT0")], [pT_buf])
    cnt = [0]
    for g in range(4):
        def cons(j, ps, pb, g=g):
            i = cnt[0] % 2
            cnt[0] += 1
            P.dma("sp", f"hs{i}", st[i], hin[j * 128:(j + 1) * 128, g * 512:(g + 1) * 512], reads=[B(names["hin"])], writes=[tb[i]])
            TT(C, HP[:, j, g * 512:(g + 1) * 512], ps, st[i], ALU.add, [pb, tb[i]], [hpb[j]])
        tm_group(C, wd, slot, 4, lambda kc, j: C.big[:, kc, j * 128:(j + 1) * 128], lambda kc, j: [C.bigb[kc]], cons)
        slot += 4
    if C.stop <= 6:
        return slot
    plw = [load_w(C, wd, slot + i) for i in range(4)]
    slot += 4
    sm = C.small
    junk = tmps[2]
    for j in range(NT):
        i = j % 4
        if j % 2 == 0:
            ACT(C, hb2[i], HP[:, j, :], AF.Copy, [hpb[j]], [hb2b[i]])
        else:
            CP(C, hb2[i], HP[:, j, :], [hpb[j]], [hb2b[i]])
        for g in range(4):
            PS, nm = [(C.PA, "PA"), (C.PB, "PB")][g // 2]
            ps = PS[:, (g % 2) * 512:(g % 2) * 512 + 512]
            pb = B(f"{nm}h{g % 2}")
            wt, wb = plw[g]
            for kc in range(2):
                MM(C, ps, pT[:, kc, j * 128:(j + 1) * 128], wt[:, kc * 512:(kc + 1) * 512], kc == 0, kc == 1, [wb, pT_buf], [pb])
        to_T(C, hb2[i], hb2b[i], j, C.big, C.bigb)
        q = 8 + 4 * (j % 2)
        ACT(C, junk[:, 0:1024], C.PA[:], AF.Square, [B("PAh0"), B("PAh1")], [tb[2], B(f"sm_q0{j % 2}")], accum_out=sm[:, q:q + 1])
        ACT(C, junk[:, 0:1024], C.PB[:], AF.Square, [B("PBh0"), B("PBh1")], [tb[2], B(f"sm_q1{j % 2}")], accum_out=sm[:, q + 1:q + 2])
        TT(C, sm[:, q + 2:q + 3], sm[:, q:q + 1], sm[:, q + 1:q + 2], ALU.add, [B(f"sm_q0{j % 2}"), B(f"sm_q1{j % 2}")], [B(f"sm_q2{j % 2}")])
        ACT(C, sm[:, q + 3:q + 4], sm[:, q + 2:q + 3], AF.Sqrt, [B(f"sm_q2{j % 2}")], [B(f"sm_q3{j % 2}")], scale=1.0 / D, bias=EPS)
        P.emit("dve", lambda e, j=j, q=q: e.reciprocal(out=rpl[:, j:j + 1], in_=sm[:, q + 3:q + 4]), reads=[B(f"sm_q3{j % 2}")], writes=[B("rpl")])
    if C.stop <= 8:
        return slot
    P.dma("sp", "g", C.gain[:], g_ple.partition_broadcast(128), writes=[B("gain")])
    SG, PL = tmps[3], tmps[4]
    bSG, bPL = tb[3], tb[4]
    for g in range(4):
        pw, pwb = None, None

        def cons(j, ps, pb, g=g):
            ps2 = C.PC[:, (j % 2) * 512:(j % 2) * 512 + 512]
            pb2 = B(f"PCh{j % 2}")
            for kc in range(2):
                MM(C, ps2, pT[:, kc, j * 128:(j + 1) * 128], cons.pw[:, kc * 512:(kc + 1) * 512], kc == 0, kc == 1, [cons.pwb, pT_buf], [pb2])
            ACT(C, SG[:, 0:512], ps, AF.Sigmoid, [pb], [bSG])
            STT(C, PL[:, 0:512], ps2, rpl[:, j:j + 1], C.gain[:, g * 512:(g + 1) * 512], ALU.mult, ALU.mult, [pb2, B("rpl"), B("gain")], [bPL])
            TT(C, PL[:, 0:512], PL[:, 0:512], SG[:, 0:512], ALU.mult, [bPL, bSG], [bPL])
            TT(C, HP[:, j, g * 512:(g + 1) * 512], HP[:, j, g * 512:(g + 1) * 512], PL[:, 0:512], ALU.add, [hpb[j], bPL], [hpb[j]])
        cons.pw, cons.pwb = load_w(C, wd, slot + 4)
        tm_group(C, wd, slot, 4, lambda kc, j: C.big[:, kc, j * 128:(j + 1) * 128], lambda kc, j: [C.bigb[kc]], cons)
        slot += 5
    if C.stop <= 9:
        return slot
    if final_gain is not None:
        P.dma("sp", "g", C.gain[:], final_gain.partition_broadcast(128), writes=[B("gain")])
    for j in range(NT):
        i = j % 2
        if final_gain is None:
            for q in range(4):
                P.dma("sp", f"hout{i}", hout[j * 128:(j + 1) * 128, q * 512:(q + 1) * 512], HP[:, j, q * 512:(q + 1) * 512], reads=[hpb[j]], writes=[B(names["hout"])])
        else:
            ot = A[:, 32768 + i * 4096:32768 + (i + 1) * 4096].bitcast(F32)
            ob = B(f"ot{i}")
            if j < 2:
                alias_bufs([ob], hb2b)
            sm2 = C.small
            ACT(C, ot, HP[:, j, :], AF.Square, [hpb[j]], [ob, B("sm_ssq")], accum_out=sm2[:, 0:1])
            ACT(C, sm2[:, 1:2], sm2[:, 0:1], AF.Sqrt, [B("sm_ssq")], [B("sm_sd")], scale=1.0 / D, bias=EPS)
            P.emit("dve", lambda e: e.reciprocal(out=sm2[:, 2:3], in_=sm2[:, 1:2]), reads=[B("sm_sd")], writes=[B("sm_rstd")])
            STT(C, ot, HP[:, j, :], sm2[:, 2:3], C.gain[:], ALU.mult, ALU.mult, [hpb[j], B("sm_rstd"), B("gain")], [ob])
            for q in range(4):
                P.dma("sp", f"hout{i}", hout[j * 128:(j + 1) * 128, q * 512:(q + 1) * 512], ot[:, q * 512:(q + 1) * 512], reads=[ob], writes=[B(names["hout"])])
    return slot


def _pp(v, n):
    return np.ascontiguousarray(np.asarray(v, np.float32).reshape(n, 128).T)


def prep_l0(inp):
    sm = np.zeros((128, 96), np.float32)
    sm[:, 0:8] = _pp(inp["a_lb_logits"][0], 8)
    sm[:, 8:16] = _pp(inp["a_lb_logits"][1], 8)
    sm[:, 16:24] = _pp(inp["a_norm"][0], 8)
    sm[:, 24:56] = np.asarray(inp["b_conv_w"][0], np.float32).reshape(4, 8, 128).transpose(2, 1, 0).reshape(128, 32)
    sm[:, 56:64] = _pp(inp["b_conv_b"][0], 8)
    sm[:, 64:72] = _pp(inp["b_b_r"][0], 8)
    sm[:, 72:80] = _pp(inp["b_b_i"][0], 8)
    sm[:, 80:88] = _pp(inp["b_lambda"][0], 8)
    wri = np.concatenate([np.asarray(inp["b_w_r"][0], np.float32).transpose(1, 0, 2),
                          np.asarray(inp["b_w_i"][0], np.float32).transpose(1, 0, 2)], axis=1)
    return {
        "w0": pack_l0(np.asarray(inp["e_w_in"][0], np.float32), np.asarray(inp["e_w_out"][0], np.float32),
                      np.asarray(inp["ple_gate_w"][0], np.float32), np.asarray(inp["ple_w"][0], np.float32)),
        "wri": np.ascontiguousarray(wri), "sm0": sm,
        "g_e": np.asarray(inp["e_norm"][0], np.float32), "g_ple0": np.asarray(inp["ple_norm"][0], np.float32),
        "consts_d": make_consts(),
    }


def pack_l1(o_w_in, o_w_out, gate_w, ple_w):
    slots = []
    for g in range(4):
        slots += _tm_slots(o_w_in, 2048 + 512 * g)
    for g in range(4):
        slots += _tm_slots(o_w_in, 4096 + 512 * g)
        slots += _tm_slots(o_w_in, 6144 + 512 * g)
    for h in range(8):
        slots.append(_fm_slot(o_w_in, 128 * h))
        slots.append(_fm_slot(o_w_in, 1024 + 128 * h))
    slots += pack_tail(o_w_out, gate_w, ple_w)
    return np.stack(slots).astype(np.float32)


L1_NSLOT = 16 + 32 + 16 + 40


def prep_l1(inp):
    sm = np.zeros((128, 96), np.float32)
    sm[:, 0:64] = np.asarray(inp["c_conv_w"][0], np.float32).reshape(4, 16, 128).transpose(2, 1, 0).reshape(128, 64)
    sm[:, 64:80] = _pp(inp["c_conv_b"][0], 16)
    sm[:, 80:96] = _pp(inp["c_norm"][0], 16)
    wg = np.asarray(inp["o_w_in"][0][:, 8192:8208], np.float32).reshape(16, 128, 16).transpose(1, 0, 2)
    gb = np.concatenate([np.asarray(inp["c_b_i"][0], np.float32), np.asarray(inp["c_b_f"][0], np.float32)])
    return {
        "w1": pack_l1(np.asarray(inp["o_w_in"][0], np.float32), np.asarray(inp["o_w_out"][0], np.float32),
                      np.asarray(inp["ple_gate_w"][1], np.float32), np.asarray(inp["ple_w"][1], np.float32)),
        "wg": np.ascontiguousarray(wg), "sm1": sm, "gb": gb,
        "g_o": np.asarray(inp["o_norm"][0], np.float32), "g_ple1": np.asarray(inp["ple_norm"][1], np.float32),
        "g_fin": np.asarray(inp["final_norm"], np.float32),
        "consts_d": make_consts(),
    }


def layer1(C, seg="U"):
    P, B, nc = C.P, C.B, C.nc
    dram, sb = C.dram, C.sb
    w1 = dram("w1", [L1_NSLOT, 128, 2048])
    wg_d = dram("wg", [128, 16, 16])
    sm_d = dram("sm1", [128, 96])
    gb_d = dram("gb", [16])
    g_o = dram("g_o", [D])
    g_ple = dram("g_ple1", [D])
    g_fin = dram("g_fin", [D])
    states_only = (seg == "A")
    if seg == "U":
        hin, p1 = dram("hin1", [T, D]), dram("p1", [T, 256])
        stC_d, stv_d = dram("stC", [128, 8 * 260]), dram("stv1", [128, 48])
        hout = dram("hout1", [T, D], kind="ExternalOutput")
        stC_o = dram("stC_o", [128, 8 * 260], kind="ExternalOutput")
        stv_o = dram("stv1_o", [128, 48], kind="ExternalOutput")
        nm = {"hin": "d_hin1", "hout": "d_hout1", "stC": "d_stC", "stv": "d_stv1", "stC_o": "d_stC_o", "stv_o": "d_stv1_o"}
        C.out_keys += ["hout0", "hout1", "stC_o", "stv1_o"]
    elif seg == "A":
        hin, p1 = dram("h2A", [T, D], kind="Internal"), dram("p1A", [T, 256])
        stC_d, stv_d = dram("zS", [128, 2080]), dram("zv", [128, 48])
        hout = None
        stC_o = dram("sC1", [128, 8 * 260], kind="Internal")
        stv_o = dram("sv1", [128, 48], kind="Internal")
        nm = {"hin": "d_h2A", "hout": "d_none", "stC": "d_zS", "stv": "d_zv", "stC_o": "d_sC1", "stv_o": "d_sv1"}
    else:
        hin, p1 = dram("h2B", [T, D], kind="Internal"), dram("p1B", [T, 256])
        stC_d, stv_d = dram("sC1", [128, 8 * 260], kind="Internal"), dram("sv1", [128, 48], kind="Internal")
        hout = dram("out", [T, D], kind="ExternalOutput")
        stC_o = stv_o = None
        nm = {"hin": "d_h2B", "hout": "d_out", "stC": "d_sC1", "stv": "d_sv1"}
        C.out_keys += ["hout0", "hout1"]

    A = C.arena

    def abf(off, a, b):
        return A[:, off:off + a * b].rearrange("p (a b) -> p a b", b=b)
    qT = abf(0, 8, 1024)
    kT = abf(8192, 8, 1024)
    VX = abf(16384, 64, 260)
    GT = abf(33024, 8, 2048)
    TB1 = 49408
    SGO = A[:, TB1:TB1 + 8192].bitcast(F32).rearrange("p (a b) -> p a b", b=512)
    X = A[:, TB1:TB1 + 2064].bitcast(F32)
    XC = A[:, TB1 + 2064:TB1 + 4128].bitcast(F32)
    TB = 40960

    def tmpf(i, w=1032):
        return A[:, TB + i * 2064:TB + (i + 1) * 2064].bitcast(F32)[:, 0:w]
    tmps = [tmpf(i) for i in range(7)]
    tb = [B(f"tmp{i}") for i in range(7)]
    htile = [A[:, 4096 * i:4096 * (i + 1)].bitcast(F32) for i in range(4)]
    hbf = [A[:, 16384 + 2048 * i:16384 + 2048 * (i + 1)] for i in range(4)]
    C.htile, C.hbf = htile, hbf
    HP = A[:, 0:32768].bitcast(F32).rearrange("p (a b) -> p a b", b=2048)

    wg = sb("wg_sb", [128, 16, 16], BF16)
    sm = sb("sm1_sb", [128, 96], F32)
    gbb = sb("gbb", [128, 16], F32)
    stv = sb("stv1_sb", [128, 48], F32)
    stvo = sb("stvo1_sb", [128, 48], F32)
    EK = sb("EK", [128, 8, 8], F32)
    EKC = sb("EKC", [128, 8, 8], F32)
    THR = sb("THR", [128, 8, 8], F32)
    DEC = sb("DEC", [128, 8, 16], F32)
    g8 = sb("g8", [128, 4, 8], F32)
    r8 = sb("r8", [128, 2, 8], F32)
    ycb = A[:, TB1:TB1 + 2048]
    jk = A[:, TB1 + 2048:TB1 + 2304]
    ktok = [A[:, TB1 + 2304 + 128 * i:TB1 + 2432 + 128 * i] for i in range(2)]
    scw = [A[:, TB1 + 2560 + 128 * i:TB1 + 2688 + 128 * i] for i in range(2)]
    ndsb = [A[:, TB1 + 2816 + 520 * i:TB1 + 2816 + 520 * i + 516].bitcast(F32) for i in range(2)]
    rpl = sb("rpl1", [128, 16], F32)
    Cf = C.stf[:, :].rearrange("p (h e) -> p h e", e=260)
    Cb = C.stb[:, :].rearrange("p (h e) -> p h e", e=260)
    Chs = sb("Chs", [128, 8 * 260], BF16)
    Ch = Chs[:, :].rearrange("p (h e) -> p h e", e=260)

    P.dma("pool", "wri", wg[:], wg_d, writes=[B("wg")])
    P.dma("sp", "sm", sm[:], sm_d, writes=[B("sm1")])
    P.dma("sp", "sm", gbb[:], gb_d.partition_broadcast(128), writes=[B("gbb")])
    P.dma("sp", "stv", stv[:], stv_d, reads=[B(nm["stv"])], writes=[B("stv1")])
    for q in range(4):
        P.dma("sp", "stS", C.stf[:, q * 520:(q + 1) * 520], stC_d[:, q * 520:(q + 1) * 520], reads=[B(nm["stC"])], writes=[B("stf")])
    TS(C, stv[:], stv[:], C.msk[:, 0:1], None, ALU.mult, ALU.bypass, [B("stv1"), B("msk")], [B("stv1")])
    TS(C, C.stf[:], C.stf[:], C.msk[:, 0:1], None, ALU.mult, ALU.bypass, [B("stf"), B("msk")], [B("stf")])
    P.dma("sp", "g", C.gain[:], g_o.partition_broadcast(128), writes=[B("gain")])
    for j in range(NT):
        hb = j % 4
        P.dma("sp", f"h{hb}", htile[hb], hin[j * 128:(j + 1) * 128, :], reads=[B(nm["hin"])], writes=[B(f"htile{hb}")])
        rms_to_T(C, htile[hb], B(f"htile{hb}"), j, B("gain"), C.big, C.bigb, hb)
    if C.stop <= 1:
        return
    mix_bufs = [B(n) for n in ("qT1", "kT1", "VX", "GT", "SGO", "X1", "XC1")]
    alias_bufs(mix_bufs, [B(f"htile{i}") for i in range(4)] + [B(f"hbf{i}") for i in range(4)])
    P.emit("pool", lambda e: e.memset(VX[:, :, 256:260], 0.0), writes=[B("VX")])
    P.emit("pool", lambda e: e.memset(VX[:, :, 256:257], 1.0), writes=[B("VX")])
    maskA = C.consts[:, 128:256]
    H0, H1 = C.consts[:, 256:384], C.consts[:, 384:512]
    SAME = C.consts[:, 640:768]
    LN_S = float(np.log(np.sqrt(128.0)))
    for j in range(NT):
        tk = slice(j * 128, (j + 1) * 128)
        pg = C.PC[:, 0:16]
        for kc in range(KC):
            MM(C, pg, C.big[:, kc, tk], wg[:, kc, :], kc == 0, kc == KC - 1, [B("wg"), C.bigb[kc]], [B("PCh0")])
        li, nlf, a1, t1 = g8[:, 0, :], g8[:, 1, :], g8[:, 2, :], g8[:, 3, :]
        TT(C, t1, pg[:, 8:16], gbb[:, 8:16], ALU.add, [B("PCh0"), B("gbb")], [B("g8t")])
        TT(C, li, pg[:, 0:8], gbb[:, 0:8], ALU.add, [B("PCh0"), B("gbb")], [B("g8l")])
        ACT(C, t1, t1, AF.Exp, [B("g8t")], [B("g8t")], scale=-1.0)
        ACT(C, nlf, t1, AF.Ln, [B("g8t")], [B("g8n")], bias=1.0)
        pq = C.PC[:, 512:544]
        MM(C, pq[:, 0:8], maskA, nlf, True, True, [B("consts"), B("g8n")], [B("PCh1")])
        MM(C, pq[:, 8:16], SAME, nlf, True, True, [B("consts"), B("g8n")], [B("PCh1")])
        MM(C, pq[:, 16:24], H0, nlf, True, True, [B("consts"), B("g8n")], [B("PCh1")])
        MM(C, pq[:, 24:32], H1, nlf, True, True, [B("consts"), B("g8n")], [B("PCh1")])
        TT(C, a1, li, pq[:, 0:8], ALU.add, [B("g8l"), B("PCh1")], [B("g8a")])
        ACT(C, EK[:, j, :], a1, AF.Exp, [B("g8a")], [B("EK")])
        TT(C, a1, a1, pq[:, 8:16], ALU.subtract, [B("g8a"), B("PCh1")], [B("g8a")])
        ACT(C, EKC[:, j, :], a1, AF.Exp, [B("g8a")], [B("EKC")])
        ACT(C, THR[:, j, :], pq[:, 0:8], AF.Exp, [B("PCh1")], [B("THR")], bias=LN_S)
        ACT(C, DEC[:, j, :], pq[:, 16:32], AF.Exp, [B("PCh1")], [B("DEC")], scale=-1.0)
    if C.stop <= 2:
        return
    alias_bufs([B("PAh0"), B("PAh1"), B("PBh0"), B("PBh1")], [B("PA"), B("PB")])
    slot = 0
    for g in range(4):
        def cons(j, ps, pb, g=g):
            ACT(C, VX[:, j * 8 + 2 * g:j * 8 + 2 * g + 2, 0:256], ps.rearrange("p (a b) -> p a b", b=256), AF.Copy, [pb], [B("VX")])
        tm_group(C, w1, slot, 4, lambda kc, j: C.big[:, kc, j * 128:(j + 1) * 128], lambda kc, j: [C.bigb[kc]], cons)
        slot += 4
    if C.stop <= 3:
        return
    for g in range(4):
        if states_only:
            slot = 48
            break

        def cons_o(j, ps, pb):
            ACT(C, SGO[:, j, :], ps, AF.Sigmoid, [pb], [B("SGO")])
        tm_group(C, w1, slot, 4, lambda kc, j: C.big[:, kc, j * 128:(j + 1) * 128], lambda kc, j: [C.bigb[kc]], cons_o)
        slot += 4

        def cons_z(j, ps, pb, g=g):
            ACT(C, ps, ps, AF.Silu, [pb], [pb])
            TT(C, GT[:, j, g * 512:(g + 1) * 512], ps, SGO[:, j, :], ALU.mult, [pb, B("SGO")], [B("GT")])
        tm_group(C, w1, slot, 4, lambda kc, j: C.big[:, kc, j * 128:(j + 1) * 128], lambda kc, j: [C.bigb[kc]], cons_z)
        slot += 4
    if C.stop <= 4:
        return
    alias_bufs([B("X1"), B("XC1")], [B("SGO")])
    alias_bufs([B("PA"), B("PB")], [B("PAh0"), B("PAh1"), B("PBh0"), B("PBh1")])
    for h in range(8):
        for qk in range(2):
            i = qk * 8 + h
            PS, psb = [(C.PA, B("PA")), (C.PB, B("PB"))][qk]
            inproj_fm(C, w1, slot, PS, psb); slot += 1
            ACT(C, X[:, 3:1027], PS[:], AF.Copy, [psb], [B("X1")])
            CP(C, X[:, 0:3], stv[:, 3 * i:3 * i + 3], [B("stv1")], [B("X1")])
            CP(C, stvo[:, 3 * i:3 * i + 3], X[:, 1024:1027], [B("X1")], [B("stvo1")])
            cw = lambda k: sm[:, 4 * i + k:4 * i + k + 1]
            TS(C, XC[:, 0:1024], X[:, 3:1027], cw(3), sm[:, 64 + i:65 + i], ALU.mult, ALU.add, [B("X1"), B("sm1")], [B("XC1")])
            for k in (2, 1, 0):
                STT(C, XC[:, 0:1024], X[:, k:k + 1024], cw(k), XC[:, 0:1024], ALU.mult, ALU.add, [B("X1"), B("XC1"), B("sm1")], [B("XC1")])
            dst, dbuf = (qT, B("qT1")) if qk == 0 else (kT, B("kT1"))
            ACT(C, dst[:, h, :], XC[:, 0:1024], AF.Silu, [B("XC1")], [dbuf])
    alias_bufs([B("PAh0"), B("PAh1"), B("PBh0"), B("PBh1"), B("PCh0"), B("PCh1")], [B("PA"), B("PB"), B("PC")])
    if C.stop <= 5:
        return
    for h in range(8):
        ACT(C, Cb[:, h, 0:258], Cf[:, h, 0:258], AF.Copy, [B("stf")], [B(f"Cb{h}")])
        if not states_only:
            ACT(C, Ch[:, h, 0:258], Cf[:, h, 0:258], AF.Copy, [B("stf"), B("DEC")], [B(f"Ch{h}")], scale=DEC[:, 0, h:h + 1])
    def FM(j, h, r, kvsel=None):
        tk = slice(j * 128, (j + 1) * 128)
        p_sc = C.PC[:, r * 512 + 384:r * 512 + 512]
        b_sc = B(f"PCh{r}")
        KVT, kvn = [(C.PB, "PB"), (C.PA, "PA"), (C.PC, "PC")][r if kvsel is None else kvsel]
        p_kv = [KVT[:, 0:258], KVT[:, 512:770]]
        b_kv = [B(kvn + "h0"), B(kvn + "h1")]
        p_nd = C.PC[:, r * 512:r * 512 + 258]
        b_nd = B(f"PCh{r}")
        jh = j * 8 + h
        ptk = C.PT[:, r * 1024:r * 1024 + 128]
        TR(C, ptk, kT[:, h, tk], [B("kT1")], [B(f"PT{r}")])
        if not states_only:
            MM(C, p_sc, kT[:, h, tk], qT[:, h, tk], True, True, [B("kT1"), B("qT1")], [b_sc])
        ACT(C, ktok[r][:], ptk, AF.Copy, [B(f"PT{r}"), B("EKC")], [B(f"ktk{r}")], scale=EKC[:, j, h:h + 1])
        if not states_only:
            STT(C, scw[r][:], p_sc, EK[:, j, h:h + 1], maskA, ALU.mult, ALU.mult, [b_sc, B("EK"), B("consts")], [B(f"scw{r}")])
            ACT(C, scw[r][0:64, 64:128], p_sc[0:64, 64:128], AF.Copy, [b_sc, B("EKC")], [B(f"scw{r}")], scale=EKC[0:64, j, h:h + 1])
        for c in range(2):
            rows = slice(64 * c, 64 * c + 64)
            MM(C, p_kv[c], ktok[r][rows, :], VX[rows, jh, 0:258], True, True, [B(f"ktk{r}"), B("VX")], [b_kv[c]])
        if not states_only:
            MM(C, p_nd[0:64, :], qT[:, h, j * 128:j * 128 + 64], Cb[:, h, 0:258], True, False, [B("qT1"), B(f"Cb{h}")], [b_nd])
            MM(C, p_nd[64:128, :], qT[:, h, j * 128 + 64:(j + 1) * 128], Ch[:, h, 0:258], True, False, [B("qT1"), B(f"Ch{h}")], [b_nd])
            MM(C, p_nd, scw[r][:], VX[:, jh, 0:258], False, True, [B(f"scw{r}"), B("VX")], [b_nd])
            ACT(C, ndsb[r][:], p_nd, AF.Copy, [b_nd], [B(f"ndsb{r}")])
        for c in range(2):
            STT(C, Cf[:, h, 0:258], Cf[:, h, 0:258], DEC[:, j, 8 * c + h:8 * c + h + 1], p_kv[c], ALU.mult, ALU.add,
                [B(f"Cf{h}"), B("stf"), B("DEC"), b_kv[c]], [B(f"Cf{h}")])
        if not states_only:
            ACT(C, Cb[:, h, 0:258], Cf[:, h, 0:258], AF.Copy, [B(f"Cf{h}")], [B(f"Cb{h}")])
            if j + 1 < NT:
                TS(C, Ch[:, h, 0:258], Cf[:, h, 0:258], DEC[:, j + 1, h:h + 1], None, ALU.mult, ALU.bypass, [B(f"Cf{h}"), B("DEC")], [B(f"Ch{h}")])

    def KK_(j, h, r):
        nd = ndsb[r]
        b_nd = B(f"ndsb{r}")
        s_ = r8[:, r, :]
        sbn = B(f"r8_{r}")
        ACT(C, s_[:, 0:1], nd[:, 256:257], AF.Abs, [b_nd], [sbn])
        TT(C, s_[:, 0:1], s_[:, 0:1], THR[:, j, h:h + 1], ALU.max, [sbn, B("THR")], [sbn])
        P.emit("dve", lambda e, s_=s_: e.reciprocal(out=s_[:, 1:2], in_=s_[:, 0:1]), reads=[sbn], writes=[sbn])
        ACT(C, jk[:], nd[:, 0:256], AF.Square, [b_nd, sbn], [B("jk"), sbn], scale=s_[:, 1:2], accum_out=s_[:, 2:3])
        ACT(C, s_[:, 3:4], s_[:, 2:3], AF.Sqrt, [sbn], [sbn], scale=1.0 / 256.0, bias=EPS)
        P.emit("dve", lambda e, s_=s_: e.reciprocal(out=s_[:, 4:5], in_=s_[:, 3:4]), reads=[sbn], writes=[sbn])
        TT(C, s_[:, 5:6], s_[:, 4:5], s_[:, 1:2], ALU.mult, [sbn], [sbn])
        STT(C, ycb[:, h * 256:(h + 1) * 256], nd[:, 0:256], s_[:, 5:6], GT[:, j, h * 256:(h + 1) * 256], ALU.mult, ALU.mult,
            [b_nd, sbn, B("GT")], [B("ycb")])

    def YT(j):
        tk = slice(j * 128, (j + 1) * 128)
        for half in range(2):
            ptv = C.PT[:, half * 1024:(half + 1) * 1024]
            pb = B(f"PT{half}")
            for k in range(8):
                kc = half * 8 + k
                TR(C, ptv[:, k * 128:(k + 1) * 128], ycb[:, kc * 128:(kc + 1) * 128], [B("ycb")], [pb])
            for k in range(8):
                kc = half * 8 + k
                ACT(C, C.big[:, kc, tk], ptv[:, k * 128:(k + 1) * 128], AF.Copy, [pb, B("sm1")], [C.bigb[kc]], scale=sm[:, 80 + kc:81 + kc])

    its = [(j, h) for j in range(NT) for h in range(8)]
    if states_only:
        for i, (j, h) in enumerate(its):
            FM(j, h, i % 2, kvsel=i % 3)
    else:
        FM(its[0][0], its[0][1], 0)
        for i in range(len(its)):
            if i + 1 < len(its):
                FM(its[i + 1][0], its[i + 1][1], (i + 1) % 2)
            KK_(its[i][0], its[i][1], i % 2)
            if its[i][1] == 7:
                YT(its[i][0])
    if stC_o is not None:
        for q in range(4):
            P.dma("sp", "stC_o", stC_o[:, q * 520:(q + 1) * 520], C.stf[:, q * 520:(q + 1) * 520], reads=[B(f"Cf{h}") for h in range(8)],
                  writes=[B(nm["stC_o"])])
        P.dma("sp", "stv1_o", stv_o, stvo[:], reads=[B("stvo1")], writes=[B(nm["stv_o"])])
    if states_only:
        return
    stage4(C, w1, slot, hin, p1, rpl, g_ple, hout, mix_bufs + [B("ycb"), B("jk"), B("ktk0"), B("ktk1"), B("scw0"), B("scw1"), B("ndsb0"), B("ndsb1")],
           HP, tmps, tb, g_fin, nm)


def _run_layer_unfused(nc, shared, per_core, st_names, out_name):
    zeros = {k: np.zeros(shape, np.float32) for k, (shape, _) in st_names.items()}

    def maps(states):
        ms = []
        for c in range(8):
            m = dict(shared)
            m.update(per_core[c])
            for k in st_names:
                m[k] = states[c][k]
            ms.append(m)
        return ms
    r1 = run_bass_kernel_spmd(nc, maps([zeros] * 8), core_ids=list(range(8)))
    st = []
    for c in range(8):
        if c % 2 == 1:
            st.append({k: np.asarray(r1.results[c - 1][o], np.float32) for k, (_, o) in st_names.items()})
        else:
            st.append(zeros)
    r2 = run_bass_kernel_spmd(nc, maps(st), core_ids=list(range(8)))
    return [np.asarray(r2.results[c][out_name]) for c in range(8)]


def kernel_unfused(**inp):
    inp = {k: np.asarray(v) for k, v in inp.items()}
    x, p = inp["x"], inp["p"]
    s0 = prep_l0(inp)
    s0["msk"] = np.ones((128, 1), np.float32)
    nc0 = build_program([0])
    pc = [{"hin": np.ascontiguousarray(x[c // 2, (c % 2) * T:(c % 2 + 1) * T], dtype=np.float32),
           "p0": np.ascontiguousarray(p[0, c // 2, (c % 2) * T:(c % 2 + 1) * T], dtype=np.float32)} for c in range(8)]
    h2 = _run_layer_unfused(nc0, s0, pc, {"stS": ((128, 1024), "stS_o"), "stv": ((128, 32), "stv_o")}, "hout")
    del s0
    s1 = prep_l1(inp)
    s1["msk"] = np.ones((128, 1), np.float32)
    nc1 = build_program([1])
    pc = [{"hin1": np.ascontiguousarray(h2[c], dtype=np.float32),
           "p1": np.ascontiguousarray(p[1, c // 2, (c % 2) * T:(c % 2 + 1) * T], dtype=np.float32)} for c in range(8)]
    out = _run_layer_unfused(nc1, s1, pc, {"stC": ((128, 8 * 260), "stC_o"), "stv1": ((128, 48), "stv1_o")}, "hout1")
    return np.stack(out).reshape(4, 2 * T, D).astype(np.float32)


def make_in_maps(inp):
    inp = {k: np.asarray(v) for k, v in inp.items()}
    x, p = np.asarray(inp["x"], np.float32), np.asarray(inp["p"], np.float32)
    shared = {}
    shared.update(prep_l0(inp))
    shared.update(prep_l1(inp))
    shared["zS"] = np.zeros((128, 2080), np.float32)
    shared["zv"] = np.zeros((128, 48), np.float32)
    zx = np.zeros((T, D), np.float32)
    zp = np.zeros((T, 256), np.float32)
    maps = []
    for c in range(8):
        b, hf = c // 2, c % 2
        m = dict(shared)
        m["xB"] = np.ascontiguousarray(x[b, hf * T:(hf + 1) * T])
        m["p0B"] = np.ascontiguousarray(p[0, b, hf * T:(hf + 1) * T])
        m["p1B"] = np.ascontiguousarray(p[1, b, hf * T:(hf + 1) * T])
        if hf == 1:
            m["xA"] = np.ascontiguousarray(x[b, 0:T])
            m["p0A"] = np.ascontiguousarray(p[0, b, 0:T])
            m["p1A"] = np.ascontiguousarray(p[1, b, 0:T])
        else:
            m["xA"], m["p0A"], m["p1A"] = zx, zp, zp
        m["msk"] = np.full((128, 1), float(hf), np.float32)
        maps.append(m)
    return maps


def kernel(**inp):
    maps = make_in_maps(inp)
    nc = build_program("fused")
    res = run_bass_kernel_spmd(nc, maps, core_ids=list(range(8)))
    out = [np.asarray(res.results[c]["out"], np.float32) for c in range(8)]
    return np.stack(out).reshape(4, 2 * T, D)
```
